# Optimizing a Trainium2 kernel written in Bass

```python
import math
import jax, jax.numpy as jnp
from jax import lax
import numpy as np

D_MODEL = 4096
BATCH = 2
SEQ = 4096
DEPTH = 4

N_MIXERS = 2
BLOCK = 128
SB_HEAD_DIM = 128
SB_HEADS = D_MODEL // SB_HEAD_DIM
SWA_HEAD_DIM = 64
SWA_HEADS = D_MODEL // SWA_HEAD_DIM
SWA_KV_HEADS = SWA_HEADS // 8
SWA_GROUP = SWA_HEADS // SWA_KV_HEADS
SWA_QKV_DIM = (SWA_HEADS + 2 * SWA_KV_HEADS) * SWA_HEAD_DIM
WINDOW = 128
D_FF = (7 * D_MODEL) // 2
CONV_WIDTH = 3
RMS_EPS = 1e-6
N_SB_LAYERS = (DEPTH + 1) // N_MIXERS
N_SWA_LAYERS = DEPTH // N_MIXERS

kernel_name = "stickbreak_swa_sink_alibi_convffn_hybrid"


def rms_norm(x, gain):
    x32 = x.astype(jnp.float32)
    y = x32 * lax.rsqrt(jnp.mean(x32 * x32, axis=-1, keepdims=True) + RMS_EPS)
    return (y * gain.astype(jnp.float32)).astype(x.dtype)


def alibi_slopes(n_heads):
    return jnp.asarray(2.0 ** (-8.0 * np.arange(1, n_heads + 1) / n_heads), jnp.float32)


def stick_breaking_attention(xn, w_qkv, w_o):
    B, S, _ = xn.shape
    q, k, v = jnp.split(xn @ w_qkv, 3, axis=-1)

    def to_heads(t):
        return t.reshape(B, S, SB_HEADS, SB_HEAD_DIM).transpose(0, 2, 1, 3)

    q, k, v = to_heads(q), to_heads(k), to_heads(v)
    scale = SB_HEAD_DIM ** -0.5
    outs = []
    for i in range(S // BLOCK):
        n_keys = (i + 1) * BLOCK
        q_blk = q[:, :, i * BLOCK:(i + 1) * BLOCK]
        k_pre = k[:, :, :n_keys]
        v_pre = v[:, :, :n_keys]
        z = jnp.einsum('bhqd,bhkd->bhqk', q_blk, k_pre).astype(jnp.float32) * scale
        t_pos = i * BLOCK + jnp.arange(BLOCK)
        s_pos = jnp.arange(n_keys)
        strict = s_pos[None, :] < t_pos[:, None]
        log_stay = jnp.where(strict, jax.nn.log_sigmoid(-z), 0.0)
        later = lax.cumsum(log_stay, axis=3, reverse=True) - log_stay
        weights = jnp.where(strict, jnp.exp(jax.nn.log_sigmoid(z) + later), 0.0)
        outs.append(jnp.einsum('bhqk,bhkd->bhqd', weights.astype(v.dtype), v_pre))
    o = jnp.concatenate(outs, axis=2).transpose(0, 2, 1, 3).reshape(B, S, SB_HEADS * SB_HEAD_DIM)
    return o @ w_o


def sliding_window_attention(xn, w_qkv, w_o, sinks):
    B, S, _ = xn.shape
    nb = S // BLOCK
    qd = SWA_HEADS * SWA_HEAD_DIM
    kd = SWA_KV_HEADS * SWA_HEAD_DIM
    qkv = xn @ w_qkv
    q = qkv[..., :qd].reshape(B, nb, BLOCK, SWA_KV_HEADS, SWA_GROUP, SWA_HEAD_DIM)
    k = qkv[..., qd:qd + kd]
    v = qkv[..., qd + kd:]

    def band(t):
        cur = t.reshape(B, nb, BLOCK, SWA_KV_HEADS, SWA_HEAD_DIM)
        prev = jnp.pad(cur, ((0, 0), (1, 0), (0, 0), (0, 0), (0, 0)))[:, :-1]
        return jnp.concatenate([prev, cur], axis=2).swapaxes(0, 1)

    k_band, v_band = band(k), band(v)
    q_blk = q.swapaxes(0, 1)
    q_idx = jnp.arange(BLOCK)[:, None]
    k_idx = jnp.arange(2 * BLOCK)[None, :]
    dist = q_idx + BLOCK - k_idx
    in_window = (dist >= 0) & (dist < WINDOW)
    slopes = alibi_slopes(SWA_HEADS).reshape(SWA_KV_HEADS, SWA_GROUP)
    alibi = -slopes[:, :, None, None] * dist.astype(jnp.float32)
    sink_logits = sinks.astype(jnp.float32).reshape(SWA_KV_HEADS, SWA_GROUP)
    scale = SWA_HEAD_DIM ** -0.5

    def attend_block(args):
        qb, kb, vb, blk = args
        s = jnp.einsum('bqhgd,bkhd->bhgqk', qb, kb).astype(jnp.float32) * scale + alibi
        valid = in_window & ((blk * BLOCK + k_idx - BLOCK) >= 0)
        s = jnp.where(valid, s, -jnp.inf)
        sink = jnp.broadcast_to(sink_logits[None, :, :, None, None], s.shape[:-1] + (1,))
        p = jax.nn.softmax(jnp.concatenate([s, sink], axis=-1), axis=-1)[..., :-1]
        return jnp.einsum('bhgqk,bkhd->bqhgd', p.astype(vb.dtype), vb)

    o = lax.map(attend_block, (q_blk, k_band, v_band, jnp.arange(nb)))
    o = o.swapaxes(0, 1).reshape(B, S, qd)
    return o @ w_o


def conv_ffn(xn, w_in, conv_w, conv_b, w_down):
    S = xn.shape[1]
    h = xn @ w_in
    hp = jnp.pad(h, ((0, 0), (CONV_WIDTH - 1, 0), (0, 0)))
    y = conv_b
    for j in range(CONV_WIDTH):
        y = y + conv_w[j] * hp[:, j:j + S]
    gate, up = jnp.split(y, 2, axis=-1)
    return (jax.nn.silu(gate) * up) @ w_down


def setup_inputs(seed: int = 0) -> dict:
    key = jax.random.key(seed)
    ks = jax.random.split(key, 13)
    f32 = jnp.float32
    x = jax.random.normal(ks[0], (BATCH, SEQ, D_MODEL), f32)
    attn_norm = 1.0 + 0.02 * jax.random.normal(ks[1], (DEPTH, D_MODEL), f32)
    ffn_norm = 1.0 + 0.02 * jax.random.normal(ks[2], (DEPTH, D_MODEL), f32)
    sb_w_qkv = jax.random.normal(ks[3], (N_SB_LAYERS, D_MODEL, 3 * SB_HEADS * SB_HEAD_DIM), f32) * D_MODEL ** -0.5
    sb_w_o = jax.random.normal(ks[4], (N_SB_LAYERS, SB_HEADS * SB_HEAD_DIM, D_MODEL), f32) * (SB_HEADS * SB_HEAD_DIM) ** -0.5
    swa_w_qkv = jax.random.normal(ks[5], (N_SWA_LAYERS, D_MODEL, SWA_QKV_DIM), f32) * D_MODEL ** -0.5
    swa_w_o = jax.random.normal(ks[6], (N_SWA_LAYERS, SWA_HEADS * SWA_HEAD_DIM, D_MODEL), f32) * (SWA_HEADS * SWA_HEAD_DIM) ** -0.5
    swa_sinks = 0.5 * jax.random.normal(ks[7], (N_SWA_LAYERS, SWA_HEADS), f32)
    ffn_w_in = jax.random.normal(ks[8], (DEPTH, D_MODEL, 2 * D_FF), f32) * D_MODEL ** -0.5
    ffn_conv_w = jax.random.normal(ks[9], (DEPTH, CONV_WIDTH, 2 * D_FF), f32) * CONV_WIDTH ** -0.5
    ffn_conv_b = 0.02 * jax.random.normal(ks[10], (DEPTH, 2 * D_FF), f32)
    ffn_w_down = jax.random.normal(ks[11], (DEPTH, D_FF, D_MODEL), f32) * D_FF ** -0.5
    final_norm = 1.0 + 0.02 * jax.random.normal(ks[12], (D_MODEL,), f32)
    return {"x": x, "attn_norm": attn_norm, "ffn_norm": ffn_norm,
            "sb_w_qkv": sb_w_qkv, "sb_w_o": sb_w_o,
            "swa_w_qkv": swa_w_qkv, "swa_w_o": swa_w_o, "swa_sinks": swa_sinks,
            "ffn_w_in": ffn_w_in, "ffn_conv_w": ffn_conv_w, "ffn_conv_b": ffn_conv_b,
            "ffn_w_down": ffn_w_down, "final_norm": final_norm}


def reference(x, attn_norm, ffn_norm, sb_w_qkv, sb_w_o, swa_w_qkv, swa_w_o, swa_sinks,
              ffn_w_in, ffn_conv_w, ffn_conv_b, ffn_w_down, final_norm):
    h = x
    for i in range(DEPTH):
        xn = rms_norm(h, attn_norm[i])
        j = i // N_MIXERS
        if i % N_MIXERS == 0:
            mix = stick_breaking_attention(xn, sb_w_qkv[j], sb_w_o[j])
        else:
            mix = sliding_window_attention(xn, swa_w_qkv[j], swa_w_o[j], swa_sinks[j])
        h = h + mix
        h = h + conv_ffn(rms_norm(h, ffn_norm[i]), ffn_w_in[i], ffn_conv_w[i],
                         ffn_conv_b[i], ffn_w_down[i])
    return rms_norm(h, final_norm)
```

```python
import contextlib
import os
import math
import numpy as np
import ml_dtypes
import concourse.bass as bass
import concourse.mybir as mybir
from concourse.bass_utils import run_bass_kernel_spmd

F32 = mybir.dt.float32
BF16 = mybir.dt.bfloat16
AF = mybir.ActivationFunctionType
ALU = mybir.AluOpType
NEG = -30000.0


class Cfg:
    def __init__(self, D=4096, T=1024, NG=4, NB=2, G=8, depth=4, debug=False):
        self.D, self.T, self.NG, self.NB, self.G, self.depth = D, T, NG, NB, G, depth
        self.NC = NG * NB
        self.KC = D // 128
        self.SBH = D // 128
        self.SWH = D // 64
        self.KVH = self.SWH // G
        self.DKV = self.KVH * 64
        self.DFF = 7 * D // 2
        self.MTF = self.DFF // 128
        self.TT = T + 2
        self.NKB = NG * T // 128
        self.OFF = 128 * (self.NKB - 1)
        self.MW = self.OFF + 2 * 512
        self.debug = debug
        self.stop = None

    def piece_rows(self, K, N, esize=2):
        q = K // self.NG
        mx = max(1, (1 << 20) // (esize * N))
        nr = 1
        for d in range(1, q + 1):
            if q % d == 0 and d <= mx:
                nr = d
        return nr

    def nqkv(self, l):
        return 3 * self.D if l % 2 == 0 else self.D + 2 * self.DKV


class Sem:
    def __init__(self, h):
        self.h = h
        self.cnt = 0


class Buf:
    def __init__(self, name, ap=None):
        self.name = name
        self.ap = ap
        self.w = None
        self.r = {}
        self.dsem = None


class Eng:
    def __init__(self, name, e, sem):
        self.name, self.e, self.sem = name, e, sem
        self.seen = {}


class KB:
    def __init__(self, nc, stack):
        self.nc = nc
        self.stack = stack
        self.sems = []
        mk = lambda n, e: Eng(n, e, self.new_sem("s_" + n))
        self.pe = mk("pe", nc.tensor)
        self.act = mk("act", nc.scalar)
        self.dve = mk("dve", nc.vector)
        self.pool = mk("pool", nc.gpsimd)
        self.sp = mk("sp", nc.sync)
        self.engs = [self.pe, self.act, self.dve, self.pool, self.sp]
        self.local_bufs = []
        self.free_sems = []
        self.nobar_sems = set()

    def new_sem(self, name):
        s = Sem(self.stack.enter_context(self.nc.semaphore(f"{name}_{len(self.sems)}")))
        self.sems.append(s)
        return s

    def buf(self, name, persistent=False):
        b = Buf(name)
        if not persistent:
            self.local_bufs.append(b)
        return b

    def get_dsem(self, b, prefix):
        if b.dsem is None:
            if b in self.local_bufs and self.free_sems:
                b.dsem = self.free_sems.pop()
            else:
                b.dsem = self.new_sem(prefix + b.name)
                if getattr(b, "nobar", False):
                    self.nobar_sems.add(b.dsem)
        return b.dsem

    def wait(self, E, sem, cnt):
        if cnt <= 0 or E.seen.get(sem, 0) >= cnt:
            return
        if sem is E.sem and E is self.pe:
            return
        E.e.wait_ge(sem.h, cnt)
        E.seen[sem] = cnt

    def deps(self, E, reads, writes, nowaw=False):
        for b in reads:
            if b.w:
                self.wait(E, *b.w)
        for b in writes:
            if b.w and not nowaw:
                self.wait(E, *b.w)
            for sem, c in list(b.r.items()):
                self.wait(E, sem, c)

    def op(self, E, fn, reads, writes, inc=True):
        self.deps(E, reads, writes)
        ins = fn(E.e)
        if inc:
            E.sem.cnt += 1
            ins.then_inc(E.sem.h, 1)
            c = E.sem.cnt
        else:
            c = E.sem.cnt + 1
        for b in reads:
            b.r[E.sem] = max(b.r.get(E.sem, 0), c)
        for b in writes:
            b.w = (E.sem, c)
            b.r = {}
        return ins

    def dma(self, out_ap, in_ap, src, dst, nowaw=False, Q=None):
        Q = Q or self.sp
        self.deps(Q, [src], [dst], nowaw=nowaw)
        sem = self.get_dsem(dst, "d_")
        ins = Q.e.dma_start(out=out_ap, in_=in_ap)
        sem.cnt += 16
        ins.then_inc(sem.h, 16)
        src.r[sem] = sem.cnt
        dst.w = (sem, sem.cnt)
        if not nowaw:
            dst.r = {}

    def coll(self, out_ap, in_ap, src, dst, groups, nowaw=False):
        Q = self.pool
        self.deps(Q, [src], [dst], nowaw=nowaw)
        sem = self.get_dsem(dst, "c_")
        ins = Q.e.collective_compute("AllGather", ALU.bypass, replica_groups=groups,
                                     ins=[in_ap], outs=[out_ap])
        sem.cnt += 1
        ins.then_inc(sem.h, 1)
        src.r[sem] = sem.cnt
        dst.w = (sem, sem.cnt)
        if not nowaw:
            dst.r = {}

    def barrier(self):
        for E in self.engs:
            for s in self.sems:
                if s is E.sem or s in self.nobar_sems:
                    continue
                self.wait(E, s, s.cnt)
        for b in self.local_bufs:
            if b.dsem is not None:
                self.free_sems.append(b.dsem)
                b.dsem = None
        self.local_bufs = []


def build_program(cfg):
    nc = bass.Bass("TRN2", target_bir_lowering=False)
    _orig_sbuf_tensor = nc.sbuf_tensor
    _uid = [0]

    def _sbuf_tensor(name, shape, dt):
        _uid[0] += 1
        return _orig_sbuf_tensor(f"{name}_u{_uid[0]}", shape, dt)

    D, T, TT, KC, NG, NC = cfg.D, cfg.T, cfg.TT, cfg.KC, cfg.NG, cfg.NC
    DFF, MTF, depth = cfg.DFF, cfg.MTF, cfg.depth
    groups = [list(range(b * NG, (b + 1) * NG)) for b in range(cfg.NB)]
    allg = [list(range(NC))]

    def din(name, shape, dt=F32):
        return nc.dram_tensor(name, list(shape), dt, kind="ExternalInput")

    def dint(name, shape, dt=F32):
        return nc.dram_tensor(name, list(shape), dt)

    xT = din("xT", [D, T])
    wsh = {}
    wdims = {}
    for l in range(depth):
        wdims[("qkv", l)] = (D, cfg.nqkv(l))
        wdims[("o", l)] = (D, D)
        wdims[("ing", l)] = (D, DFF)
        wdims[("inu", l)] = (D, DFF)
        wdims[("down", l)] = (DFF, D)
    for (nm, l), (K, N) in wdims.items():
        wsh[(nm, l)] = din(f"w_{nm}{l}", [N // NG, K])
    gA = din("gA", [128, depth, KC])
    gF = din("gF", [128, depth, KC])
    gO = din("gO", [128, KC])
    cw_d = din("cw", [128, depth, 3, 2 * MTF])
    cb_d = din("cb", [128, depth, 2 * MTF])
    sink_d = din("sink", [128, depth // 2, cfg.SWH])
    mbig_d = din("mbig", [128, cfg.MW], BF16)
    sel_d = din("sel", [128, NG])
    bt_d = din("bt", [128, cfg.SWH, 256], BF16)
    hm_d = din("hm", [128, 128], BF16)
    cst_d = din("cst", [128, 4, 128], BF16)
    yT = nc.dram_tensor("yT", [D, T], F32, kind="ExternalOutput")
    dbg = {}
    if cfg.debug:
        for l in range(depth):
            dbg[("mid", l)] = nc.dram_tensor(f"dbg_mid{l}", [D, T], F32, kind="ExternalOutput")
            dbg[("h", l)] = nc.dram_tensor(f"dbg_h{l}", [D, T], F32, kind="ExternalOutput")

    wbn = {k: dint(f"wb_{k[0]}{k[1]}", [wdims[k][1] // NG, wdims[k][0]], BF16) for k in wdims}
    wfl = {k: dint(f"wf_{k[0]}{k[1]}", [wdims[k][1], wdims[k][0]], BF16) for k in wdims}
    TTP = T + 32
    hA = dint("hA", [D, TTP])
    hB = dint("hB", [D, TTP])
    xsT = dint("xsT", [D, TTP], BF16)
    qT = dint("qT", [D, T], BF16)
    kv_own = dint("kv_own", [2 * D, T], BF16)
    kv_all = dint("kv_all", [NG * 2 * D, T], BF16)
    oT = dint("oT", [D, T], BF16)
    gT = dint("gT", [DFF, T], BF16)
    HW = max(16, 2 * KC)
    hal_own = dint("hal_own", [128, HW])
    hal_all = dint("hal_all", [NG * 128, HW])
    kvh_own = dint("kvh_own", [2 * cfg.DKV, 128], BF16)
    kvh_all = dint("kvh_all", [NG * 2 * cfg.DKV, 128], BF16)
    kvh_sel = dint("kvh_sel", [2 * cfg.DKV, 128], BF16)

    with contextlib.ExitStack() as stack:
        kb = KB(nc, stack)
        PB = lambda n: kb.buf(n, persistent=True)
        B = PB
        b_x = B("xT")
        b_wsh = {k: B("wsh") for k in wdims}
        b_wbn = {k: B(f"wbn{k[0]}{k[1]}") for k in wdims}
        b_wfl = {k: B(f"wfl{k[0]}{k[1]}") for k in wdims}
        for _d in (b_wsh, b_wbn, b_wfl):
            for _b in _d.values():
                _b.nobar = True
        b_hA, b_hB, b_xs, b_q, b_kvo, b_kva, b_o, b_g = (B("hA"), B("hB"), B("xsT"), B("qT"), B("kvo"),
                                                         B("kva"), B("oT"), B("gT"))
        b_halo, b_hala, b_kvho, b_kvha, b_kvhs = B("halo"), B("hala"), B("kvho"), B("kvha"), B("kvhs")
        b_y = B("yT")
        b_dbg = B("dbg")
        b_cin = B("cin")

        B = kb.buf

        def sb(name, shape, dt):
            return stack.enter_context(_sbuf_tensor(name, list(shape), dt))

        cst = sb("cst", [128, 4, 128], BF16)
        gA_s = sb("gA_s", [128, depth, KC], F32)
        gF_s = sb("gF_s", [128, depth, KC], F32)
        gO_s = sb("gO_s", [128, KC], F32)
        cw_s = sb("cw_s", [128, depth, 3, 2 * MTF], F32)
        cb_s = sb("cb_s", [128, depth, 2 * MTF], F32)
        sel_s = sb("sel_s", [128, NG], F32)
        b_cst = PB("cst")
        for t_s, t_d in ((cst, cst_d), (gA_s, gA), (gF_s, gF), (gO_s, gO), (cw_s, cw_d), (cb_s, cb_d),
                         (sel_s, sel_d)):
            kb.dma(t_s[:], t_d.ap(), b_cin, b_cst, nowaw=True)
        ident = cst[:, 0, :]
        negtri = cst[:, 1, :]
        ones_b = cst[:, 2, :]
        negones = cst[:, 3, :]

        ps = [stack.enter_context(nc.psum_tensor(f"ps{i}", [128, 512], F32)) for i in range(8)]
        b_ps = [PB(f"ps{i}") for i in range(8)]

        def emit_gather(l, names=("qkv", "o", "ing", "inu", "down")):
            ks = [(n_, l) for n_ in names]
            for k in ks:
                Kk, Nk = wdims[k]
                rows = Nk // NG
                rstep = max(1, min(rows, (4 << 20) // (4 * Kk)))
                for r0 in range(0, rows, rstep):
                    r1 = min(rows, r0 + rstep)
                    kb.dma(wbn[k][r0:r1, :], wsh[k][r0:r1, :], b_wsh[k], b_wbn[k], nowaw=(r0 > 0), Q=kb.pool)
            for k in ks:
                Kk, Nk = wdims[k]
                nr = cfg.piece_rows(Nk, Kk)
                for p in range((Nk // NG) // nr):
                    kb.coll(wfl[k][p * NG * nr:(p + 1) * NG * nr, :], wbn[k][p * nr:(p + 1) * nr, :],
                            b_wbn[k], b_wfl[k], groups, nowaw=(p > 0))

        emit_gather(0)
        with contextlib.ExitStack() as ph:
            if os.environ.get("SKIP_INIT"):
                raise_skip = True
            zt = ph.enter_context(_sbuf_tensor("zt", [128, KC, 2], F32))
            b_zt = B("zt")
            kb.op(kb.dve, lambda e: e.memset(zt[:], 0.0), [], [b_zt])
            if not os.environ.get("SKIP_INIT"):
                kb.dma(hA[:, 0:2].rearrange("(k p) c -> p k c", p=128), zt[:], b_zt, b_hA)
                kb.dma(hB[:, 0:2].rearrange("(k p) c -> p k c", p=128), zt[:], b_zt, b_hB, )
            if not os.environ.get("SKIP_X"):
                kb.dma(hA[:, 2:TT], xT.ap(), b_x, b_hA, nowaw=True)
            kb.barrier()

        def norm_phase(h_src, b_src, gain_ap_fn, dst_kind):
            with contextlib.ExitStack() as ph:
                hb = [ph.enter_context(_sbuf_tensor(f"n_hb{i}", [128, TT], F32)) for i in range(2)]
                sq = [ph.enter_context(_sbuf_tensor(f"n_sq{i}", [128, TT], BF16)) for i in range(2)]
                rt = ph.enter_context(_sbuf_tensor("n_rt", [128, TT], F32))
                rb = ph.enter_context(_sbuf_tensor("n_rb", [128, TT], F32))
                odt = BF16 if dst_kind == "xs" else F32
                ob = [ph.enter_context(_sbuf_tensor(f"n_ob{i}", [128, TT], odt)) for i in range(2)]
                b_hb = [B("hb0"), B("hb1")]
                b_sq = [B("sq0"), B("sq1")]
                b_rt, b_rb = B("rt"), B("rb")
                b_ob = [B("ob0"), B("ob1")]
                tiles = [(0, 512, 0), (512, 512, 1), (1024, TT - 1024, 2)]
                import os
                CUT = int(os.environ.get("NORM_CUT", "99"))
                for kc in range(KC):
                    s = kc % 2
                    kb.dma(hb[s][:], h_src[kc * 128:(kc + 1) * 128, 0:TT], b_src, b_hb[s])
                    if CUT < 2:
                        continue
                    kb.op(kb.act, lambda e: e.activation(out=sq[s][:], in_=hb[s][:], func=AF.Square),
                          [b_hb[s]], [b_sq[s]])
                    if CUT < 3:
                        continue
                    for (c0, n, bi) in tiles:
                        kb.op(kb.pe, lambda e: e.matmul(ps[bi][:, 0:n], lhsT=ones_b, rhs=sq[s][:, c0:c0 + n],
                                                        start=(kc == 0), stop=(kc == KC - 1)),
                              [b_sq[s], b_cst], [b_ps[bi]], inc=(kc == KC - 1 or bi == 2))
                if CUT < 4:
                    kb.barrier()
                    return
                for (c0, n, bi) in tiles:
                    kb.op(kb.act, lambda e: e.activation(out=rt[:, c0:c0 + n], in_=ps[bi][:, 0:n], func=AF.Sqrt,
                                                         bias=1e-6, scale=1.0 / D),
                          [b_ps[bi]], [b_rt])
                kb.op(kb.dve, lambda e: e.reciprocal(out=rb[:], in_=rt[:]), [b_rt], [b_rb])
                if CUT < 5:
                    kb.barrier()
                    return
                for kc in range(KC):
                    s = kc % 2
                    kb.dma(hb[s][:], h_src[kc * 128:(kc + 1) * 128, 0:TT], b_src, b_hb[s])
                    kb.op(kb.dve, lambda e: e.scalar_tensor_tensor(out=ob[s][:], in0=hb[s][:],
                                                                   scalar=gain_ap_fn(kc), in1=rb[:],
                                                                   op0=ALU.mult, op1=ALU.mult),
                          [b_hb[s], b_rb, b_cst], [b_ob[s]])
                    if dst_kind == "xs":
                        kb.dma(xsT[kc * 128:(kc + 1) * 128, 0:TT], ob[s][:], b_ob[s], b_xs, nowaw=True)
                    else:
                        kb.dma(yT[kc * 128:(kc + 1) * 128, :], ob[s][:, 2:TT], b_ob[s], b_y, nowaw=True)
                kb.barrier()

        def linear_phase(wkey, K, mtiles, src, b_srcact, src_cols, tok_tiles, epilogue, ph_alloc=None,
                         kseg=32, wsel=None):
            KCl = K // 128
            nseg = (KCl + kseg - 1) // kseg
            assert KCl % nseg == 0
            kseg = KCl // nseg
            c0s, ns = src_cols
            if wsel is None:
                wsel = lambda mt: (wkey, mt)
            with contextlib.ExitStack() as ph:
                insb = ph.enter_context(_sbuf_tensor("l_in", [128, KCl, ns], BF16))
                wbf = [ph.enter_context(_sbuf_tensor(f"l_wbf{i}", [128, kseg * 128], BF16)) for i in range(3)]
                b_in = B("l_in")
                b_wbf = [B("wbf0"), B("wbf1"), B("wbf2")]
                ctx = ph_alloc(ph) if ph_alloc else None
                step = max(1, KCl // 4)
                for k0 in range(0, KCl, step):
                    k1 = min(KCl, k0 + step)
                    kb.dma(insb[:, k0:k1, :],
                           src[k0 * 128:k1 * 128, c0s:c0s + ns].rearrange("(k p) t -> p k t", p=128),
                           b_srcact, b_in, nowaw=True)
                it = 0
                for mi, mt in enumerate(mtiles):
                    bset = (mi % 2) * 4
                    wk, mtl = wsel(mt)
                    W = wfl[wk]
                    for sg in range(nseg):
                        s2, s3 = it % 2, it % 3
                        it += 1
                        c0w = sg * kseg * 128
                        kb.dma(wbf[s3][:], W[mtl * 128:(mtl + 1) * 128, c0w:c0w + kseg * 128],
                               b_wfl[wk], b_wbf[s3])
                        for kc in range(kseg):
                            gk = sg * kseg + kc
                            last = (gk == KCl - 1)
                            for ti, (t0, n, bi) in enumerate(tok_tiles):
                                kb.op(kb.pe, lambda e: e.matmul(ps[bset + bi][:, 0:n],
                                                                lhsT=wbf[s3][:, kc * 128:(kc + 1) * 128],
                                                                rhs=insb[:, gk, t0:t0 + n],
                                                                start=(gk == 0), stop=last),
                                      [b_wbf[s3], b_in], [b_ps[bset + bi]],
                                      inc=(last or (kc == kseg - 1 and ti == len(tok_tiles) - 1)))
                    epilogue(mi, mt, bset, ctx)
                kb.barrier()

        def qkv_phase(l):
            sbl = (l % 2 == 0)
            nq = cfg.nqkv(l) // 128
            qscale = (128 ** -0.5) if sbl else 0.125

            def alloc(ph):
                o = [ph.enter_context(_sbuf_tensor(f"q_o{i}", [128, T], BF16)) for i in range(2)]
                return (o, [B("qo0"), B("qo1")])

            def epi(mi, mt, bset, ctx):
                o, b_o2 = ctx
                s = mi % 2
                sc = qscale if mt < KC else 1.0
                for (t0, n, bi) in ((0, 512, 0), (512, 512, 1)):
                    kb.op(kb.act, lambda e: e.activation(out=o[s][:, t0:t0 + n], in_=ps[bset + bi][:, 0:n],
                                                         func=AF.Identity, scale=sc),
                          [b_ps[bset + bi]], [b_o2[s]])
                if mt < KC:
                    kb.dma(qT[mt * 128:(mt + 1) * 128, :], o[s][:], b_o2[s], b_q, nowaw=True)
                else:
                    r = (mt - KC) * 128
                    kb.dma(kv_own[r:r + 128, :], o[s][:], b_o2[s], b_kvo, nowaw=True)

            linear_phase(("qkv", l), D, list(range(nq)), xsT, b_xs, (2, T),
                         [(0, 512, 0), (512, 512, 1)], epi, alloc)

        def resid_phase(wkey, K, src, b_srcact, h_in, b_hin, h_out, b_hout, dbg_out=None):
            ntt = 1 if K > D else 2
            passes = [(0, T)] if K == D else [(0, 512), (512, 512)]
            for (p0, pn) in passes:
                def alloc(ph):
                    r = [ph.enter_context(_sbuf_tensor(f"r_r{i}", [128, pn], F32)) for i in range(2)]
                    o = [ph.enter_context(_sbuf_tensor(f"r_o{i}", [128, pn], F32)) for i in range(2)]
                    return (r, o, [B("rr0"), B("rr1")], [B("ro0"), B("ro1")])

                tiles = [(i * 512, 512, i) for i in range(pn // 512)]

                def epi(mi, mt, bset, ctx):
                    r, o, b_r, b_o2 = ctx
                    s = mi % 2
                    kb.dma(r[s][:], h_in[mt * 128:(mt + 1) * 128, 2 + p0:2 + p0 + pn], b_hin, b_r[s])
                    for (t0, n, bi) in tiles:
                        kb.op(kb.dve, lambda e: e.tensor_tensor(out=o[s][:, t0:t0 + n], in0=ps[bset + bi][:, 0:n],
                                                                in1=r[s][:, t0:t0 + n], op=ALU.add),
                              [b_ps[bset + bi], b_r[s]], [b_o2[s]])
                    kb.dma(h_out[mt * 128:(mt + 1) * 128, 2 + p0:2 + p0 + pn], o[s][:], b_o2[s], b_hout, nowaw=True)
                    if dbg_out is not None:
                        kb.dma(dbg_out[mt * 128:(mt + 1) * 128, p0:p0 + pn], o[s][:], b_o2[s], b_dbg, nowaw=True)

                linear_phase(wkey, K, list(range(KC)), src, b_srcact, (p0, pn), tiles, epi, alloc, kseg=28 if K > D else 32)

        def ffn_in_phase(l):
            mts = []
            for i in range(MTF):
                mts += [i, MTF + i]

            def alloc(ph):
                hs = [ph.enter_context(_sbuf_tensor(f"f_hs{i}", [128, TT], F32)) for i in range(2)]
                yg = ph.enter_context(_sbuf_tensor("f_yg", [128, T], F32))
                yu = ph.enter_context(_sbuf_tensor("f_yu", [128, T], F32))
                sg = ph.enter_context(_sbuf_tensor("f_sg", [128, T], F32))
                go = [ph.enter_context(_sbuf_tensor(f"f_go{i}", [128, T], BF16)) for i in range(2)]
                return dict(hs=hs, yg=yg, yu=yu, sg=sg, go=go, b_hs=[B("hs0"), B("hs1")], b_yg=B("yg"),
                            b_yu=B("yu"), b_sg=B("sg"), b_go=[B("go0"), B("go1")])

            def epi(mi, mt, bset, c):
                s = mi % 2
                hs, b_hs = c["hs"][s], c["b_hs"][s]
                isg = (mi % 2 == 0)
                y, b_yy = (c["yg"], c["b_yg"]) if isg else (c["yu"], c["b_yu"])
                kb.op(kb.dve, lambda e: e.tensor_copy(out=hs[:, 0:2], in_=ps[bset + 0][:, 0:2]),
                      [b_ps[bset + 0]], [b_hs])
                kb.op(kb.act, lambda e: e.activation(out=hs[:, 2:514], in_=ps[bset + 1][:, 0:512], func=AF.Identity),
                      [b_ps[bset + 1]], [b_hs])
                kb.op(kb.dve, lambda e: e.tensor_copy(out=hs[:, 514:TT], in_=ps[bset + 2][:, 0:512]),
                      [b_ps[bset + 2]], [b_hs])
                w0, w1, w2 = (cw_s[:, l, j, mt:mt + 1] for j in range(3))
                kb.op(kb.act, lambda e: e.activation(out=y[:], in_=hs[:, 2:TT], func=AF.Identity,
                                                     bias=cb_s[:, l, mt:mt + 1], scale=w2),
                      [b_hs, b_cst], [b_yy])
                kb.op(kb.dve, lambda e: e.scalar_tensor_tensor(out=y[:], in0=hs[:, 1:TT - 1], scalar=w1, in1=y[:],
                                                               op0=ALU.mult, op1=ALU.add),
                      [b_hs, b_yy, b_cst], [b_yy])
                kb.op(kb.dve, lambda e: e.scalar_tensor_tensor(out=y[:], in0=hs[:, 0:T], scalar=w0, in1=y[:],
                                                               op0=ALU.mult, op1=ALU.add),
                      [b_hs, b_yy, b_cst], [b_yy])
                if isg:
                    kb.op(kb.act, lambda e: e.activation(out=c["sg"][:], in_=y[:], func=AF.Silu),
                          [b_yy], [c["b_sg"]])
                else:
                    i = mt - MTF
                    gs = (mi // 2) % 2
                    kb.op(kb.dve, lambda e: e.tensor_tensor(out=c["go"][gs][:], in0=c["sg"][:], in1=y[:],
                                                            op=ALU.mult),
                          [c["b_sg"], b_yy], [c["b_go"][gs]])
                    kb.dma(gT[i * 128:(i + 1) * 128, :], c["go"][gs][:], c["b_go"][gs], b_g, nowaw=True)

            linear_phase(("ing", l), D, mts, xsT, b_xs, (0, TT),
                         [(0, 2, 0), (2, 512, 1), (514, 512, 2)], epi, alloc,
                         wsel=lambda mt: (("ing", l), mt) if mt < MTF else (("inu", l), mt - MTF))

        def sb_attn_phase(pre=None):
            NKB, OFF = cfg.NKB, cfg.OFF
            PR = 512
            for p in range(2 * D // PR):
                kb.coll(kv_all[p * NG * PR:(p + 1) * NG * PR, :], kv_own[p * PR:(p + 1) * PR, :], b_kvo, b_kva,
                        groups, nowaw=(p > 0))
            if pre is not None:
                pre()
            with contextlib.ExitStack() as ph:
                A = lambda n, s, d: ph.enter_context(_sbuf_tensor(n, list(s), d))
                mb = A("a_mb", [128, cfg.MW], BF16)
                b_mb = B("mb")
                kb.dma(mb[:], mbig_d.ap(), b_cin, b_mb)
                qh = [A(f"a_q{i}", [128, T], BF16) for i in range(2)]
                kh = [A(f"a_k{i}", [128, NG, T], BF16) for i in range(2)]
                vth = [A(f"a_vt{i}", [128, NG, T], BF16) for i in range(2)]
                vh = [A(f"a_v{i}", [128, NKB, 128], BF16) for i in range(2)]
                ee = [A(f"a_e{i}", [128, 512], F32) for i in range(3)]
                spb = [A(f"a_sp{i}", [128, 512], BF16) for i in range(3)]
                wb = [A(f"a_w{i}", [128, 512], BF16) for i in range(3)]
                s32 = A("a_s32", [128, 512], F32)
                sbf = [A(f"a_sbf{i}", [128, 512], BF16) for i in range(2)]
                ob = [A(f"a_o{i}", [128, 512], BF16) for i in range(2)]
                b_qh, b_kh, b_vth, b_vh = ([B("qh0"), B("qh1")], [B("kh0"), B("kh1")], [B("vt0"), B("vt1")],
                                           [B("vh0"), B("vh1")])
                b_ee, b_spb, b_wb = [B(f"e{i}") for i in range(3)], [B(f"sp{i}") for i in range(3)], [B(f"w{i}") for i in range(3)]
                b_s32, b_sbf, b_ob = B("s32"), [B("sbf0"), B("sbf1")], [B("ob0"), B("ob1")]
                kva5 = kv_all.ap().rearrange("(p r i) t -> p i r t", r=NG, i=PR)

                def kvrows(f0):
                    return kva5[f0 // PR, (f0 % PR):(f0 % PR) + 128, :, :]
                zi = 0
                si = 0
                for h in range(cfg.SBH):
                    s = h % 2
                    kb.dma(qh[s][:], qT[h * 128:(h + 1) * 128, :], b_q, b_qh[s])
                    kb.dma(kh[s][:], kvrows(h * 128), b_kva, b_kh[s])
                    kb.dma(vth[s][:], kvrows(D + h * 128), b_kva, b_vth[s])
                    for g4 in range(NKB // 4):
                        bi = 5 + (g4 % 2)
                        for j in range(4):
                            kbk = g4 * 4 + j
                            r, c = kbk // (T // 128), (kbk % (T // 128)) * 128
                            kb.op(kb.pe, lambda e: e.matmul(ps[bi][:, j * 128:(j + 1) * 128],
                                                            lhsT=vth[s][:, r, c:c + 128], rhs=ident,
                                                            start=True, stop=True),
                                  [b_vth[s], b_cst], [b_ps[bi]], inc=(j == 3))
                        kb.op(kb.dve, lambda e: e.tensor_copy(
                            out=vh[s][:, g4 * 4:(g4 + 1) * 4, :],
                            in_=ps[bi][:, :].rearrange("p (j d) -> p j d", d=128)),
                              [b_ps[bi]], [b_vh[s]])
                    for qb in range(T // 512):
                        ob_i = 3 + (qb % 2)
                        osl = (h * (T // 512) + qb) % 2
                        for n_i, kbk in enumerate(range(NKB - 1, -1, -1)):
                            zb = zi % 3
                            zi += 1
                            r, c = kbk // (T // 128), (kbk % (T // 128)) * 128
                            v0 = 512 * qb - 128 * kbk + OFF
                            kb.op(kb.pe, lambda e: e.matmul(ps[zb][:, :], lhsT=kh[s][:, r, c:c + 128],
                                                            rhs=qh[s][:, qb * 512:(qb + 1) * 512],
                                                            start=True, stop=False),
                                  [b_kh[s], b_qh[s]], [b_ps[zb]], inc=False)
                            kb.op(kb.pe, lambda e: e.matmul(ps[zb][:, :], lhsT=ident, rhs=mb[:, v0:v0 + 512],
                                                            start=False, stop=True),
                                  [b_mb, b_cst], [b_ps[zb]])
                            kb.op(kb.act, lambda e: e.activation(out=ee[zb][:], in_=ps[zb][:, :], func=AF.Exp),
                                  [b_ps[zb]], [b_ee[zb]])
                            kb.op(kb.act, lambda e: e.activation(out=spb[zb][:], in_=ee[zb][:], func=AF.Ln,
                                                                 bias=1.0, scale=1.0),
                                  [b_ee[zb]], [b_spb[zb]])
                            lastmm = (n_i == 0)
                            kb.op(kb.pe, lambda e: e.matmul(ps[zb][:, :], lhsT=negtri, rhs=spb[zb][:],
                                                            start=False, stop=lastmm),
                                  [b_spb[zb], b_cst], [b_ps[zb]], inc=lastmm)
                            if n_i > 0:
                                sl = si % 2
                                kb.op(kb.pe, lambda e: e.matmul(ps[zb][:, :], lhsT=negones, rhs=sbf[sl][:],
                                                                start=False, stop=True),
                                      [b_sbf[sl], b_cst], [b_ps[zb]])
                            kb.op(kb.act, lambda e: e.activation(out=wb[zb][:], in_=ps[zb][:, :], func=AF.Exp),
                                  [b_ps[zb]], [b_wb[zb]])
                            if n_i == 0:
                                kb.op(kb.dve, lambda e: e.tensor_copy(out=s32[:], in_=spb[zb][:]),
                                      [b_spb[zb]], [b_s32])
                            else:
                                kb.op(kb.dve, lambda e: e.tensor_tensor(out=s32[:], in0=s32[:], in1=spb[zb][:],
                                                                         op=ALU.add),
                                      [b_spb[zb], b_s32], [b_s32])
                            if n_i < NKB - 1:
                                si += 1
                                sl = si % 2
                                kb.op(kb.dve, lambda e: e.tensor_copy(out=sbf[sl][:], in_=s32[:]),
                                      [b_s32], [b_sbf[sl]])
                            kb.op(kb.pe, lambda e: e.matmul(ps[ob_i][:, :], lhsT=vh[s][:, kbk, :], rhs=wb[zb][:],
                                                            start=(n_i == 0), stop=(n_i == NKB - 1)),
                                  [b_vh[s], b_wb[zb]], [b_ps[ob_i]], inc=(n_i == NKB - 1))
                        kb.op(kb.dve, lambda e: e.tensor_copy(out=ob[osl][:], in_=ps[ob_i][:, :]),
                              [b_ps[ob_i]], [b_ob[osl]])
                        kb.dma(oT[h * 128:(h + 1) * 128, qb * 512:(qb + 1) * 512], ob[osl][:], b_ob[osl], b_o,
                               nowaw=True)
                kb.barrier()

        def swa_attn_phase(jl):
            DKV, KVH, G, SWH = cfg.DKV, cfg.KVH, cfg.G, cfg.SWH
            NQB = T // 128
            NCH = 2 * DKV // 128
            with contextlib.ExitStack() as ph:
                A = lambda n, s, d: ph.enter_context(_sbuf_tensor(n, list(s), d))
                kb.dma(kvh_own.ap(), kv_own[0:2 * DKV, T - 128:T], b_kvo, b_kvho)
                kb.coll(kvh_all.ap().opt(), kvh_own.ap().opt(), b_kvho, b_kvha, groups)
                ha = A("s_ha", [128, NG, NCH, 128], BF16)
                hsel = A("s_hsel", [128, NCH, 128], BF16)
                b_ha, b_hsel = B("ha"), B("hsel")
                kb.dma(ha[:], kvh_all.ap().rearrange("(r c p) t -> p r c t", r=NG, p=128), b_kvha, b_ha)
                for r in range(NG):
                    if r == 0:
                        kb.op(kb.dve, lambda e: e.tensor_scalar(out=hsel[:], in0=ha[:, r], scalar1=sel_s[:, r:r + 1],
                                                                scalar2=None, op0=ALU.mult),
                              [b_ha, b_cst], [b_hsel])
                    else:
                        kb.op(kb.dve, lambda e: e.scalar_tensor_tensor(out=hsel[:], in0=ha[:, r],
                                                                       scalar=sel_s[:, r:r + 1], in1=hsel[:],
                                                                       op0=ALU.mult, op1=ALU.add),
                              [b_ha, b_hsel, b_cst], [b_hsel])
                kb.dma(kvh_sel.ap().rearrange("(c p) t -> p c t", p=128), hsel[:], b_hsel, b_kvhs)
                bt = A("s_bt", [128, SWH, 256], BF16)
                hm = A("s_hm", [128, 128], BF16)
                sk = A("s_sk", [128, SWH], F32)
                esk = A("s_esk", [128, SWH], F32)
                b_bt, b_sk, b_esk = B("bt"), B("sk"), B("esk")
                kb.dma(bt[:], bt_d.ap(), b_cin, b_bt)
                kb.dma(hm[:], hm_d.ap(), b_cin, b_bt, nowaw=True)
                kb.dma(sk[:], sink_d[:, jl, :], b_cin, b_sk)
                kb.op(kb.act, lambda e: e.activation(out=esk[:], in_=sk[:], func=AF.Exp), [b_sk], [b_esk])
                k2 = [A(f"s_k2{i}", [128, 128 + T], BF16) for i in range(2)]
                vt = [A(f"s_vt{i}", [64, 128 + T], BF16) for i in range(2)]
                vg = [A(f"s_vg{i}", [128, NQB + 1, 64], BF16) for i in range(2)]
                qs = [A(f"s_q{i}", [128, T], BF16) for i in range(2)]
                pc = [A(f"s_pc{i}", [128, T], BF16) for i in range(2)]
                pp = [A(f"s_pp{i}", [128, T], BF16) for i in range(2)]
                dn = A("s_dn", [64, T], F32)
                rd = A("s_rd", [64, T], F32)
                oo = [A(f"s_oo{i}", [64, T], BF16) for i in range(2)]
                b_k2, b_vt, b_vg, b_qs = ([B("k20"), B("k21")], [B("vt0"), B("vt1")], [B("vg0"), B("vg1")],
                                          [B("qs0"), B("qs1")])
                b_pc, b_pp, b_dn, b_rd, b_oo = ([B("pc0"), B("pc1")], [B("pp0"), B("pp1")], B("dn"), B("rd"),
                                                [B("oo0"), B("oo1")])
                hi = 0
                for g in range(KVH):
                    s = g % 2
                    for half in range(2):
                        kb.dma(k2[s][half * 64:(half + 1) * 64, 0:128], kvh_sel[g * 64:(g + 1) * 64, :], b_kvhs,
                               b_k2[s], nowaw=(half == 1))
                        kb.dma(k2[s][half * 64:(half + 1) * 64, 128:128 + T], kv_own[g * 64:(g + 1) * 64, :],
                               b_kvo, b_k2[s], nowaw=True)
                    kb.dma(vt[s][:, 0:128], kvh_sel[DKV + g * 64:DKV + (g + 1) * 64, :], b_kvhs, b_vt[s])
                    kb.dma(vt[s][:, 128:128 + T], kv_own[DKV + g * 64:DKV + (g + 1) * 64, :], b_kvo, b_vt[s],
                           nowaw=True)
                    nblk = NQB + 1
                    for b0 in range(0, nblk, 8):
                        bi = 0 + ((b0 // 8) % 2)
                        nb_ = min(8, nblk - b0)
                        for j in range(nb_):
                            blk = b0 + j
                            kb.op(kb.pe, lambda e: e.matmul(ps[bi][:, j * 64:(j + 1) * 64],
                                                            lhsT=vt[s][:, blk * 128:(blk + 1) * 128],
                                                            rhs=cst[0:64, 0, 0:64], start=True, stop=True),
                                  [b_vt[s], b_cst], [b_ps[bi]], inc=(j == nb_ - 1))
                        kb.op(kb.dve, lambda e: e.tensor_copy(
                            out=vg[s][:, b0:b0 + nb_, :],
                            in_=ps[bi][:, 0:nb_ * 64].rearrange("p (j d) -> p j d", d=64)),
                              [b_ps[bi]], [b_vg[s]])
                    for gi in range(G):
                        h = g * G + gi
                        par = h % 2
                        qsl = (h // 2) % 2
                        if par == 0:
                            kb.dma(qs[qsl][:], qT[(h // 2) * 128:(h // 2 + 1) * 128, :], b_q, b_qs[qsl])
                        hs_ = hi % 2
                        hi += 1
                        P0, P1 = par * 64, (par + 1) * 64
                        for i in range(NQB):
                            bi = 0 + i // 4
                            reg = ps[bi][:, (i % 4) * 128:(i % 4 + 1) * 128]
                            kb.op(kb.pe, lambda e: e.matmul(reg, lhsT=k2[s][P0:P1, 128 + i * 128:256 + i * 128],
                                                            rhs=qs[qsl][P0:P1, i * 128:(i + 1) * 128],
                                                            start=True, stop=False),
                                  [b_k2[s], b_qs[qsl]], [b_ps[bi]], inc=False)
                            kb.op(kb.pe, lambda e: e.matmul(reg, lhsT=ident, rhs=bt[:, h, 0:128],
                                                            start=False, stop=True),
                                  [b_bt, b_cst], [b_ps[bi]], inc=(i % 4 == 3))
                        for i in range(NQB):
                            bi = 2 + i // 4
                            reg = ps[bi][:, (i % 4) * 128:(i % 4 + 1) * 128]
                            kb.op(kb.pe, lambda e: e.matmul(reg, lhsT=k2[s][P0:P1, i * 128:(i + 1) * 128],
                                                            rhs=qs[qsl][P0:P1, i * 128:(i + 1) * 128],
                                                            start=True, stop=False),
                                  [b_k2[s], b_qs[qsl]], [b_ps[bi]], inc=False)
                            if i == 0:
                                kb.op(kb.pe, lambda e: e.matmul(reg, lhsT=ident, rhs=hm[:, :], start=False,
                                                                stop=False),
                                      [b_bt, b_cst], [b_ps[bi]], inc=False)
                            kb.op(kb.pe, lambda e: e.matmul(reg, lhsT=ident, rhs=bt[:, h, 128:256],
                                                            start=False, stop=True),
                                  [b_bt, b_cst], [b_ps[bi]], inc=(i % 4 == 3))
                        for half in range(2):
                            kb.op(kb.act, lambda e: e.activation(out=pc[hs_][:, half * 512:(half + 1) * 512],
                                                                 in_=ps[0 + half][:, :], func=AF.Exp),
                                  [b_ps[0 + half]], [b_pc[hs_]])
                            kb.op(kb.act, lambda e: e.activation(out=pp[hs_][:, half * 512:(half + 1) * 512],
                                                                 in_=ps[2 + half][:, :], func=AF.Exp),
                                  [b_ps[2 + half]], [b_pp[hs_]])
                        for i in range(NQB):
                            bi = 4 + i // 4
                            reg = ps[bi][0:64, (i % 4) * 128:(i % 4 + 1) * 128]
                            kb.op(kb.pe, lambda e: e.matmul(reg, lhsT=vg[s][:, i + 1, :],
                                                            rhs=pc[hs_][:, i * 128:(i + 1) * 128],
                                                            start=True, stop=False),
                                  [b_vg[s], b_pc[hs_]], [b_ps[bi]], inc=False)
                            kb.op(kb.pe, lambda e: e.matmul(reg, lhsT=vg[s][:, i, :],
                                                            rhs=pp[hs_][:, i * 128:(i + 1) * 128],
                                                            start=False, stop=True),
                                  [b_vg[s], b_pp[hs_]], [b_ps[bi]], inc=(i % 4 == 3))
                        for half in range(2):
                            bi = 6 + half
                            kb.op(kb.pe, lambda e: e.matmul(ps[bi][0:64, :], lhsT=cst[:, 2, 0:64],
                                                            rhs=pc[hs_][:, half * 512:(half + 1) * 512],
                                                            start=True, stop=False),
                                  [b_pc[hs_], b_cst], [b_ps[bi]], inc=False)
                            kb.op(kb.pe, lambda e: e.matmul(ps[bi][0:64, :], lhsT=cst[:, 2, 0:64],
                                                            rhs=pp[hs_][:, half * 512:(half + 1) * 512],
                                                            start=False, stop=True),
                                  [b_pp[hs_], b_cst], [b_ps[bi]])
                            kb.op(kb.dve, lambda e: e.tensor_scalar(out=dn[:, half * 512:(half + 1) * 512],
                                                                    in0=ps[bi][0:64, :], scalar1=esk[0:64, h:h + 1],
                                                                    scalar2=None, op0=ALU.add),
                                  [b_ps[bi], b_esk], [b_dn])
                        kb.op(kb.dve, lambda e: e.reciprocal(out=rd[:], in_=dn[:]), [b_dn], [b_rd])
                        for half in range(2):
                            kb.op(kb.dve, lambda e: e.tensor_tensor(out=oo[hs_][:, half * 512:(half + 1) * 512],
                                                                    in0=ps[4 + half][0:64, :],
                                                                    in1=rd[:, half * 512:(half + 1) * 512],
                                                                    op=ALU.mult),
                                  [b_ps[4 + half], b_rd], [b_oo[hs_]])
                        kb.dma(oT[h * 64:(h + 1) * 64, :], oo[hs_][:], b_oo[hs_], b_o, nowaw=True)
                kb.barrier()

        def halo_phase(h_t, b_h):
            with contextlib.ExitStack() as ph:
                A = lambda n, s, d: ph.enter_context(_sbuf_tensor(n, list(s), d))
                kb.dma(hal_own[:, 0:2 * KC].rearrange("p (k c) -> p k c", c=2),
                       h_t[:, TT - 2:TT].rearrange("(k p) c -> p k c", p=128), b_h, b_halo)
                kb.coll(hal_all.ap().opt(), hal_own.ap().opt(), b_halo, b_hala, groups)
                ha = A("h_ha", [128, NG, KC, 2], F32)
                hs_ = A("h_hs", [128, KC, 2], F32)
                b_ha, b_hs2 = B("hha"), B("hhs")
                kb.dma(ha[:], hal_all[:, 0:2 * KC].rearrange("(r p) (k c) -> p r k c", r=NG, c=2), b_hala, b_ha)
                for r in range(NG):
                    if r == 0:
                        kb.op(kb.dve, lambda e: e.tensor_scalar(out=hs_[:], in0=ha[:, r], scalar1=sel_s[:, r:r + 1],
                                                                scalar2=None, op0=ALU.mult),
                              [b_ha, b_cst], [b_hs2])
                    else:
                        kb.op(kb.dve, lambda e: e.scalar_tensor_tensor(out=hs_[:], in0=ha[:, r],
                                                                       scalar=sel_s[:, r:r + 1], in1=hs_[:],
                                                                       op0=ALU.mult, op1=ALU.add),
                              [b_ha, b_hs2, b_cst], [b_hs2])
                kb.dma(h_t[:, 0:2].rearrange("(k p) c -> p k c", p=128), hs_[:], b_hs2, b_h)
                kb.barrier()

        _pc = [0]

        def _lim(fn):
            def w(*a, **k):
                _pc[0] += 1
                if cfg.stop is not None and _pc[0] > cfg.stop:
                    return
                print("phase", _pc[0], fn.__name__, flush=True) if cfg.debug else None
                return fn(*a, **k)
            return w
        norm_phase, qkv_phase, sb_attn_phase, swa_attn_phase, resid_phase, halo_phase, ffn_in_phase = map(
            _lim, (norm_phase, qkv_phase, sb_attn_phase, swa_attn_phase, resid_phase, halo_phase, ffn_in_phase))
        for l in range(depth):
            norm_phase(hA, b_hA, lambda kc, l=l: gA_s[:, l, kc:kc + 1], "xs")
            qkv_phase(l)
            nxt = (l + 1 < depth) and (cfg.stop is None)
            if l % 2 == 0:
                sb_attn_phase(pre=(lambda l=l: emit_gather(l + 1, ("qkv", "o", "ing"))) if nxt else None)
            else:
                swa_attn_phase(l // 2)
            resid_phase(("o", l), D, oT, b_o, hA, b_hA, hB, b_hB, dbg.get(("mid", l)))
            halo_phase(hB, b_hB)
            if nxt:
                emit_gather(l + 1, ("inu", "down") if l % 2 == 0 else ("qkv", "o", "ing", "inu", "down"))
            norm_phase(hB, b_hB, lambda kc, l=l: gF_s[:, l, kc:kc + 1], "xs")
            ffn_in_phase(l)
            resid_phase(("down", l), DFF, gT, b_g, hB, b_hB, hA, b_hA, dbg.get(("h", l)))
        norm_phase(hA, b_hA, lambda kc: gO_s[:, kc:kc + 1], "y")
        kb.barrier()
    return nc


def host_tables(cfg, core):
    T, NG = cfg.T, cfg.NG
    j = core % NG
    q0 = j * T
    p = np.arange(128)[:, None]
    v = np.arange(cfg.MW)[None, :]
    mbig = np.where(p < q0 + v - cfg.OFF, 0.0, NEG).astype(ml_dtypes.bfloat16)
    sel = np.zeros((128, NG), np.float32)
    if j > 0:
        sel[:, j - 1] = 1.0
    slopes = (2.0 ** (-8.0 * np.arange(1, cfg.SWH + 1) / cfg.SWH)).astype(np.float32)
    k = np.arange(128)[:, None]
    q = np.arange(128)[None, :]
    bt = np.zeros((128, cfg.SWH, 256), np.float32)
    for h in range(cfg.SWH):
        bt[:, h, 0:128] = np.where(k <= q, -slopes[h] * (q - k), NEG)
        bt[:, h, 128:256] = np.where(k > q, -slopes[h] * (128 + q - k), NEG)
    hm = np.full((128, 128), 0.0 if j > 0 else NEG, np.float32)
    cst = np.zeros((128, 4, 128), np.float32)
    cst[:, 0, :] = np.eye(128)
    cst[:, 1, :] = -(k >= q).astype(np.float32)
    cst[:, 2, :] = 1.0
    cst[:, 3, :] = -1.0
    return dict(mbig=mbig, sel=sel, bt=bt.astype(ml_dtypes.bfloat16), hm=hm.astype(ml_dtypes.bfloat16),
                cst=cst.astype(ml_dtypes.bfloat16))


def make_in_maps(cfg, inp):
    D, T, NG, NC, depth, KC, MTF = cfg.D, cfg.T, cfg.NG, cfg.NC, cfg.depth, cfg.KC, cfg.MTF
    S = NG * T

    def fm(a):
        dd, F = a.shape
        return np.ascontiguousarray(a.reshape(dd, F // 128, 128).transpose(2, 0, 1)).astype(np.float32)

    common = {}
    common["gA"] = fm(inp["attn_norm"])
    common["gF"] = fm(inp["ffn_norm"])
    common["gO"] = np.ascontiguousarray(inp["final_norm"].reshape(KC, 128).T).astype(np.float32)
    cw = inp["ffn_conv_w"]
    common["cw"] = np.ascontiguousarray(cw.reshape(depth, 3, 2 * MTF, 128).transpose(3, 0, 1, 2)).astype(np.float32)
    common["cb"] = fm(inp["ffn_conv_b"])
    common["sink"] = np.ascontiguousarray(np.broadcast_to(inp["swa_sinks"][None], (128,) + inp["swa_sinks"].shape)).astype(np.float32)
    wl = {}
    for l in range(depth):
        j = l // 2
        if l % 2 == 0:
            wl[("qkv", l)] = inp["sb_w_qkv"][j]
            wl[("o", l)] = inp["sb_w_o"][j]
        else:
            wl[("qkv", l)] = inp["swa_w_qkv"][j]
            wl[("o", l)] = inp["swa_w_o"][j]
        wl[("ing", l)] = inp["ffn_w_in"][l][:, :cfg.DFF]
        wl[("inu", l)] = inp["ffn_w_in"][l][:, cfg.DFF:]
        wl[("down", l)] = inp["ffn_w_down"][l]
    wblk = {}
    for key, w in wl.items():
        K, N = w.shape
        wblk[key] = np.ascontiguousarray(w.reshape(K // 128, 128, N // 128, 128).transpose(2, 1, 0, 3)).reshape(N, K)
    maps = []
    for c in range(NC):
        b, j = c // NG, c % NG
        m = dict(common)
        m["xT"] = np.ascontiguousarray(inp["x"][b, j * T:(j + 1) * T, :].T)
        for (nm, l), w in wl.items():
            K = w.shape[0]
            wb = wblk[(nm, l)]
            Nw, Kw = wb.shape
            nr = cfg.piece_rows(Nw, Kw)
            P = (Nw // NG) // nr
            m[f"w_{nm}{l}"] = np.ascontiguousarray(wb.reshape(P, NG, nr, Kw)[:, j].reshape(P * nr, Kw))
        m.update(host_tables(cfg, c))
        maps.append(m)
    return maps


_CACHE = {}


def run(cfg, inp):
    key = (cfg.D, cfg.T, cfg.G, cfg.depth, cfg.debug)
    if key not in _CACHE:
        _CACHE[key] = build_program(cfg)
    nc = _CACHE[key]
    maps = make_in_maps(cfg, inp)
    res = run_bass_kernel_spmd(nc, maps, core_ids=list(range(cfg.NC)))
    return res


def kernel(x, attn_norm, ffn_norm, sb_w_qkv, sb_w_o, swa_w_qkv, swa_w_o, swa_sinks,
           ffn_w_in, ffn_conv_w, ffn_conv_b, ffn_w_down, final_norm):
    cfg = Cfg()
    inp = dict(x=np.asarray(x), attn_norm=np.asarray(attn_norm), ffn_norm=np.asarray(ffn_norm),
               sb_w_qkv=np.asarray(sb_w_qkv), sb_w_o=np.asarray(sb_w_o), swa_w_qkv=np.asarray(swa_w_qkv),
               swa_w_o=np.asarray(swa_w_o), swa_sinks=np.asarray(swa_sinks), ffn_w_in=np.asarray(ffn_w_in),
               ffn_conv_w=np.asarray(ffn_conv_w), ffn_conv_b=np.asarray(ffn_conv_b),
               ffn_w_down=np.asarray(ffn_w_down), final_norm=np.asarray(final_norm))
    res = run(cfg, inp)
    out = np.empty((cfg.NB, cfg.NG * cfg.T, cfg.D), np.float32)
    for c in range(cfg.NC):
        b, j = c // cfg.NG, c % cfg.NG
        out[b, j * cfg.T:(j + 1) * cfg.T, :] = res.results[c]["yT"].T
    return out
```

```python
import contextlib
import os
import math
import numpy as np
import ml_dtypes
import concourse.bass as bass
import concourse.mybir as mybir
from concourse.bass_utils import run_bass_kernel_spmd

F32 = mybir.dt.float32
BF16 = mybir.dt.bfloat16
AF = mybir.ActivationFunctionType
ALU = mybir.AluOpType
NEG = -30000.0


class Cfg:
    def __init__(self, D=4096, T=1024, NG=4, NB=2, G=8, depth=4, debug=False):
        self.D, self.T, self.NG, self.NB, self.G, self.depth = D, T, NG, NB, G, depth
        self.NC = NG * NB
        self.KC = D // 128
        self.SBH = D // 128
        self.SWH = D // 64
        self.KVH = self.SWH // G
        self.DKV = self.KVH * 64
        self.DFF = 7 * D // 2
        self.MTF = self.DFF // 128
        self.TT = T + 2
        self.NKB = NG * T // 128
        self.OFF = 128 * (self.NKB - 1)
        self.MW = self.OFF + 2 * 512
        self.debug = debug
        self.stop = None

    def piece_rows(self, K, N, esize=2):
        q = K // self.NG
        mx = max(1, (1 << 20) // (esize * N))
        nr = 1
        for d in range(1, q + 1):
            if q % d == 0 and d <= mx:
                nr = d
        return nr

    def nqkv(self, l):
        return 3 * self.D if l % 2 == 0 else self.D + 2 * self.DKV


class Sem:
    def __init__(self, h):
        self.h = h
        self.cnt = 0


class Buf:
    def __init__(self, name, ap=None):
        self.name = name
        self.ap = ap
        self.w = None
        self.r = {}
        self.dsem = None


class Eng:
    def __init__(self, name, e, sem):
        self.name, self.e, self.sem = name, e, sem
        self.seen = {}


class KB:
    def __init__(self, nc, stack):
        self.nc = nc
        self.stack = stack
        self.sems = []
        mk = lambda n, e: Eng(n, e, self.new_sem("s_" + n))
        self.pe = mk("pe", nc.tensor)
        self.act = mk("act", nc.scalar)
        self.dve = mk("dve", nc.vector)
        self.pool = mk("pool", nc.gpsimd)
        self.sp = mk("sp", nc.sync)
        self.engs = [self.pe, self.act, self.dve, self.pool, self.sp]
        self.local_bufs = []
        self.free_sems = []
        self.nobar_sems = set()

    def new_sem(self, name):
        s = Sem(self.stack.enter_context(self.nc.semaphore(f"{name}_{len(self.sems)}")))
        self.sems.append(s)
        return s

    def buf(self, name, persistent=False):
        b = Buf(name)
        if not persistent:
            self.local_bufs.append(b)
        return b

    def get_dsem(self, b, prefix):
        if b.dsem is None:
            if b in self.local_bufs and self.free_sems:
                b.dsem = self.free_sems.pop()
            else:
                b.dsem = self.new_sem(prefix + b.name)
                if getattr(b, "nobar", False):
                    self.nobar_sems.add(b.dsem)
        return b.dsem

    def wait(self, E, sem, cnt):
        if cnt <= 0 or E.seen.get(sem, 0) >= cnt:
            return
        if sem is E.sem and E is self.pe:
            return
        E.e.wait_ge(sem.h, cnt)
        E.seen[sem] = cnt

    def deps(self, E, reads, writes, nowaw=False):
        for b in reads:
            if b.w:
                self.wait(E, *b.w)
        for b in writes:
            if b.w and not nowaw:
                self.wait(E, *b.w)
            for sem, c in list(b.r.items()):
                self.wait(E, sem, c)

    def op(self, E, fn, reads, writes, inc=True):
        self.deps(E, reads, writes)
        ins = fn(E.e)
        if inc:
            E.sem.cnt += 1
            ins.then_inc(E.sem.h, 1)
            c = E.sem.cnt
        else:
            c = E.sem.cnt + 1
        for b in reads:
            b.r[E.sem] = max(b.r.get(E.sem, 0), c)
        for b in writes:
            b.w = (E.sem, c)
            b.r = {}
        return ins

    def dma(self, out_ap, in_ap, src, dst, nowaw=False, Q=None):
        Q = Q or self.sp
        self.deps(Q, [src], [dst], nowaw=nowaw)
        sem = self.get_dsem(dst, "d_")
        ins = Q.e.dma_start(out=out_ap, in_=in_ap)
        sem.cnt += 16
        ins.then_inc(sem.h, 16)
        src.r[sem] = sem.cnt
        dst.w = (sem, sem.cnt)
        if not nowaw:
            dst.r = {}

    def coll(self, out_ap, in_ap, src, dst, groups, nowaw=False):
        Q = self.pool
        self.deps(Q, [src], [dst], nowaw=nowaw)
        sem = self.get_dsem(dst, "c_")
        ins = Q.e.collective_compute("AllGather", ALU.bypass, replica_groups=groups,
                                     ins=[in_ap], outs=[out_ap])
        sem.cnt += 1
        ins.then_inc(sem.h, 1)
        src.r[sem] = sem.cnt
        dst.w = (sem, sem.cnt)
        if not nowaw:
            dst.r = {}

    def barrier(self):
        for E in self.engs:
            for s in self.sems:
                if s is E.sem or s in self.nobar_sems:
                    continue
                self.wait(E, s, s.cnt)
        for b in self.local_bufs:
            if b.dsem is not None:
                self.free_sems.append(b.dsem)
                b.dsem = None
        self.local_bufs = []


def build_program(cfg):
    nc = bass.Bass("TRN2", target_bir_lowering=False)
    _orig_sbuf_tensor = nc.sbuf_tensor
    _uid = [0]

    def _sbuf_tensor(name, shape, dt):
        _uid[0] += 1
        return _orig_sbuf_tensor(f"{name}_u{_uid[0]}", shape, dt)

    D, T, TT, KC, NG, NC = cfg.D, cfg.T, cfg.TT, cfg.KC, cfg.NG, cfg.NC
    DFF, MTF, depth = cfg.DFF, cfg.MTF, cfg.depth
    groups = [list(range(b * NG, (b + 1) * NG)) for b in range(cfg.NB)]
    allg = [list(range(NC))]

    def din(name, shape, dt=F32):
        return nc.dram_tensor(name, list(shape), dt, kind="ExternalInput")

    def dint(name, shape, dt=F32):
        return nc.dram_tensor(name, list(shape), dt)

    xT = din("xT", [D, T])
    wsh = {}
    wdims = {}
    for l in range(depth):
        wdims[("qkv", l)] = (D, cfg.nqkv(l))
        wdims[("o", l)] = (D, D)
        wdims[("ing", l)] = (D, DFF)
        wdims[("inu", l)] = (D, DFF)
        wdims[("down", l)] = (DFF, D)
    for (nm, l), (K, N) in wdims.items():
        wsh[(nm, l)] = din(f"w_{nm}{l}", [N // NG, K])
    gA = din("gA", [128, depth, KC])
    gF = din("gF", [128, depth, KC])
    gO = din("gO", [128, KC])
    cw_d = din("cw", [128, depth, 3, 2 * MTF])
    cb_d = din("cb", [128, depth, 2 * MTF])
    sink_d = din("sink", [128, depth // 2, cfg.SWH])
    mbig_d = din("mbig", [128, cfg.MW], BF16)
    sel_d = din("sel", [128, NG])
    bt_d = din("bt", [128, cfg.SWH, 256], BF16)
    hm_d = din("hm", [128, 128], BF16)
    cst_d = din("cst", [128, 4, 128], BF16)
    yT = nc.dram_tensor("yT", [D, T], F32, kind="ExternalOutput")
    dbg = {}
    if cfg.debug:
        for l in range(depth):
            dbg[("mid", l)] = nc.dram_tensor(f"dbg_mid{l}", [D, T], F32, kind="ExternalOutput")
            dbg[("h", l)] = nc.dram_tensor(f"dbg_h{l}", [D, T], F32, kind="ExternalOutput")

    wbn = {k: dint(f"wb_{k[0]}{k[1]}", [wdims[k][1] // NG, wdims[k][0]], BF16) for k in wdims}
    wfl = {k: dint(f"wf_{k[0]}{k[1]}", [wdims[k][1], wdims[k][0]], BF16) for k in wdims}
    TTP = T + 32
    hA = dint("hA", [D, TTP])
    hB = dint("hB", [D, TTP])
    xsT = dint("xsT", [D, TTP], BF16)
    qT = dint("qT", [D, T], BF16)
    kv_own = dint("kv_own", [2 * D, T], BF16)
    kv_all = dint("kv_all", [NG * 2 * D, T], BF16)
    oT = dint("oT", [D, T], BF16)
    gT = dint("gT", [DFF, T], BF16)
    HW = max(16, 2 * KC)
    hal_own = dint("hal_own", [128, HW])
    hal_all = dint("hal_all", [NG * 128, HW])
    kvh_own = dint("kvh_own", [2 * cfg.DKV, 128], BF16)
    kvh_all = dint("kvh_all", [NG * 2 * cfg.DKV, 128], BF16)
    kvh_sel = dint("kvh_sel", [2 * cfg.DKV, 128], BF16)

    with contextlib.ExitStack() as stack:
        kb = KB(nc, stack)
        PB = lambda n: kb.buf(n, persistent=True)
        B = PB
        b_x = B("xT")
        b_wsh = {k: B("wsh") for k in wdims}
        b_wbn = {k: B(f"wbn{k[0]}{k[1]}") for k in wdims}
        b_wfl = {k: B(f"wfl{k[0]}{k[1]}") for k in wdims}
        for _d in (b_wsh, b_wbn, b_wfl):
            for _b in _d.values():
                _b.nobar = True
        b_hA, b_hB, b_xs, b_q, b_kvo, b_kva, b_o, b_g = (B("hA"), B("hB"), B("xsT"), B("qT"), B("kvo"),
                                                         B("kva"), B("oT"), B("gT"))
        b_halo, b_hala, b_kvho, b_kvha, b_kvhs = B("halo"), B("hala"), B("kvho"), B("kvha"), B("kvhs")
        b_y = B("yT")
        b_dbg = B("dbg")
        b_cin = B("cin")

        B = kb.buf

        def sb(name, shape, dt):
            return stack.enter_context(_sbuf_tensor(name, list(shape), dt))

        cst = sb("cst", [128, 4, 128], BF16)
        gA_s = sb("gA_s", [128, depth, KC], F32)
        gF_s = sb("gF_s", [128, depth, KC], F32)
        gO_s = sb("gO_s", [128, KC], F32)
        cw_s = sb("cw_s", [128, depth, 3, 2 * MTF], F32)
        cb_s = sb("cb_s", [128, depth, 2 * MTF], F32)
        sel_s = sb("sel_s", [128, NG], F32)
        b_cst = PB("cst")
        for t_s, t_d in ((cst, cst_d), (gA_s, gA), (gF_s, gF), (gO_s, gO), (cw_s, cw_d), (cb_s, cb_d),
                         (sel_s, sel_d)):
            kb.dma(t_s[:], t_d.ap(), b_cin, b_cst, nowaw=True)
        ident = cst[:, 0, :]
        negtri = cst[:, 1, :]
        ones_b = cst[:, 2, :]
        negones = cst[:, 3, :]

        ps = [stack.enter_context(nc.psum_tensor(f"ps{i}", [128, 512], F32)) for i in range(8)]
        b_ps = [PB(f"ps{i}") for i in range(8)]

        def emit_gather(l, names=("qkv", "o", "ing", "inu", "down")):
            ks = [(n_, l) for n_ in names]
            for k in ks:
                Kk, Nk = wdims[k]
                rows = Nk // NG
                rstep = max(1, min(rows, (4 << 20) // (4 * Kk)))
                for r0 in range(0, rows, rstep):
                    r1 = min(rows, r0 + rstep)
                    kb.dma(wbn[k][r0:r1, :], wsh[k][r0:r1, :], b_wsh[k], b_wbn[k], nowaw=(r0 > 0), Q=kb.pool)
            for k in ks:
                Kk, Nk = wdims[k]
                nr = cfg.piece_rows(Nk, Kk)
                for p in range((Nk // NG) // nr):
                    kb.coll(wfl[k][p * NG * nr:(p + 1) * NG * nr, :], wbn[k][p * nr:(p + 1) * nr, :],
                            b_wbn[k], b_wfl[k], groups, nowaw=(p > 0))

        emit_gather(0, ("qkv",))
        with contextlib.ExitStack() as ph:
            if os.environ.get("SKIP_INIT"):
                raise_skip = True
            zt = ph.enter_context(_sbuf_tensor("zt", [128, KC, 2], F32))
            b_zt = B("zt")
            kb.op(kb.dve, lambda e: e.memset(zt[:], 0.0), [], [b_zt])
            if not os.environ.get("SKIP_INIT"):
                kb.dma(hA[:, 0:2].rearrange("(k p) c -> p k c", p=128), zt[:], b_zt, b_hA)
                kb.dma(hB[:, 0:2].rearrange("(k p) c -> p k c", p=128), zt[:], b_zt, b_hB, )
            if not os.environ.get("SKIP_X"):
                kb.dma(hA[:, 2:TT], xT.ap(), b_x, b_hA, nowaw=True)
            kb.barrier()

        def norm_phase(h_src, b_src, gain_ap_fn, dst_kind):
            with contextlib.ExitStack() as ph:
                hb = [ph.enter_context(_sbuf_tensor(f"n_hb{i}", [128, TT], F32)) for i in range(2)]
                sq = [ph.enter_context(_sbuf_tensor(f"n_sq{i}", [128, TT], BF16)) for i in range(2)]
                rt = ph.enter_context(_sbuf_tensor("n_rt", [128, TT], F32))
                rb = ph.enter_context(_sbuf_tensor("n_rb", [128, TT], F32))
                odt = BF16 if dst_kind == "xs" else F32
                ob = [ph.enter_context(_sbuf_tensor(f"n_ob{i}", [128, TT], odt)) for i in range(2)]
                b_hb = [B("hb0"), B("hb1")]
                b_sq = [B("sq0"), B("sq1")]
                b_rt, b_rb = B("rt"), B("rb")
                b_ob = [B("ob0"), B("ob1")]
                tiles = [(0, 512, 0), (512, 512, 1), (1024, TT - 1024, 2)]
                import os
                CUT = int(os.environ.get("NORM_CUT", "99"))
                for kc in range(KC):
                    s = kc % 2
                    kb.dma(hb[s][:], h_src[kc * 128:(kc + 1) * 128, 0:TT], b_src, b_hb[s])
                    if CUT < 2:
                        continue
                    kb.op(kb.act, lambda e: e.activation(out=sq[s][:], in_=hb[s][:], func=AF.Square),
                          [b_hb[s]], [b_sq[s]])
                    if CUT < 3:
                        continue
                    for (c0, n, bi) in tiles:
                        kb.op(kb.pe, lambda e: e.matmul(ps[bi][:, 0:n], lhsT=ones_b, rhs=sq[s][:, c0:c0 + n],
                                                        start=(kc == 0), stop=(kc == KC - 1)),
                              [b_sq[s], b_cst], [b_ps[bi]], inc=(kc == KC - 1 or bi == 2))
                if CUT < 4:
                    kb.barrier()
                    return
                for (c0, n, bi) in tiles:
                    kb.op(kb.act, lambda e: e.activation(out=rt[:, c0:c0 + n], in_=ps[bi][:, 0:n], func=AF.Sqrt,
                                                         bias=1e-6, scale=1.0 / D),
                          [b_ps[bi]], [b_rt])
                kb.op(kb.dve, lambda e: e.reciprocal(out=rb[:], in_=rt[:]), [b_rt], [b_rb])
                if CUT < 5:
                    kb.barrier()
                    return
                for kc in range(KC):
                    s = kc % 2
                    kb.dma(hb[s][:], h_src[kc * 128:(kc + 1) * 128, 0:TT], b_src, b_hb[s])
                    kb.op(kb.dve, lambda e: e.scalar_tensor_tensor(out=ob[s][:], in0=hb[s][:],
                                                                   scalar=gain_ap_fn(kc), in1=rb[:],
                                                                   op0=ALU.mult, op1=ALU.mult),
                          [b_hb[s], b_rb, b_cst], [b_ob[s]])
                    if dst_kind == "xs":
                        kb.dma(xsT[kc * 128:(kc + 1) * 128, 0:TT], ob[s][:], b_ob[s], b_xs, nowaw=True)
                    else:
                        kb.dma(yT[kc * 128:(kc + 1) * 128, :], ob[s][:, 2:TT], b_ob[s], b_y, nowaw=True)
                kb.barrier()

        def linear_phase(wkey, K, mtiles, src, b_srcact, src_cols, tok_tiles, epilogue, ph_alloc=None,
                         kseg=32, wsel=None):
            KCl = K // 128
            nseg = (KCl + kseg - 1) // kseg
            assert KCl % nseg == 0
            kseg = KCl // nseg
            c0s, ns = src_cols
            if wsel is None:
                wsel = lambda mt: (wkey, mt)
            with contextlib.ExitStack() as ph:
                insb = ph.enter_context(_sbuf_tensor("l_in", [128, KCl, ns], BF16))
                wbf = [ph.enter_context(_sbuf_tensor(f"l_wbf{i}", [128, kseg * 128], BF16)) for i in range(3)]
                b_in = B("l_in")
                b_wbf = [B("wbf0"), B("wbf1"), B("wbf2")]
                ctx = ph_alloc(ph) if ph_alloc else None
                step = max(1, KCl // 4)
                for k0 in range(0, KCl, step):
                    k1 = min(KCl, k0 + step)
                    kb.dma(insb[:, k0:k1, :],
                           src[k0 * 128:k1 * 128, c0s:c0s + ns].rearrange("(k p) t -> p k t", p=128),
                           b_srcact, b_in, nowaw=True)
                it = 0
                for mi, mt in enumerate(mtiles):
                    bset = (mi % 2) * 4
                    wk, mtl = wsel(mt)
                    W = wfl[wk]
                    for sg in range(nseg):
                        s2, s3 = it % 2, it % 3
                        it += 1
                        c0w = sg * kseg * 128
                        kb.dma(wbf[s3][:], W[mtl * 128:(mtl + 1) * 128, c0w:c0w + kseg * 128],
                               b_wfl[wk], b_wbf[s3])
                        for kc in range(kseg):
                            gk = sg * kseg + kc
                            last = (gk == KCl - 1)
                            for ti, (t0, n, bi) in enumerate(tok_tiles):
                                kb.op(kb.pe, lambda e: e.matmul(ps[bset + bi][:, 0:n],
                                                                lhsT=wbf[s3][:, kc * 128:(kc + 1) * 128],
                                                                rhs=insb[:, gk, t0:t0 + n],
                                                                start=(gk == 0), stop=last),
                                      [b_wbf[s3], b_in], [b_ps[bset + bi]],
                                      inc=(last or (kc == kseg - 1 and ti == len(tok_tiles) - 1)))
                    epilogue(mi, mt, bset, ctx)
                kb.barrier()

        def qkv_phase(l):
            sbl = (l % 2 == 0)
            nq = cfg.nqkv(l) // 128
            qscale = (128 ** -0.5) if sbl else 0.125

            def alloc(ph):
                o = [ph.enter_context(_sbuf_tensor(f"q_o{i}", [128, T], BF16)) for i in range(2)]
                return (o, [B("qo0"), B("qo1")])

            def epi(mi, mt, bset, ctx):
                o, b_o2 = ctx
                s = mi % 2
                sc = qscale if mt < KC else 1.0
                for (t0, n, bi) in ((0, 512, 0), (512, 512, 1)):
                    kb.op(kb.act, lambda e: e.activation(out=o[s][:, t0:t0 + n], in_=ps[bset + bi][:, 0:n],
                                                         func=AF.Identity, scale=sc),
                          [b_ps[bset + bi]], [b_o2[s]])
                if mt < KC:
                    kb.dma(qT[mt * 128:(mt + 1) * 128, :], o[s][:], b_o2[s], b_q, nowaw=True)
                else:
                    r = (mt - KC) * 128
                    kb.dma(kv_own[r:r + 128, :], o[s][:], b_o2[s], b_kvo, nowaw=True)

            linear_phase(("qkv", l), D, list(range(nq)), xsT, b_xs, (2, T),
                         [(0, 512, 0), (512, 512, 1)], epi, alloc)

        def resid_phase(wkey, K, src, b_srcact, h_in, b_hin, h_out, b_hout, dbg_out=None):
            ntt = 1 if K > D else 2
            passes = [(0, T)] if K == D else [(0, 512), (512, 512)]
            for (p0, pn) in passes:
                def alloc(ph):
                    r = [ph.enter_context(_sbuf_tensor(f"r_r{i}", [128, pn], F32)) for i in range(2)]
                    o = [ph.enter_context(_sbuf_tensor(f"r_o{i}", [128, pn], F32)) for i in range(2)]
                    return (r, o, [B("rr0"), B("rr1")], [B("ro0"), B("ro1")])

                tiles = [(i * 512, 512, i) for i in range(pn // 512)]

                def epi(mi, mt, bset, ctx):
                    r, o, b_r, b_o2 = ctx
                    s = mi % 2
                    kb.dma(r[s][:], h_in[mt * 128:(mt + 1) * 128, 2 + p0:2 + p0 + pn], b_hin, b_r[s])
                    for (t0, n, bi) in tiles:
                        kb.op(kb.dve, lambda e: e.tensor_tensor(out=o[s][:, t0:t0 + n], in0=ps[bset + bi][:, 0:n],
                                                                in1=r[s][:, t0:t0 + n], op=ALU.add),
                              [b_ps[bset + bi], b_r[s]], [b_o2[s]])
                    kb.dma(h_out[mt * 128:(mt + 1) * 128, 2 + p0:2 + p0 + pn], o[s][:], b_o2[s], b_hout, nowaw=True)
                    if dbg_out is not None:
                        kb.dma(dbg_out[mt * 128:(mt + 1) * 128, p0:p0 + pn], o[s][:], b_o2[s], b_dbg, nowaw=True)

                linear_phase(wkey, K, list(range(KC)), src, b_srcact, (p0, pn), tiles, epi, alloc, kseg=28 if K > D else 32)

        def ffn_in_phase(l):
            mts = []
            for i in range(MTF):
                mts += [i, MTF + i]

            def alloc(ph):
                hs = [ph.enter_context(_sbuf_tensor(f"f_hs{i}", [128, TT], F32)) for i in range(2)]
                yg = ph.enter_context(_sbuf_tensor("f_yg", [128, T], F32))
                yu = ph.enter_context(_sbuf_tensor("f_yu", [128, T], F32))
                sg = ph.enter_context(_sbuf_tensor("f_sg", [128, T], F32))
                go = [ph.enter_context(_sbuf_tensor(f"f_go{i}", [128, T], BF16)) for i in range(2)]
                return dict(hs=hs, yg=yg, yu=yu, sg=sg, go=go, b_hs=[B("hs0"), B("hs1")], b_yg=B("yg"),
                            b_yu=B("yu"), b_sg=B("sg"), b_go=[B("go0"), B("go1")])

            def epi(mi, mt, bset, c):
                s = mi % 2
                hs, b_hs = c["hs"][s], c["b_hs"][s]
                isg = (mi % 2 == 0)
                y, b_yy = (c["yg"], c["b_yg"]) if isg else (c["yu"], c["b_yu"])
                kb.op(kb.dve, lambda e: e.tensor_copy(out=hs[:, 0:2], in_=ps[bset + 0][:, 0:2]),
                      [b_ps[bset + 0]], [b_hs])
                kb.op(kb.act, lambda e: e.activation(out=hs[:, 2:514], in_=ps[bset + 1][:, 0:512], func=AF.Identity),
                      [b_ps[bset + 1]], [b_hs])
                kb.op(kb.dve, lambda e: e.tensor_copy(out=hs[:, 514:TT], in_=ps[bset + 2][:, 0:512]),
                      [b_ps[bset + 2]], [b_hs])
                w0, w1, w2 = (cw_s[:, l, j, mt:mt + 1] for j in range(3))
                kb.op(kb.act, lambda e: e.activation(out=y[:], in_=hs[:, 2:TT], func=AF.Identity,
                                                     bias=cb_s[:, l, mt:mt + 1], scale=w2),
                      [b_hs, b_cst], [b_yy])
                kb.op(kb.dve, lambda e: e.scalar_tensor_tensor(out=y[:], in0=hs[:, 1:TT - 1], scalar=w1, in1=y[:],
                                                               op0=ALU.mult, op1=ALU.add),
                      [b_hs, b_yy, b_cst], [b_yy])
                kb.op(kb.dve, lambda e: e.scalar_tensor_tensor(out=y[:], in0=hs[:, 0:T], scalar=w0, in1=y[:],
                                                               op0=ALU.mult, op1=ALU.add),
                      [b_hs, b_yy, b_cst], [b_yy])
                if isg:
                    kb.op(kb.act, lambda e: e.activation(out=c["sg"][:], in_=y[:], func=AF.Silu),
                          [b_yy], [c["b_sg"]])
                else:
                    i = mt - MTF
                    gs = (mi // 2) % 2
                    kb.op(kb.dve, lambda e: e.tensor_tensor(out=c["go"][gs][:], in0=c["sg"][:], in1=y[:],
                                                            op=ALU.mult),
                          [c["b_sg"], b_yy], [c["b_go"][gs]])
                    kb.dma(gT[i * 128:(i + 1) * 128, :], c["go"][gs][:], c["b_go"][gs], b_g, nowaw=True)

            linear_phase(("ing", l), D, mts, xsT, b_xs, (0, TT),
                         [(0, 2, 0), (2, 512, 1), (514, 512, 2)], epi, alloc,
                         wsel=lambda mt: (("ing", l), mt) if mt < MTF else (("inu", l), mt - MTF))

        def sb_attn_phase(pre=None):
            NKB, OFF = cfg.NKB, cfg.OFF
            PR = 512
            for p in range(2 * D // PR):
                kb.coll(kv_all[p * NG * PR:(p + 1) * NG * PR, :], kv_own[p * PR:(p + 1) * PR, :], b_kvo, b_kva,
                        groups, nowaw=(p > 0))
            if pre is not None:
                pre()
            with contextlib.ExitStack() as ph:
                A = lambda n, s, d: ph.enter_context(_sbuf_tensor(n, list(s), d))
                mb = A("a_mb", [128, cfg.MW], BF16)
                b_mb = B("mb")
                kb.dma(mb[:], mbig_d.ap(), b_cin, b_mb)
                qh = [A(f"a_q{i}", [128, T], BF16) for i in range(2)]
                kh = [A(f"a_k{i}", [128, NG, T], BF16) for i in range(2)]
                vth = [A(f"a_vt{i}", [128, NG, T], BF16) for i in range(2)]
                vh = [A(f"a_v{i}", [128, NKB, 128], BF16) for i in range(2)]
                ee = [A(f"a_e{i}", [128, 512], F32) for i in range(4)]
                spb = [A(f"a_sp{i}", [128, 512], BF16) for i in range(4)]
                wb = [A(f"a_w{i}", [128, 512], BF16) for i in range(4)]
                s32 = [A(f"a_s32{q}", [128, 512], F32) for q in range(T // 512)]
                sbf = [[A(f"a_sbf{q}{i}", [128, 512], BF16) for i in range(2)] for q in range(T // 512)]
                ob = [A(f"a_o{i}", [128, 512], BF16) for i in range(T // 512)]
                b_qh, b_kh, b_vth, b_vh = ([B("qh0"), B("qh1")], [B("kh0"), B("kh1")], [B("vt0"), B("vt1")],
                                           [B("vh0"), B("vh1")])
                b_ee, b_spb, b_wb = [B(f"e{i}") for i in range(4)], [B(f"sp{i}") for i in range(4)], [B(f"w{i}") for i in range(4)]
                b_s32 = [B(f"s32{q}") for q in range(T // 512)]
                b_sbf = [[B(f"sbf{q}{i}") for i in range(2)] for q in range(T // 512)]
                b_ob = [B(f"ob{q}") for q in range(T // 512)]
                kva5 = kv_all.ap().rearrange("(p r i) t -> p i r t", r=NG, i=PR)

                def kvrows(f0):
                    return kva5[f0 // PR, (f0 % PR):(f0 % PR) + 128, :, :]
                NQ = T // 512
                ZB = [0, 1, 2, 7]
                zi = 0
                si = [0] * NQ
                for h in range(cfg.SBH):
                    s = h % 2
                    kb.dma(qh[s][:], qT[h * 128:(h + 1) * 128, :], b_q, b_qh[s])
                    kb.dma(kh[s][:], kvrows(h * 128), b_kva, b_kh[s])
                    kb.dma(vth[s][:], kvrows(D + h * 128), b_kva, b_vth[s])
                    for g4 in range(NKB // 4):
                        bi = 5 + (g4 % 2)
                        for j in range(4):
                            kbk = g4 * 4 + j
                            r, c = kbk // (T // 128), (kbk % (T // 128)) * 128
                            kb.op(kb.pe, lambda e: e.matmul(ps[bi][:, j * 128:(j + 1) * 128],
                                                            lhsT=vth[s][:, r, c:c + 128], rhs=ident,
                                                            start=True, stop=True),
                                  [b_vth[s], b_cst], [b_ps[bi]], inc=(j == 3))
                        kb.op(kb.dve, lambda e: e.tensor_copy(
                            out=vh[s][:, g4 * 4:(g4 + 1) * 4, :],
                            in_=ps[bi][:, :].rearrange("p (j d) -> p j d", d=128)),
                              [b_ps[bi]], [b_vh[s]])
                    for n_i, kbk in enumerate(range(NKB - 1, -1, -1)):
                        for qb in range(NQ):
                            ob_i = 3 + qb
                            zs = zi % 4
                            zb = ZB[zs]
                            zi += 1
                            r, c = kbk // (T // 128), (kbk % (T // 128)) * 128
                            v0 = 512 * qb - 128 * kbk + OFF
                            kb.op(kb.pe, lambda e: e.matmul(ps[zb][:, :], lhsT=kh[s][:, r, c:c + 128],
                                                            rhs=qh[s][:, qb * 512:(qb + 1) * 512],
                                                            start=True, stop=False),
                                  [b_kh[s], b_qh[s]], [b_ps[zb]], inc=False)
                            kb.op(kb.pe, lambda e: e.matmul(ps[zb][:, :], lhsT=ident, rhs=mb[:, v0:v0 + 512],
                                                            start=False, stop=True),
                                  [b_mb, b_cst], [b_ps[zb]])
                            kb.op(kb.act, lambda e: e.activation(out=ee[zs][:], in_=ps[zb][:, :], func=AF.Exp),
                                  [b_ps[zb]], [b_ee[zs]])
                            kb.op(kb.act, lambda e: e.activation(out=spb[zs][:], in_=ee[zs][:], func=AF.Ln,
                                                                 bias=1.0, scale=1.0),
                                  [b_ee[zs]], [b_spb[zs]])
                            lastmm = (n_i == 0)
                            kb.op(kb.pe, lambda e: e.matmul(ps[zb][:, :], lhsT=negtri, rhs=spb[zs][:],
                                                            start=False, stop=lastmm),
                                  [b_spb[zs], b_cst], [b_ps[zb]], inc=lastmm)
                            if n_i > 0:
                                sl = si[qb] % 2
                                kb.op(kb.pe, lambda e: e.matmul(ps[zb][:, :], lhsT=negones, rhs=sbf[qb][sl][:],
                                                                start=False, stop=True),
                                      [b_sbf[qb][sl], b_cst], [b_ps[zb]])
                            kb.op(kb.act, lambda e: e.activation(out=wb[zs][:], in_=ps[zb][:, :], func=AF.Exp),
                                  [b_ps[zb]], [b_wb[zs]])
                            if n_i == 0:
                                kb.op(kb.dve, lambda e: e.tensor_copy(out=s32[qb][:], in_=spb[zs][:]),
                                      [b_spb[zs]], [b_s32[qb]])
                            else:
                                kb.op(kb.dve, lambda e: e.tensor_tensor(out=s32[qb][:], in0=s32[qb][:],
                                                                         in1=spb[zs][:], op=ALU.add),
                                      [b_spb[zs], b_s32[qb]], [b_s32[qb]])
                            if n_i < NKB - 1:
                                si[qb] += 1
                                sl = si[qb] % 2
                                kb.op(kb.dve, lambda e: e.tensor_copy(out=sbf[qb][sl][:], in_=s32[qb][:]),
                                      [b_s32[qb]], [b_sbf[qb][sl]])
                            kb.op(kb.pe, lambda e: e.matmul(ps[ob_i][:, :], lhsT=vh[s][:, kbk, :], rhs=wb[zs][:],
                                                            start=(n_i == 0), stop=(n_i == NKB - 1)),
                                  [b_vh[s], b_wb[zs]], [b_ps[ob_i]], inc=(n_i == NKB - 1))
                    for qb in range(NQ):
                        ob_i = 3 + qb
                        kb.op(kb.dve, lambda e: e.tensor_copy(out=ob[qb][:], in_=ps[ob_i][:, :]),
                              [b_ps[ob_i]], [b_ob[qb]])
                        kb.dma(oT[h * 128:(h + 1) * 128, qb * 512:(qb + 1) * 512], ob[qb][:], b_ob[qb], b_o,
                               nowaw=True)
                kb.barrier()

        def swa_attn_phase(jl):
            DKV, KVH, G, SWH = cfg.DKV, cfg.KVH, cfg.G, cfg.SWH
            NQB = T // 128
            NCH = 2 * DKV // 128
            with contextlib.ExitStack() as ph:
                A = lambda n, s, d: ph.enter_context(_sbuf_tensor(n, list(s), d))
                kb.dma(kvh_own.ap(), kv_own[0:2 * DKV, T - 128:T], b_kvo, b_kvho)
                kb.coll(kvh_all.ap().opt(), kvh_own.ap().opt(), b_kvho, b_kvha, groups)
                ha = A("s_ha", [128, NG, NCH, 128], BF16)
                hsel = A("s_hsel", [128, NCH, 128], BF16)
                b_ha, b_hsel = B("ha"), B("hsel")
                kb.dma(ha[:], kvh_all.ap().rearrange("(r c p) t -> p r c t", r=NG, p=128), b_kvha, b_ha)
                for r in range(NG):
                    if r == 0:
                        kb.op(kb.dve, lambda e: e.tensor_scalar(out=hsel[:], in0=ha[:, r], scalar1=sel_s[:, r:r + 1],
                                                                scalar2=None, op0=ALU.mult),
                              [b_ha, b_cst], [b_hsel])
                    else:
                        kb.op(kb.dve, lambda e: e.scalar_tensor_tensor(out=hsel[:], in0=ha[:, r],
                                                                       scalar=sel_s[:, r:r + 1], in1=hsel[:],
                                                                       op0=ALU.mult, op1=ALU.add),
                              [b_ha, b_hsel, b_cst], [b_hsel])
                kb.dma(kvh_sel.ap().rearrange("(c p) t -> p c t", p=128), hsel[:], b_hsel, b_kvhs)
                bt = A("s_bt", [128, SWH, 256], BF16)
                hm = A("s_hm", [128, 128], BF16)
                sk = A("s_sk", [128, SWH], F32)
                esk = A("s_esk", [128, SWH], F32)
                b_bt, b_sk, b_esk = B("bt"), B("sk"), B("esk")
                kb.dma(bt[:], bt_d.ap(), b_cin, b_bt)
                kb.dma(hm[:], hm_d.ap(), b_cin, b_bt, nowaw=True)
                kb.dma(sk[:], sink_d[:, jl, :], b_cin, b_sk)
                kb.op(kb.act, lambda e: e.activation(out=esk[:], in_=sk[:], func=AF.Exp), [b_sk], [b_esk])
                k2 = [A(f"s_k2{i}", [128, 128 + T], BF16) for i in range(2)]
                vt = [A(f"s_vt{i}", [64, 128 + T], BF16) for i in range(2)]
                vg = [A(f"s_vg{i}", [128, NQB + 1, 64], BF16) for i in range(2)]
                qs = [A(f"s_q{i}", [128, T], BF16) for i in range(2)]
                pc = [A(f"s_pc{i}", [128, T], BF16) for i in range(2)]
                pp = [A(f"s_pp{i}", [128, T], BF16) for i in range(2)]
                dn = A("s_dn", [64, T], F32)
                rd = A("s_rd", [64, T], F32)
                oo = [A(f"s_oo{i}", [64, T], BF16) for i in range(2)]
                b_k2, b_vt, b_vg, b_qs = ([B("k20"), B("k21")], [B("vt0"), B("vt1")], [B("vg0"), B("vg1")],
                                          [B("qs0"), B("qs1")])
                b_pc, b_pp, b_dn, b_rd, b_oo = ([B("pc0"), B("pc1")], [B("pp0"), B("pp1")], B("dn"), B("rd"),
                                                [B("oo0"), B("oo1")])
                hi = 0
                for g in range(KVH):
                    s = g % 2
                    for half in range(2):
                        kb.dma(k2[s][half * 64:(half + 1) * 64, 0:128], kvh_sel[g * 64:(g + 1) * 64, :], b_kvhs,
                               b_k2[s], nowaw=(half == 1))
                        kb.dma(k2[s][half * 64:(half + 1) * 64, 128:128 + T], kv_own[g * 64:(g + 1) * 64, :],
                               b_kvo, b_k2[s], nowaw=True)
                    kb.dma(vt[s][:, 0:128], kvh_sel[DKV + g * 64:DKV + (g + 1) * 64, :], b_kvhs, b_vt[s])
                    kb.dma(vt[s][:, 128:128 + T], kv_own[DKV + g * 64:DKV + (g + 1) * 64, :], b_kvo, b_vt[s],
                           nowaw=True)
                    nblk = NQB + 1
                    for b0 in range(0, nblk, 8):
                        bi = 0 + ((b0 // 8) % 2)
                        nb_ = min(8, nblk - b0)
                        for j in range(nb_):
                            blk = b0 + j
                            kb.op(kb.pe, lambda e: e.matmul(ps[bi][:, j * 64:(j + 1) * 64],
                                                            lhsT=vt[s][:, blk * 128:(blk + 1) * 128],
                                                            rhs=cst[0:64, 0, 0:64], start=True, stop=True),
                                  [b_vt[s], b_cst], [b_ps[bi]], inc=(j == nb_ - 1))
                        kb.op(kb.dve, lambda e: e.tensor_copy(
                            out=vg[s][:, b0:b0 + nb_, :],
                            in_=ps[bi][:, 0:nb_ * 64].rearrange("p (j d) -> p j d", d=64)),
                              [b_ps[bi]], [b_vg[s]])
                    for gi in range(G):
                        h = g * G + gi
                        par = h % 2
                        qsl = (h // 2) % 2
                        if par == 0:
                            kb.dma(qs[qsl][:], qT[(h // 2) * 128:(h // 2 + 1) * 128, :], b_q, b_qs[qsl])
                        hs_ = hi % 2
                        hi += 1
                        P0, P1 = par * 64, (par + 1) * 64
                        for i in range(NQB):
                            bi = 0 + i // 4
                            reg = ps[bi][:, (i % 4) * 128:(i % 4 + 1) * 128]
                            kb.op(kb.pe, lambda e: e.matmul(reg, lhsT=k2[s][P0:P1, 128 + i * 128:256 + i * 128],
                                                            rhs=qs[qsl][P0:P1, i * 128:(i + 1) * 128],
                                                            start=True, stop=False),
                                  [b_k2[s], b_qs[qsl]], [b_ps[bi]], inc=False)
                            kb.op(kb.pe, lambda e: e.matmul(reg, lhsT=ident, rhs=bt[:, h, 0:128],
                                                            start=False, stop=True),
                                  [b_bt, b_cst], [b_ps[bi]], inc=(i % 4 == 3))
                        for i in range(NQB):
                            bi = 2 + i // 4
                            reg = ps[bi][:, (i % 4) * 128:(i % 4 + 1) * 128]
                            kb.op(kb.pe, lambda e: e.matmul(reg, lhsT=k2[s][P0:P1, i * 128:(i + 1) * 128],
                                                            rhs=qs[qsl][P0:P1, i * 128:(i + 1) * 128],
                                                            start=True, stop=False),
                                  [b_k2[s], b_qs[qsl]], [b_ps[bi]], inc=False)
                            if i == 0:
                                kb.op(kb.pe, lambda e: e.matmul(reg, lhsT=ident, rhs=hm[:, :], start=False,
                                                                stop=False),
                                      [b_bt, b_cst], [b_ps[bi]], inc=False)
                            kb.op(kb.pe, lambda e: e.matmul(reg, lhsT=ident, rhs=bt[:, h, 128:256],
                                                            start=False, stop=True),
                                  [b_bt, b_cst], [b_ps[bi]], inc=(i % 4 == 3))
                        for half in range(2):
                            kb.op(kb.act, lambda e: e.activation(out=pc[hs_][:, half * 512:(half + 1) * 512],
                                                                 in_=ps[0 + half][:, :], func=AF.Exp),
                                  [b_ps[0 + half]], [b_pc[hs_]])
                            kb.op(kb.act, lambda e: e.activation(out=pp[hs_][:, half * 512:(half + 1) * 512],
                                                                 in_=ps[2 + half][:, :], func=AF.Exp),
                                  [b_ps[2 + half]], [b_pp[hs_]])
                        for i in range(NQB):
                            bi = 4 + i // 4
                            reg = ps[bi][0:64, (i % 4) * 128:(i % 4 + 1) * 128]
                            kb.op(kb.pe, lambda e: e.matmul(reg, lhsT=vg[s][:, i + 1, :],
                                                            rhs=pc[hs_][:, i * 128:(i + 1) * 128],
                                                            start=True, stop=False),
                                  [b_vg[s], b_pc[hs_]], [b_ps[bi]], inc=False)
                            kb.op(kb.pe, lambda e: e.matmul(reg, lhsT=vg[s][:, i, :],
                                                            rhs=pp[hs_][:, i * 128:(i + 1) * 128],
                                                            start=False, stop=True),
                                  [b_vg[s], b_pp[hs_]], [b_ps[bi]], inc=(i % 4 == 3))
                        for half in range(2):
                            bi = 6 + half
                            kb.op(kb.pe, lambda e: e.matmul(ps[bi][0:64, :], lhsT=cst[:, 2, 0:64],
                                                            rhs=pc[hs_][:, half * 512:(half + 1) * 512],
                                                            start=True, stop=False),
                                  [b_pc[hs_], b_cst], [b_ps[bi]], inc=False)
                            kb.op(kb.pe, lambda e: e.matmul(ps[bi][0:64, :], lhsT=cst[:, 2, 0:64],
                                                            rhs=pp[hs_][:, half * 512:(half + 1) * 512],
                                                            start=False, stop=True),
                                  [b_pp[hs_], b_cst], [b_ps[bi]])
                            kb.op(kb.dve, lambda e: e.tensor_scalar(out=dn[:, half * 512:(half + 1) * 512],
                                                                    in0=ps[bi][0:64, :], scalar1=esk[0:64, h:h + 1],
                                                                    scalar2=None, op0=ALU.add),
                                  [b_ps[bi], b_esk], [b_dn])
                        kb.op(kb.dve, lambda e: e.reciprocal(out=rd[:], in_=dn[:]), [b_dn], [b_rd])
                        for half in range(2):
                            kb.op(kb.dve, lambda e: e.tensor_tensor(out=oo[hs_][:, half * 512:(half + 1) * 512],
                                                                    in0=ps[4 + half][0:64, :],
                                                                    in1=rd[:, half * 512:(half + 1) * 512],
                                                                    op=ALU.mult),
                                  [b_ps[4 + half], b_rd], [b_oo[hs_]])
                        kb.dma(oT[h * 64:(h + 1) * 64, :], oo[hs_][:], b_oo[hs_], b_o, nowaw=True)
                kb.barrier()

        def halo_phase(h_t, b_h):
            with contextlib.ExitStack() as ph:
                A = lambda n, s, d: ph.enter_context(_sbuf_tensor(n, list(s), d))
                kb.dma(hal_own[:, 0:2 * KC].rearrange("p (k c) -> p k c", c=2),
                       h_t[:, TT - 2:TT].rearrange("(k p) c -> p k c", p=128), b_h, b_halo)
                kb.coll(hal_all.ap().opt(), hal_own.ap().opt(), b_halo, b_hala, groups)
                ha = A("h_ha", [128, NG, KC, 2], F32)
                hs_ = A("h_hs", [128, KC, 2], F32)
                b_ha, b_hs2 = B("hha"), B("hhs")
                kb.dma(ha[:], hal_all[:, 0:2 * KC].rearrange("(r p) (k c) -> p r k c", r=NG, c=2), b_hala, b_ha)
                for r in range(NG):
                    if r == 0:
                        kb.op(kb.dve, lambda e: e.tensor_scalar(out=hs_[:], in0=ha[:, r], scalar1=sel_s[:, r:r + 1],
                                                                scalar2=None, op0=ALU.mult),
                              [b_ha, b_cst], [b_hs2])
                    else:
                        kb.op(kb.dve, lambda e: e.scalar_tensor_tensor(out=hs_[:], in0=ha[:, r],
                                                                       scalar=sel_s[:, r:r + 1], in1=hs_[:],
                                                                       op0=ALU.mult, op1=ALU.add),
                              [b_ha, b_hs2, b_cst], [b_hs2])
                kb.dma(h_t[:, 0:2].rearrange("(k p) c -> p k c", p=128), hs_[:], b_hs2, b_h)
                kb.barrier()

        _pc = [0]

        def _lim(fn):
            def w(*a, **k):
                _pc[0] += 1
                if cfg.stop is not None and _pc[0] > cfg.stop:
                    return
                print("phase", _pc[0], fn.__name__, flush=True) if cfg.debug else None
                return fn(*a, **k)
            return w
        norm_phase, qkv_phase, sb_attn_phase, swa_attn_phase, resid_phase, halo_phase, ffn_in_phase = map(
            _lim, (norm_phase, qkv_phase, sb_attn_phase, swa_attn_phase, resid_phase, halo_phase, ffn_in_phase))
        for l in range(depth):
            norm_phase(hA, b_hA, lambda kc, l=l: gA_s[:, l, kc:kc + 1], "xs")
            qkv_phase(l)
            nxt = (l + 1 < depth) and (cfg.stop is None)
            if l == 0:
                sb_attn_phase(pre=lambda: emit_gather(0, ("o", "ing", "inu", "down")))
            elif l % 2 == 0:
                sb_attn_phase(pre=(lambda l=l: emit_gather(l + 1, ("qkv", "o", "ing"))) if nxt else None)
            else:
                swa_attn_phase(l // 2)
            resid_phase(("o", l), D, oT, b_o, hA, b_hA, hB, b_hB, dbg.get(("mid", l)))
            halo_phase(hB, b_hB)
            if nxt:
                emit_gather(l + 1, ("inu", "down") if (l % 2 == 0 and l > 0) else ("qkv", "o", "ing", "inu", "down"))
            norm_phase(hB, b_hB, lambda kc, l=l: gF_s[:, l, kc:kc + 1], "xs")
            ffn_in_phase(l)
            resid_phase(("down", l), DFF, gT, b_g, hB, b_hB, hA, b_hA, dbg.get(("h", l)))
        norm_phase(hA, b_hA, lambda kc: gO_s[:, kc:kc + 1], "y")
        kb.barrier()
    return nc


def host_tables(cfg, core):
    T, NG = cfg.T, cfg.NG
    j = core % NG
    q0 = j * T
    p = np.arange(128)[:, None]
    v = np.arange(cfg.MW)[None, :]
    mbig = np.where(p < q0 + v - cfg.OFF, 0.0, NEG).astype(ml_dtypes.bfloat16)
    sel = np.zeros((128, NG), np.float32)
    if j > 0:
        sel[:, j - 1] = 1.0
    slopes = (2.0 ** (-8.0 * np.arange(1, cfg.SWH + 1) / cfg.SWH)).astype(np.float32)
    k = np.arange(128)[:, None]
    q = np.arange(128)[None, :]
    bt = np.zeros((128, cfg.SWH, 256), np.float32)
    for h in range(cfg.SWH):
        bt[:, h, 0:128] = np.where(k <= q, -slopes[h] * (q - k), NEG)
        bt[:, h, 128:256] = np.where(k > q, -slopes[h] * (128 + q - k), NEG)
    hm = np.full((128, 128), 0.0 if j > 0 else NEG, np.float32)
    cst = np.zeros((128, 4, 128), np.float32)
    cst[:, 0, :] = np.eye(128)
    cst[:, 1, :] = -(k >= q).astype(np.float32)
    cst[:, 2, :] = 1.0
    cst[:, 3, :] = -1.0
    return dict(mbig=mbig, sel=sel, bt=bt.astype(ml_dtypes.bfloat16), hm=hm.astype(ml_dtypes.bfloat16),
                cst=cst.astype(ml_dtypes.bfloat16))


def make_in_maps(cfg, inp):
    D, T, NG, NC, depth, KC, MTF = cfg.D, cfg.T, cfg.NG, cfg.NC, cfg.depth, cfg.KC, cfg.MTF
    S = NG * T

    def fm(a):
        dd, F = a.shape
        return np.ascontiguousarray(a.reshape(dd, F // 128, 128).transpose(2, 0, 1)).astype(np.float32)

    common = {}
    common["gA"] = fm(inp["attn_norm"])
    common["gF"] = fm(inp["ffn_norm"])
    common["gO"] = np.ascontiguousarray(inp["final_norm"].reshape(KC, 128).T).astype(np.float32)
    cw = inp["ffn_conv_w"]
    common["cw"] = np.ascontiguousarray(cw.reshape(depth, 3, 2 * MTF, 128).transpose(3, 0, 1, 2)).astype(np.float32)
    common["cb"] = fm(inp["ffn_conv_b"])
    common["sink"] = np.ascontiguousarray(np.broadcast_to(inp["swa_sinks"][None], (128,) + inp["swa_sinks"].shape)).astype(np.float32)
    wl = {}
    for l in range(depth):
        j = l // 2
        if l % 2 == 0:
            wl[("qkv", l)] = inp["sb_w_qkv"][j]
            wl[("o", l)] = inp["sb_w_o"][j]
        else:
            wl[("qkv", l)] = inp["swa_w_qkv"][j]
            wl[("o", l)] = inp["swa_w_o"][j]
        wl[("ing", l)] = inp["ffn_w_in"][l][:, :cfg.DFF]
        wl[("inu", l)] = inp["ffn_w_in"][l][:, cfg.DFF:]
        wl[("down", l)] = inp["ffn_w_down"][l]
    wblk = {}
    for key, w in wl.items():
        K, N = w.shape
        wblk[key] = np.ascontiguousarray(w.reshape(K // 128, 128, N // 128, 128).transpose(2, 1, 0, 3)).reshape(N, K)
    maps = []
    for c in range(NC):
        b, j = c // NG, c % NG
        m = dict(common)
        m["xT"] = np.ascontiguousarray(inp["x"][b, j * T:(j + 1) * T, :].T)
        for (nm, l), w in wl.items():
            K = w.shape[0]
            wb = wblk[(nm, l)]
            Nw, Kw = wb.shape
            nr = cfg.piece_rows(Nw, Kw)
            P = (Nw // NG) // nr
            m[f"w_{nm}{l}"] = np.ascontiguousarray(wb.reshape(P, NG, nr, Kw)[:, j].reshape(P * nr, Kw))
        m.update(host_tables(cfg, c))
        maps.append(m)
    return maps


_CACHE = {}


def run(cfg, inp):
    key = (cfg.D, cfg.T, cfg.G, cfg.depth, cfg.debug)
    if key not in _CACHE:
        _CACHE[key] = build_program(cfg)
    nc = _CACHE[key]
    maps = make_in_maps(cfg, inp)
    res = run_bass_kernel_spmd(nc, maps, core_ids=list(range(cfg.NC)))
    return res


def kernel(x, attn_norm, ffn_norm, sb_w_qkv, sb_w_o, swa_w_qkv, swa_w_o, swa_sinks,
           ffn_w_in, ffn_conv_w, ffn_conv_b, ffn_w_down, final_norm):
    cfg = Cfg()
    inp = dict(x=np.asarray(x), attn_norm=np.asarray(attn_norm), ffn_norm=np.asarray(ffn_norm),
               sb_w_qkv=np.asarray(sb_w_qkv), sb_w_o=np.asarray(sb_w_o), swa_w_qkv=np.asarray(swa_w_qkv),
               swa_w_o=np.asarray(swa_w_o), swa_sinks=np.asarray(swa_sinks), ffn_w_in=np.asarray(ffn_w_in),
               ffn_conv_w=np.asarray(ffn_conv_w), ffn_conv_b=np.asarray(ffn_conv_b),
               ffn_w_down=np.asarray(ffn_w_down), final_norm=np.asarray(final_norm))
    res = run(cfg, inp)
    out = np.empty((cfg.NB, cfg.NG * cfg.T, cfg.D), np.float32)
    for c in range(cfg.NC):
        b, j = c // cfg.NG, c % cfg.NG
        out[b, j * cfg.T:(j + 1) * cfg.T, :] = res.results[c]["yT"].T
    return out
```

```python
import contextlib
import os
import math
import numpy as np
import ml_dtypes
import concourse.bass as bass
import concourse.mybir as mybir
from concourse.bass_utils import run_bass_kernel_spmd

F32 = mybir.dt.float32
BF16 = mybir.dt.bfloat16
AF = mybir.ActivationFunctionType
ALU = mybir.AluOpType
NEG = -30000.0


class Cfg:
    def __init__(self, D=4096, T=1024, NG=4, NB=2, G=8, depth=4, debug=False):
        self.D, self.T, self.NG, self.NB, self.G, self.depth = D, T, NG, NB, G, depth
        self.NC = NG * NB
        self.KC = D // 128
        self.SBH = D // 128
        self.SWH = D // 64
        self.KVH = self.SWH // G
        self.DKV = self.KVH * 64
        self.DFF = 7 * D // 2
        self.MTF = self.DFF // 128
        self.TT = T + 2
        self.NKB = NG * T // 128
        self.OFF = 128 * (self.NKB - 1)
        self.MW = self.OFF + 2 * 512
        self.debug = debug
        self.stop = None

    def piece_rows(self, K, N, esize=2):
        q = K // self.NG
        mx = max(1, (1 << 20) // (esize * N))
        nr = 1
        for d in range(1, q + 1):
            if q % d == 0 and d <= mx:
                nr = d
        return nr

    def nqkv(self, l):
        return 3 * self.D if l % 2 == 0 else self.D + 2 * self.DKV


class Sem:
    def __init__(self, h):
        self.h = h
        self.cnt = 0


class Buf:
    def __init__(self, name, ap=None):
        self.name = name
        self.ap = ap
        self.w = None
        self.r = {}
        self.dsem = None


class Eng:
    def __init__(self, name, e, sem):
        self.name, self.e, self.sem = name, e, sem
        self.seen = {}


class KB:
    def __init__(self, nc, stack):
        self.nc = nc
        self.stack = stack
        self.sems = []
        mk = lambda n, e: Eng(n, e, self.new_sem("s_" + n))
        self.pe = mk("pe", nc.tensor)
        self.act = mk("act", nc.scalar)
        self.dve = mk("dve", nc.vector)
        self.pool = mk("pool", nc.gpsimd)
        self.sp = mk("sp", nc.sync)
        self.engs = [self.pe, self.act, self.dve, self.pool, self.sp]
        self.local_bufs = []
        self.free_sems = []
        self.nobar_sems = set()

    def new_sem(self, name):
        s = Sem(self.stack.enter_context(self.nc.semaphore(f"{name}_{len(self.sems)}")))
        self.sems.append(s)
        return s

    def buf(self, name, persistent=False):
        b = Buf(name)
        if not persistent:
            self.local_bufs.append(b)
        return b

    def get_dsem(self, b, prefix):
        if b.dsem is None:
            if b in self.local_bufs and self.free_sems:
                b.dsem = self.free_sems.pop()
            else:
                b.dsem = self.new_sem(prefix + b.name)
                if getattr(b, "nobar", False):
                    self.nobar_sems.add(b.dsem)
        return b.dsem

    def wait(self, E, sem, cnt):
        if cnt <= 0 or E.seen.get(sem, 0) >= cnt:
            return
        if sem is E.sem and E is self.pe:
            return
        E.e.wait_ge(sem.h, cnt)
        E.seen[sem] = cnt

    def deps(self, E, reads, writes, nowaw=False):
        for b in reads:
            if b.w:
                self.wait(E, *b.w)
        for b in writes:
            if b.w and not nowaw:
                self.wait(E, *b.w)
            for sem, c in list(b.r.items()):
                self.wait(E, sem, c)

    def op(self, E, fn, reads, writes, inc=True):
        self.deps(E, reads, writes)
        ins = fn(E.e)
        if inc:
            E.sem.cnt += 1
            ins.then_inc(E.sem.h, 1)
            c = E.sem.cnt
        else:
            c = E.sem.cnt + 1
        for b in reads:
            b.r[E.sem] = max(b.r.get(E.sem, 0), c)
        for b in writes:
            b.w = (E.sem, c)
            b.r = {}
        return ins

    def dma(self, out_ap, in_ap, src, dst, nowaw=False, Q=None):
        Q = Q or self.sp
        self.deps(Q, [src], [dst], nowaw=nowaw)
        sem = self.get_dsem(dst, "d_")
        ins = Q.e.dma_start(out=out_ap, in_=in_ap)
        sem.cnt += 16
        ins.then_inc(sem.h, 16)
        src.r[sem] = sem.cnt
        dst.w = (sem, sem.cnt)
        if not nowaw:
            dst.r = {}

    def coll(self, out_ap, in_ap, src, dst, groups, nowaw=False):
        Q = self.pool
        self.deps(Q, [src], [dst], nowaw=nowaw)
        sem = self.get_dsem(dst, "c_")
        ins = Q.e.collective_compute("AllGather", ALU.bypass, replica_groups=groups,
                                     ins=[in_ap], outs=[out_ap])
        sem.cnt += 1
        ins.then_inc(sem.h, 1)
        src.r[sem] = sem.cnt
        dst.w = (sem, sem.cnt)
        if not nowaw:
            dst.r = {}

    def barrier(self):
        for E in self.engs:
            for s in self.sems:
                if s is E.sem or s in self.nobar_sems:
                    continue
                self.wait(E, s, s.cnt)
        for b in self.local_bufs:
            if b.dsem is not None:
                self.free_sems.append(b.dsem)
                b.dsem = None
        self.local_bufs = []


def build_program(cfg):
    nc = bass.Bass("TRN2", target_bir_lowering=False)
    _orig_sbuf_tensor = nc.sbuf_tensor
    _uid = [0]

    def _sbuf_tensor(name, shape, dt):
        _uid[0] += 1
        return _orig_sbuf_tensor(f"{name}_u{_uid[0]}", shape, dt)

    D, T, TT, KC, NG, NC = cfg.D, cfg.T, cfg.TT, cfg.KC, cfg.NG, cfg.NC
    DFF, MTF, depth = cfg.DFF, cfg.MTF, cfg.depth
    groups = [list(range(b * NG, (b + 1) * NG)) for b in range(cfg.NB)]
    allg = [list(range(NC))]

    def din(name, shape, dt=F32):
        return nc.dram_tensor(name, list(shape), dt, kind="ExternalInput")

    def dint(name, shape, dt=F32):
        return nc.dram_tensor(name, list(shape), dt)

    xT = din("xT", [D, T])
    wsh = {}
    wdims = {}
    for l in range(depth):
        wdims[("qkv", l)] = (D, cfg.nqkv(l))
        wdims[("o", l)] = (D, D)
        wdims[("ing", l)] = (D, DFF)
        wdims[("inu", l)] = (D, DFF)
        wdims[("down", l)] = (DFF, D)
    for (nm, l), (K, N) in wdims.items():
        wsh[(nm, l)] = din(f"w_{nm}{l}", [N // NG, K])
    gA = din("gA", [128, depth, KC])
    gF = din("gF", [128, depth, KC])
    gO = din("gO", [128, KC])
    cw_d = din("cw", [128, depth, 3, 2 * MTF])
    cb_d = din("cb", [128, depth, 2 * MTF])
    sink_d = din("sink", [128, depth // 2, cfg.SWH])
    mbig_d = din("mbig", [128, cfg.MW], BF16)
    sel_d = din("sel", [128, NG])
    bt_d = din("bt", [128, cfg.SWH, 256], BF16)
    hm_d = din("hm", [128, 128], BF16)
    cst_d = din("cst", [128, 4, 128], BF16)
    yT = nc.dram_tensor("yT", [D, T], F32, kind="ExternalOutput")
    dbg = {}
    if cfg.debug:
        for l in range(depth):
            dbg[("mid", l)] = nc.dram_tensor(f"dbg_mid{l}", [D, T], F32, kind="ExternalOutput")
            dbg[("h", l)] = nc.dram_tensor(f"dbg_h{l}", [D, T], F32, kind="ExternalOutput")

    wbn = {k: dint(f"wb_{k[0]}{k[1]}", [wdims[k][1] // NG, wdims[k][0]], BF16) for k in wdims}
    wfl = {k: dint(f"wf_{k[0]}{k[1]}", [wdims[k][1], wdims[k][0]], BF16) for k in wdims}
    TTP = T + 32
    hA = dint("hA", [D, TTP])
    hB = dint("hB", [D, TTP])
    xsT = dint("xsT", [D, TTP], BF16)
    qT = dint("qT", [D, T], BF16)
    kv_own = dint("kv_own", [2 * D, T], BF16)
    kv_all = dint("kv_all", [NG * 2 * D, T], BF16)
    oT = dint("oT", [D, T], BF16)
    gT = dint("gT", [DFF, T], BF16)
    HW = max(16, 2 * KC)
    hal_own = dint("hal_own", [128, HW])
    hal_all = dint("hal_all", [NG * 128, HW])
    kvh_own = dint("kvh_own", [2 * cfg.DKV, 128], BF16)
    kvh_all = dint("kvh_all", [NG * 2 * cfg.DKV, 128], BF16)
    kvh_sel = dint("kvh_sel", [2 * cfg.DKV, 128], BF16)

    with contextlib.ExitStack() as stack:
        kb = KB(nc, stack)
        PB = lambda n: kb.buf(n, persistent=True)
        B = PB
        b_x = B("xT")
        b_wsh = {k: B("wsh") for k in wdims}
        b_wbn = {k: B(f"wbn{k[0]}{k[1]}") for k in wdims}
        b_wfl = {k: B(f"wfl{k[0]}{k[1]}") for k in wdims}
        for _d in (b_wsh, b_wbn, b_wfl):
            for _b in _d.values():
                _b.nobar = True
        b_hA, b_hB, b_xs, b_q, b_kvo, b_kva, b_o, b_g = (B("hA"), B("hB"), B("xsT"), B("qT"), B("kvo"),
                                                         B("kva"), B("oT"), B("gT"))
        b_halo, b_hala, b_kvho, b_kvha, b_kvhs = B("halo"), B("hala"), B("kvho"), B("kvha"), B("kvhs")
        b_y = B("yT")
        b_dbg = B("dbg")
        b_cin = B("cin")

        B = kb.buf

        def sb(name, shape, dt):
            return stack.enter_context(_sbuf_tensor(name, list(shape), dt))

        cst = sb("cst", [128, 4, 128], BF16)
        gA_s = sb("gA_s", [128, depth, KC], F32)
        gF_s = sb("gF_s", [128, depth, KC], F32)
        gO_s = sb("gO_s", [128, KC], F32)
        cw_s = sb("cw_s", [128, depth, 3, 2 * MTF], F32)
        cb_s = sb("cb_s", [128, depth, 2 * MTF], F32)
        sel_s = sb("sel_s", [128, NG], F32)
        b_cst = PB("cst")
        for t_s, t_d in ((cst, cst_d), (gA_s, gA), (gF_s, gF), (gO_s, gO), (cw_s, cw_d), (cb_s, cb_d),
                         (sel_s, sel_d)):
            kb.dma(t_s[:], t_d.ap(), b_cin, b_cst, nowaw=True)
        ident = cst[:, 0, :]
        negtri = cst[:, 1, :]
        ones_b = cst[:, 2, :]
        negones = cst[:, 3, :]

        ps = [stack.enter_context(nc.psum_tensor(f"ps{i}", [128, 512], F32)) for i in range(8)]
        b_ps = [PB(f"ps{i}") for i in range(8)]

        def emit_gather(l, names=("qkv", "o", "ing", "inu", "down")):
            ks = [(n_, l) for n_ in names]
            for k in ks:
                Kk, Nk = wdims[k]
                rows = Nk // NG
                rstep = max(1, min(rows, (4 << 20) // (4 * Kk)))
                for r0 in range(0, rows, rstep):
                    r1 = min(rows, r0 + rstep)
                    kb.dma(wbn[k][r0:r1, :], wsh[k][r0:r1, :], b_wsh[k], b_wbn[k], nowaw=(r0 > 0), Q=kb.pool)
            for k in ks:
                Kk, Nk = wdims[k]
                nr = cfg.piece_rows(Nk, Kk)
                for p in range((Nk // NG) // nr):
                    kb.coll(wfl[k][p * NG * nr:(p + 1) * NG * nr, :], wbn[k][p * nr:(p + 1) * nr, :],
                            b_wbn[k], b_wfl[k], groups, nowaw=(p > 0))

        emit_gather(0, ("qkv",))
        with contextlib.ExitStack() as ph:
            if os.environ.get("SKIP_INIT"):
                raise_skip = True
            zt = ph.enter_context(_sbuf_tensor("zt", [128, KC, 2], F32))
            b_zt = B("zt")
            kb.op(kb.dve, lambda e: e.memset(zt[:], 0.0), [], [b_zt])
            if not os.environ.get("SKIP_INIT"):
                kb.dma(hA[:, 0:2].rearrange("(k p) c -> p k c", p=128), zt[:], b_zt, b_hA)
                kb.dma(hB[:, 0:2].rearrange("(k p) c -> p k c", p=128), zt[:], b_zt, b_hB, )
            if not os.environ.get("SKIP_X"):
                kb.dma(hA[:, 2:TT], xT.ap(), b_x, b_hA, nowaw=True)
            kb.barrier()

        def norm_phase(h_src, b_src, gain_ap_fn, dst_kind):
            with contextlib.ExitStack() as ph:
                hb = [ph.enter_context(_sbuf_tensor(f"n_hb{i}", [128, TT], F32)) for i in range(2)]
                sq = [ph.enter_context(_sbuf_tensor(f"n_sq{i}", [128, TT], BF16)) for i in range(2)]
                rt = ph.enter_context(_sbuf_tensor("n_rt", [128, TT], F32))
                rb = ph.enter_context(_sbuf_tensor("n_rb", [128, TT], F32))
                odt = BF16 if dst_kind == "xs" else F32
                ob = [ph.enter_context(_sbuf_tensor(f"n_ob{i}", [128, TT], odt)) for i in range(2)]
                b_hb = [B("hb0"), B("hb1")]
                b_sq = [B("sq0"), B("sq1")]
                b_rt, b_rb = B("rt"), B("rb")
                b_ob = [B("ob0"), B("ob1")]
                tiles = [(0, 512, 0), (512, 512, 1), (1024, TT - 1024, 2)]
                import os
                CUT = int(os.environ.get("NORM_CUT", "99"))
                for kc in range(KC):
                    s = kc % 2
                    kb.dma(hb[s][:], h_src[kc * 128:(kc + 1) * 128, 0:TT], b_src, b_hb[s])
                    if CUT < 2:
                        continue
                    kb.op(kb.act, lambda e: e.activation(out=sq[s][:], in_=hb[s][:], func=AF.Square),
                          [b_hb[s]], [b_sq[s]])
                    if CUT < 3:
                        continue
                    for (c0, n, bi) in tiles:
                        kb.op(kb.pe, lambda e: e.matmul(ps[bi][:, 0:n], lhsT=ones_b, rhs=sq[s][:, c0:c0 + n],
                                                        start=(kc == 0), stop=(kc == KC - 1)),
                              [b_sq[s], b_cst], [b_ps[bi]], inc=(kc == KC - 1 or bi == 2))
                if CUT < 4:
                    kb.barrier()
                    return
                for (c0, n, bi) in tiles:
                    kb.op(kb.act, lambda e: e.activation(out=rt[:, c0:c0 + n], in_=ps[bi][:, 0:n], func=AF.Sqrt,
                                                         bias=1e-6, scale=1.0 / D),
                          [b_ps[bi]], [b_rt])
                kb.op(kb.dve, lambda e: e.reciprocal(out=rb[:], in_=rt[:]), [b_rt], [b_rb])
                if CUT < 5:
                    kb.barrier()
                    return
                for kc in range(KC):
                    s = kc % 2
                    kb.dma(hb[s][:], h_src[kc * 128:(kc + 1) * 128, 0:TT], b_src, b_hb[s])
                    kb.op(kb.dve, lambda e: e.scalar_tensor_tensor(out=ob[s][:], in0=hb[s][:],
                                                                   scalar=gain_ap_fn(kc), in1=rb[:],
                                                                   op0=ALU.mult, op1=ALU.mult),
                          [b_hb[s], b_rb, b_cst], [b_ob[s]])
                    if dst_kind == "xs":
                        kb.dma(xsT[kc * 128:(kc + 1) * 128, 0:TT], ob[s][:], b_ob[s], b_xs, nowaw=True)
                    else:
                        kb.dma(yT[kc * 128:(kc + 1) * 128, :], ob[s][:, 2:TT], b_ob[s], b_y, nowaw=True)
                kb.barrier()

        def linear_phase(wkey, K, mtiles, src, b_srcact, src_cols, tok_tiles, epilogue, ph_alloc=None,
                         kseg=32, wsel=None):
            KCl = K // 128
            nseg = (KCl + kseg - 1) // kseg
            assert KCl % nseg == 0
            kseg = KCl // nseg
            c0s, ns = src_cols
            if wsel is None:
                wsel = lambda mt: (wkey, mt)
            with contextlib.ExitStack() as ph:
                insb = ph.enter_context(_sbuf_tensor("l_in", [128, KCl, ns], BF16))
                wbf = [ph.enter_context(_sbuf_tensor(f"l_wbf{i}", [128, kseg * 128], BF16)) for i in range(3)]
                b_in = B("l_in")
                b_wbf = [B("wbf0"), B("wbf1"), B("wbf2")]
                ctx = ph_alloc(ph) if ph_alloc else None
                step = max(1, KCl // 4)
                for k0 in range(0, KCl, step):
                    k1 = min(KCl, k0 + step)
                    kb.dma(insb[:, k0:k1, :],
                           src[k0 * 128:k1 * 128, c0s:c0s + ns].rearrange("(k p) t -> p k t", p=128),
                           b_srcact, b_in, nowaw=True)
                it = 0
                for mi, mt in enumerate(mtiles):
                    bset = (mi % 2) * 4
                    wk, mtl = wsel(mt)
                    W = wfl[wk]
                    for sg in range(nseg):
                        s2, s3 = it % 2, it % 3
                        it += 1
                        c0w = sg * kseg * 128
                        kb.dma(wbf[s3][:], W[mtl * 128:(mtl + 1) * 128, c0w:c0w + kseg * 128],
                               b_wfl[wk], b_wbf[s3])
                        for kc in range(kseg):
                            gk = sg * kseg + kc
                            last = (gk == KCl - 1)
                            for ti, (t0, n, bi) in enumerate(tok_tiles):
                                kb.op(kb.pe, lambda e: e.matmul(ps[bset + bi][:, 0:n],
                                                                lhsT=wbf[s3][:, kc * 128:(kc + 1) * 128],
                                                                rhs=insb[:, gk, t0:t0 + n],
                                                                start=(gk == 0), stop=last),
                                      [b_wbf[s3], b_in], [b_ps[bset + bi]],
                                      inc=(last or (kc == kseg - 1 and ti == len(tok_tiles) - 1)))
                    epilogue(mi, mt, bset, ctx)
                kb.barrier()

        def qkv_phase(l):
            sbl = (l % 2 == 0)
            nq = cfg.nqkv(l) // 128
            qscale = (128 ** -0.5) if sbl else 0.125

            def alloc(ph):
                o = [ph.enter_context(_sbuf_tensor(f"q_o{i}", [128, T], BF16)) for i in range(2)]
                return (o, [B("qo0"), B("qo1")])

            def epi(mi, mt, bset, ctx):
                o, b_o2 = ctx
                s = mi % 2
                sc = qscale if mt < KC else 1.0
                for (t0, n, bi) in ((0, 512, 0), (512, 512, 1)):
                    kb.op(kb.act, lambda e: e.activation(out=o[s][:, t0:t0 + n], in_=ps[bset + bi][:, 0:n],
                                                         func=AF.Identity, scale=sc),
                          [b_ps[bset + bi]], [b_o2[s]])
                if mt < KC:
                    kb.dma(qT[mt * 128:(mt + 1) * 128, :], o[s][:], b_o2[s], b_q, nowaw=True)
                else:
                    r = (mt - KC) * 128
                    kb.dma(kv_own[r:r + 128, :], o[s][:], b_o2[s], b_kvo, nowaw=True)

            linear_phase(("qkv", l), D, list(range(nq)), xsT, b_xs, (2, T),
                         [(0, 512, 0), (512, 512, 1)], epi, alloc)

        def resid_phase(wkey, K, src, b_srcact, h_in, b_hin, h_out, b_hout, dbg_out=None):
            ntt = 1 if K > D else 2
            passes = [(0, T)] if K == D else [(0, 512), (512, 512)]
            for (p0, pn) in passes:
                def alloc(ph):
                    r = [ph.enter_context(_sbuf_tensor(f"r_r{i}", [128, pn], F32)) for i in range(2)]
                    o = [ph.enter_context(_sbuf_tensor(f"r_o{i}", [128, pn], F32)) for i in range(2)]
                    return (r, o, [B("rr0"), B("rr1")], [B("ro0"), B("ro1")])

                tiles = [(i * 512, 512, i) for i in range(pn // 512)]

                def epi(mi, mt, bset, ctx):
                    r, o, b_r, b_o2 = ctx
                    s = mi % 2
                    kb.dma(r[s][:], h_in[mt * 128:(mt + 1) * 128, 2 + p0:2 + p0 + pn], b_hin, b_r[s])
                    for (t0, n, bi) in tiles:
                        kb.op(kb.dve, lambda e: e.tensor_tensor(out=o[s][:, t0:t0 + n], in0=ps[bset + bi][:, 0:n],
                                                                in1=r[s][:, t0:t0 + n], op=ALU.add),
                              [b_ps[bset + bi], b_r[s]], [b_o2[s]])
                    kb.dma(h_out[mt * 128:(mt + 1) * 128, 2 + p0:2 + p0 + pn], o[s][:], b_o2[s], b_hout, nowaw=True)
                    if dbg_out is not None:
                        kb.dma(dbg_out[mt * 128:(mt + 1) * 128, p0:p0 + pn], o[s][:], b_o2[s], b_dbg, nowaw=True)

                linear_phase(wkey, K, list(range(KC)), src, b_srcact, (p0, pn), tiles, epi, alloc, kseg=28 if K > D else 32)

        def ffn_in_phase(l):
            mts = []
            for i in range(MTF):
                mts += [i, MTF + i]

            def alloc(ph):
                hs = [ph.enter_context(_sbuf_tensor(f"f_hs{i}", [128, TT], F32)) for i in range(2)]
                yg = ph.enter_context(_sbuf_tensor("f_yg", [128, T], F32))
                yu = ph.enter_context(_sbuf_tensor("f_yu", [128, T], F32))
                sg = ph.enter_context(_sbuf_tensor("f_sg", [128, T], F32))
                go = [ph.enter_context(_sbuf_tensor(f"f_go{i}", [128, T], BF16)) for i in range(2)]
                return dict(hs=hs, yg=yg, yu=yu, sg=sg, go=go, b_hs=[B("hs0"), B("hs1")], b_yg=B("yg"),
                            b_yu=B("yu"), b_sg=B("sg"), b_go=[B("go0"), B("go1")])

            def epi(mi, mt, bset, c):
                s = mi % 2
                hs, b_hs = c["hs"][s], c["b_hs"][s]
                isg = (mi % 2 == 0)
                y, b_yy = (c["yg"], c["b_yg"]) if isg else (c["yu"], c["b_yu"])
                kb.op(kb.dve, lambda e: e.tensor_copy(out=hs[:, 0:2], in_=ps[bset + 0][:, 0:2]),
                      [b_ps[bset + 0]], [b_hs])
                kb.op(kb.act, lambda e: e.activation(out=hs[:, 2:514], in_=ps[bset + 1][:, 0:512], func=AF.Identity),
                      [b_ps[bset + 1]], [b_hs])
                kb.op(kb.dve, lambda e: e.tensor_copy(out=hs[:, 514:TT], in_=ps[bset + 2][:, 0:512]),
                      [b_ps[bset + 2]], [b_hs])
                w0, w1, w2 = (cw_s[:, l, j, mt:mt + 1] for j in range(3))
                kb.op(kb.act, lambda e: e.activation(out=y[:], in_=hs[:, 2:TT], func=AF.Identity,
                                                     bias=cb_s[:, l, mt:mt + 1], scale=w2),
                      [b_hs, b_cst], [b_yy])
                kb.op(kb.dve, lambda e: e.scalar_tensor_tensor(out=y[:], in0=hs[:, 1:TT - 1], scalar=w1, in1=y[:],
                                                               op0=ALU.mult, op1=ALU.add),
                      [b_hs, b_yy, b_cst], [b_yy])
                kb.op(kb.dve, lambda e: e.scalar_tensor_tensor(out=y[:], in0=hs[:, 0:T], scalar=w0, in1=y[:],
                                                               op0=ALU.mult, op1=ALU.add),
                      [b_hs, b_yy, b_cst], [b_yy])
                if isg:
                    kb.op(kb.act, lambda e: e.activation(out=c["sg"][:], in_=y[:], func=AF.Silu),
                          [b_yy], [c["b_sg"]])
                else:
                    i = mt - MTF
                    gs = (mi // 2) % 2
                    kb.op(kb.dve, lambda e: e.tensor_tensor(out=c["go"][gs][:], in0=c["sg"][:], in1=y[:],
                                                            op=ALU.mult),
                          [c["b_sg"], b_yy], [c["b_go"][gs]])
                    kb.dma(gT[i * 128:(i + 1) * 128, :], c["go"][gs][:], c["b_go"][gs], b_g, nowaw=True)

            linear_phase(("ing", l), D, mts, xsT, b_xs, (0, TT),
                         [(0, 2, 0), (2, 512, 1), (514, 512, 2)], epi, alloc,
                         wsel=lambda mt: (("ing", l), mt) if mt < MTF else (("inu", l), mt - MTF))

        def sb_attn_phase(pre=None):
            NKB, OFF = cfg.NKB, cfg.OFF
            PR = 512
            for p in range(2 * D // PR):
                kb.coll(kv_all[p * NG * PR:(p + 1) * NG * PR, :], kv_own[p * PR:(p + 1) * PR, :], b_kvo, b_kva,
                        groups, nowaw=(p > 0))
            if pre is not None:
                pre()
            with contextlib.ExitStack() as ph:
                A = lambda n, s, d: ph.enter_context(_sbuf_tensor(n, list(s), d))
                mb = A("a_mb", [128, cfg.MW], BF16)
                b_mb = B("mb")
                kb.dma(mb[:], mbig_d.ap(), b_cin, b_mb)
                qh = [A(f"a_q{i}", [128, T], BF16) for i in range(2)]
                kh = [A(f"a_k{i}", [128, NG, T], BF16) for i in range(2)]
                vth = [A(f"a_vt{i}", [128, NG, T], BF16) for i in range(2)]
                vh = [A(f"a_v{i}", [128, NKB, 128], BF16) for i in range(2)]
                ee = [A(f"a_e{i}", [128, 512], F32) for i in range(4)]
                spb = [A(f"a_sp{i}", [128, 512], BF16) for i in range(4)]
                wb = [A(f"a_w{i}", [128, 512], BF16) for i in range(4)]
                s32 = [A(f"a_s32{q}", [128, 512], F32) for q in range(T // 512)]
                sbf = [[A(f"a_sbf{q}{i}", [128, 512], BF16) for i in range(2)] for q in range(T // 512)]
                ob = [A(f"a_o{i}", [128, 512], BF16) for i in range(T // 512)]
                b_qh, b_kh, b_vth, b_vh = ([B("qh0"), B("qh1")], [B("kh0"), B("kh1")], [B("vt0"), B("vt1")],
                                           [B("vh0"), B("vh1")])
                b_ee, b_spb, b_wb = [B(f"e{i}") for i in range(4)], [B(f"sp{i}") for i in range(4)], [B(f"w{i}") for i in range(4)]
                b_s32 = [B(f"s32{q}") for q in range(T // 512)]
                b_sbf = [[B(f"sbf{q}{i}") for i in range(2)] for q in range(T // 512)]
                b_ob = [B(f"ob{q}") for q in range(T // 512)]
                kva5 = kv_all.ap().rearrange("(p r i) t -> p i r t", r=NG, i=PR)

                def kvrows(f0):
                    return kva5[f0 // PR, (f0 % PR):(f0 % PR) + 128, :, :]
                NQ = T // 512
                ZB = [0, 1, 2, 7]
                zi = 0
                si = [0] * NQ
                for h in range(cfg.SBH):
                    s = h % 2
                    kb.dma(qh[s][:], qT[h * 128:(h + 1) * 128, :], b_q, b_qh[s])
                    kb.dma(kh[s][:], kvrows(h * 128), b_kva, b_kh[s])
                    kb.dma(vth[s][:], kvrows(D + h * 128), b_kva, b_vth[s])
                    for g4 in range(NKB // 4):
                        bi = 5 + (g4 % 2)
                        for j in range(4):
                            kbk = g4 * 4 + j
                            r, c = kbk // (T // 128), (kbk % (T // 128)) * 128
                            kb.op(kb.pe, lambda e: e.matmul(ps[bi][:, j * 128:(j + 1) * 128],
                                                            lhsT=vth[s][:, r, c:c + 128], rhs=ident,
                                                            start=True, stop=True),
                                  [b_vth[s], b_cst], [b_ps[bi]], inc=(j == 3))
                        kb.op(kb.dve, lambda e: e.tensor_copy(
                            out=vh[s][:, g4 * 4:(g4 + 1) * 4, :],
                            in_=ps[bi][:, :].rearrange("p (j d) -> p j d", d=128)),
                              [b_ps[bi]], [b_vh[s]])
                    def stage1(tl):
                        qb, n_i, kbk, zs, zb = tl["qb"], tl["n_i"], tl["kbk"], tl["zs"], tl["zb"]
                        r, c = kbk // (T // 128), (kbk % (T // 128)) * 128
                        v0 = 512 * qb - 128 * kbk + OFF
                        kb.op(kb.pe, lambda e: e.matmul(ps[zb][:, :], lhsT=kh[s][:, r, c:c + 128],
                                                        rhs=qh[s][:, qb * 512:(qb + 1) * 512],
                                                        start=True, stop=False),
                              [b_kh[s], b_qh[s]], [b_ps[zb]], inc=False)
                        kb.op(kb.pe, lambda e: e.matmul(ps[zb][:, :], lhsT=ident, rhs=mb[:, v0:v0 + 512],
                                                        start=False, stop=True),
                              [b_mb, b_cst], [b_ps[zb]])
                        kb.op(kb.act, lambda e: e.activation(out=ee[zs][:], in_=ps[zb][:, :], func=AF.Exp),
                              [b_ps[zb]], [b_ee[zs]])
                        kb.op(kb.act, lambda e: e.activation(out=spb[zs][:], in_=ee[zs][:], func=AF.Ln,
                                                             bias=1.0, scale=1.0),
                              [b_ee[zs]], [b_spb[zs]])
                        tl["sl_in"] = si[qb] % 2
                        if n_i == 0:
                            kb.op(kb.dve, lambda e: e.tensor_copy(out=s32[qb][:], in_=spb[zs][:]),
                                  [b_spb[zs]], [b_s32[qb]])
                        else:
                            kb.op(kb.dve, lambda e: e.tensor_tensor(out=s32[qb][:], in0=s32[qb][:],
                                                                     in1=spb[zs][:], op=ALU.add),
                                  [b_spb[zs], b_s32[qb]], [b_s32[qb]])
                        if n_i < NKB - 1:
                            si[qb] += 1
                            sl = si[qb] % 2
                            kb.op(kb.dve, lambda e: e.tensor_copy(out=sbf[qb][sl][:], in_=s32[qb][:]),
                                  [b_s32[qb]], [b_sbf[qb][sl]])

                    def stage2(tl):
                        qb, n_i, zs, zb = tl["qb"], tl["n_i"], tl["zs"], tl["zb"]
                        lastmm = (n_i == 0)
                        kb.op(kb.pe, lambda e: e.matmul(ps[zb][:, :], lhsT=negtri, rhs=spb[zs][:],
                                                        start=False, stop=lastmm),
                              [b_spb[zs], b_cst], [b_ps[zb]], inc=lastmm)
                        if n_i > 0:
                            sl = tl["sl_in"]
                            kb.op(kb.pe, lambda e: e.matmul(ps[zb][:, :], lhsT=negones, rhs=sbf[qb][sl][:],
                                                            start=False, stop=True),
                                  [b_sbf[qb][sl], b_cst], [b_ps[zb]])
                        kb.op(kb.act, lambda e: e.activation(out=wb[zs][:], in_=ps[zb][:, :], func=AF.Exp),
                              [b_ps[zb]], [b_wb[zs]])

                    def stage3(tl):
                        qb, n_i, kbk, zs = tl["qb"], tl["n_i"], tl["kbk"], tl["zs"]
                        ob_i = 3 + qb
                        kb.op(kb.pe, lambda e: e.matmul(ps[ob_i][:, :], lhsT=vh[s][:, kbk, :], rhs=wb[zs][:],
                                                        start=(n_i == 0), stop=(n_i == NKB - 1)),
                              [b_vh[s], b_wb[zs]], [b_ps[ob_i]], inc=True)

                    tiles_h = []
                    for n_i, kbk in enumerate(range(NKB - 1, -1, -1)):
                        for qb in range(NQ):
                            tiles_h.append(dict(qb=qb, n_i=n_i, kbk=kbk, zs=zi % 4, zb=ZB[zi % 4]))
                            zi += 1
                    nt = len(tiles_h)
                    for step in range(nt + 2):
                        if step < nt:
                            stage1(tiles_h[step])
                        if 0 <= step - 1 < nt:
                            stage2(tiles_h[step - 1])
                        if 0 <= step - 2 < nt:
                            stage3(tiles_h[step - 2])
                    for qb in range(NQ):
                        ob_i = 3 + qb
                        kb.op(kb.dve, lambda e: e.tensor_copy(out=ob[qb][:], in_=ps[ob_i][:, :]),
                              [b_ps[ob_i]], [b_ob[qb]])
                        kb.dma(oT[h * 128:(h + 1) * 128, qb * 512:(qb + 1) * 512], ob[qb][:], b_ob[qb], b_o,
                               nowaw=True)
                kb.barrier()

        def swa_attn_phase(jl):
            DKV, KVH, G, SWH = cfg.DKV, cfg.KVH, cfg.G, cfg.SWH
            NQB = T // 128
            NCH = 2 * DKV // 128
            with contextlib.ExitStack() as ph:
                A = lambda n, s, d: ph.enter_context(_sbuf_tensor(n, list(s), d))
                kb.dma(kvh_own.ap(), kv_own[0:2 * DKV, T - 128:T], b_kvo, b_kvho)
                kb.coll(kvh_all.ap().opt(), kvh_own.ap().opt(), b_kvho, b_kvha, groups)
                ha = A("s_ha", [128, NG, NCH, 128], BF16)
                hsel = A("s_hsel", [128, NCH, 128], BF16)
                b_ha, b_hsel = B("ha"), B("hsel")
                kb.dma(ha[:], kvh_all.ap().rearrange("(r c p) t -> p r c t", r=NG, p=128), b_kvha, b_ha)
                for r in range(NG):
                    if r == 0:
                        kb.op(kb.dve, lambda e: e.tensor_scalar(out=hsel[:], in0=ha[:, r], scalar1=sel_s[:, r:r + 1],
                                                                scalar2=None, op0=ALU.mult),
                              [b_ha, b_cst], [b_hsel])
                    else:
                        kb.op(kb.dve, lambda e: e.scalar_tensor_tensor(out=hsel[:], in0=ha[:, r],
                                                                       scalar=sel_s[:, r:r + 1], in1=hsel[:],
                                                                       op0=ALU.mult, op1=ALU.add),
                              [b_ha, b_hsel, b_cst], [b_hsel])
                kb.dma(kvh_sel.ap().rearrange("(c p) t -> p c t", p=128), hsel[:], b_hsel, b_kvhs)
                bt = A("s_bt", [128, SWH, 256], BF16)
                hm = A("s_hm", [128, 128], BF16)
                sk = A("s_sk", [128, SWH], F32)
                esk = A("s_esk", [128, SWH], F32)
                b_bt, b_sk, b_esk = B("bt"), B("sk"), B("esk")
                kb.dma(bt[:], bt_d.ap(), b_cin, b_bt)
                kb.dma(hm[:], hm_d.ap(), b_cin, b_bt, nowaw=True)
                kb.dma(sk[:], sink_d[:, jl, :], b_cin, b_sk)
                kb.op(kb.act, lambda e: e.activation(out=esk[:], in_=sk[:], func=AF.Exp), [b_sk], [b_esk])
                k2 = [A(f"s_k2{i}", [128, 128 + T], BF16) for i in range(2)]
                vt = [A(f"s_vt{i}", [64, 128 + T], BF16) for i in range(2)]
                vg = [A(f"s_vg{i}", [128, NQB + 1, 64], BF16) for i in range(2)]
                qs = [A(f"s_q{i}", [128, T], BF16) for i in range(2)]
                pc = [A(f"s_pc{i}", [128, T], BF16) for i in range(2)]
                pp = [A(f"s_pp{i}", [128, T], BF16) for i in range(2)]
                dn = A("s_dn", [64, T], F32)
                rd = A("s_rd", [64, T], F32)
                oo = [A(f"s_oo{i}", [64, T], BF16) for i in range(2)]
                b_k2, b_vt, b_vg, b_qs = ([B("k20"), B("k21")], [B("vt0"), B("vt1")], [B("vg0"), B("vg1")],
                                          [B("qs0"), B("qs1")])
                b_pc, b_pp, b_dn, b_rd, b_oo = ([B("pc0"), B("pc1")], [B("pp0"), B("pp1")], B("dn"), B("rd"),
                                                [B("oo0"), B("oo1")])
                hi = 0
                for g in range(KVH):
                    s = g % 2
                    for half in range(2):
                        kb.dma(k2[s][half * 64:(half + 1) * 64, 0:128], kvh_sel[g * 64:(g + 1) * 64, :], b_kvhs,
                               b_k2[s], nowaw=(half == 1))
                        kb.dma(k2[s][half * 64:(half + 1) * 64, 128:128 + T], kv_own[g * 64:(g + 1) * 64, :],
                               b_kvo, b_k2[s], nowaw=True)
                    kb.dma(vt[s][:, 0:128], kvh_sel[DKV + g * 64:DKV + (g + 1) * 64, :], b_kvhs, b_vt[s])
                    kb.dma(vt[s][:, 128:128 + T], kv_own[DKV + g * 64:DKV + (g + 1) * 64, :], b_kvo, b_vt[s],
                           nowaw=True)
                    nblk = NQB + 1
                    for b0 in range(0, nblk, 8):
                        bi = 0 + ((b0 // 8) % 2)
                        nb_ = min(8, nblk - b0)
                        for j in range(nb_):
                            blk = b0 + j
                            kb.op(kb.pe, lambda e: e.matmul(ps[bi][:, j * 64:(j + 1) * 64],
                                                            lhsT=vt[s][:, blk * 128:(blk + 1) * 128],
                                                            rhs=cst[0:64, 0, 0:64], start=True, stop=True),
                                  [b_vt[s], b_cst], [b_ps[bi]], inc=(j == nb_ - 1))
                        kb.op(kb.dve, lambda e: e.tensor_copy(
                            out=vg[s][:, b0:b0 + nb_, :],
                            in_=ps[bi][:, 0:nb_ * 64].rearrange("p (j d) -> p j d", d=64)),
                              [b_ps[bi]], [b_vg[s]])
                    for gi in range(G):
                        h = g * G + gi
                        par = h % 2
                        qsl = (h // 2) % 2
                        if par == 0:
                            kb.dma(qs[qsl][:], qT[(h // 2) * 128:(h // 2 + 1) * 128, :], b_q, b_qs[qsl])
                        hs_ = hi % 2
                        hi += 1
                        P0, P1 = par * 64, (par + 1) * 64
                        for i in range(NQB):
                            bi = 0 + i // 4
                            reg = ps[bi][:, (i % 4) * 128:(i % 4 + 1) * 128]
                            kb.op(kb.pe, lambda e: e.matmul(reg, lhsT=k2[s][P0:P1, 128 + i * 128:256 + i * 128],
                                                            rhs=qs[qsl][P0:P1, i * 128:(i + 1) * 128],
                                                            start=True, stop=False),
                                  [b_k2[s], b_qs[qsl]], [b_ps[bi]], inc=False)
                            kb.op(kb.pe, lambda e: e.matmul(reg, lhsT=ident, rhs=bt[:, h, 0:128],
                                                            start=False, stop=True),
                                  [b_bt, b_cst], [b_ps[bi]], inc=(i % 4 == 3))
                        for i in range(NQB):
                            bi = 2 + i // 4
                            reg = ps[bi][:, (i % 4) * 128:(i % 4 + 1) * 128]
                            kb.op(kb.pe, lambda e: e.matmul(reg, lhsT=k2[s][P0:P1, i * 128:(i + 1) * 128],
                                                            rhs=qs[qsl][P0:P1, i * 128:(i + 1) * 128],
                                                            start=True, stop=False),
                                  [b_k2[s], b_qs[qsl]], [b_ps[bi]], inc=False)
                            if i == 0:
                                kb.op(kb.pe, lambda e: e.matmul(reg, lhsT=ident, rhs=hm[:, :], start=False,
                                                                stop=False),
                                      [b_bt, b_cst], [b_ps[bi]], inc=False)
                            kb.op(kb.pe, lambda e: e.matmul(reg, lhsT=ident, rhs=bt[:, h, 128:256],
                                                            start=False, stop=True),
                                  [b_bt, b_cst], [b_ps[bi]], inc=(i % 4 == 3))
                        for half in range(2):
                            kb.op(kb.act, lambda e: e.activation(out=pc[hs_][:, half * 512:(half + 1) * 512],
                                                                 in_=ps[0 + half][:, :], func=AF.Exp),
                                  [b_ps[0 + half]], [b_pc[hs_]])
                            kb.op(kb.act, lambda e: e.activation(out=pp[hs_][:, half * 512:(half + 1) * 512],
                                                                 in_=ps[2 + half][:, :], func=AF.Exp),
                                  [b_ps[2 + half]], [b_pp[hs_]])
                        for i in range(NQB):
                            bi = 4 + i // 4
                            reg = ps[bi][0:64, (i % 4) * 128:(i % 4 + 1) * 128]
                            kb.op(kb.pe, lambda e: e.matmul(reg, lhsT=vg[s][:, i + 1, :],
                                                            rhs=pc[hs_][:, i * 128:(i + 1) * 128],
                                                            start=True, stop=False),
                                  [b_vg[s], b_pc[hs_]], [b_ps[bi]], inc=False)
                            kb.op(kb.pe, lambda e: e.matmul(reg, lhsT=vg[s][:, i, :],
                                                            rhs=pp[hs_][:, i * 128:(i + 1) * 128],
                                                            start=False, stop=True),
                                  [b_vg[s], b_pp[hs_]], [b_ps[bi]], inc=(i % 4 == 3))
                        for half in range(2):
                            bi = 6 + half
                            kb.op(kb.pe, lambda e: e.matmul(ps[bi][0:64, :], lhsT=cst[:, 2, 0:64],
                                                            rhs=pc[hs_][:, half * 512:(half + 1) * 512],
                                                            start=True, stop=False),
                                  [b_pc[hs_], b_cst], [b_ps[bi]], inc=False)
                            kb.op(kb.pe, lambda e: e.matmul(ps[bi][0:64, :], lhsT=cst[:, 2, 0:64],
                                                            rhs=pp[hs_][:, half * 512:(half + 1) * 512],
                                                            start=False, stop=True),
                                  [b_pp[hs_], b_cst], [b_ps[bi]])
                            kb.op(kb.dve, lambda e: e.tensor_scalar(out=dn[:, half * 512:(half + 1) * 512],
                                                                    in0=ps[bi][0:64, :], scalar1=esk[0:64, h:h + 1],
                                                                    scalar2=None, op0=ALU.add),
                                  [b_ps[bi], b_esk], [b_dn])
                        kb.op(kb.dve, lambda e: e.reciprocal(out=rd[:], in_=dn[:]), [b_dn], [b_rd])
                        for half in range(2):
                            kb.op(kb.dve, lambda e: e.tensor_tensor(out=oo[hs_][:, half * 512:(half + 1) * 512],
                                                                    in0=ps[4 + half][0:64, :],
                                                                    in1=rd[:, half * 512:(half + 1) * 512],
                                                                    op=ALU.mult),
                                  [b_ps[4 + half], b_rd], [b_oo[hs_]])
                        kb.dma(oT[h * 64:(h + 1) * 64, :], oo[hs_][:], b_oo[hs_], b_o, nowaw=True)
                kb.barrier()

        def halo_phase(h_t, b_h):
            with contextlib.ExitStack() as ph:
                A = lambda n, s, d: ph.enter_context(_sbuf_tensor(n, list(s), d))
                kb.dma(hal_own[:, 0:2 * KC].rearrange("p (k c) -> p k c", c=2),
                       h_t[:, TT - 2:TT].rearrange("(k p) c -> p k c", p=128), b_h, b_halo)
                kb.coll(hal_all.ap().opt(), hal_own.ap().opt(), b_halo, b_hala, groups)
                ha = A("h_ha", [128, NG, KC, 2], F32)
                hs_ = A("h_hs", [128, KC, 2], F32)
                b_ha, b_hs2 = B("hha"), B("hhs")
                kb.dma(ha[:], hal_all[:, 0:2 * KC].rearrange("(r p) (k c) -> p r k c", r=NG, c=2), b_hala, b_ha)
                for r in range(NG):
                    if r == 0:
                        kb.op(kb.dve, lambda e: e.tensor_scalar(out=hs_[:], in0=ha[:, r], scalar1=sel_s[:, r:r + 1],
                                                                scalar2=None, op0=ALU.mult),
                              [b_ha, b_cst], [b_hs2])
                    else:
                        kb.op(kb.dve, lambda e: e.scalar_tensor_tensor(out=hs_[:], in0=ha[:, r],
                                                                       scalar=sel_s[:, r:r + 1], in1=hs_[:],
                                                                       op0=ALU.mult, op1=ALU.add),
                              [b_ha, b_hs2, b_cst], [b_hs2])
                kb.dma(h_t[:, 0:2].rearrange("(k p) c -> p k c", p=128), hs_[:], b_hs2, b_h)
                kb.barrier()

        _pc = [0]

        def _lim(fn):
            def w(*a, **k):
                _pc[0] += 1
                if cfg.stop is not None and _pc[0] > cfg.stop:
                    return
                print("phase", _pc[0], fn.__name__, flush=True) if cfg.debug else None
                return fn(*a, **k)
            return w
        norm_phase, qkv_phase, sb_attn_phase, swa_attn_phase, resid_phase, halo_phase, ffn_in_phase = map(
            _lim, (norm_phase, qkv_phase, sb_attn_phase, swa_attn_phase, resid_phase, halo_phase, ffn_in_phase))
        for l in range(depth):
            norm_phase(hA, b_hA, lambda kc, l=l: gA_s[:, l, kc:kc + 1], "xs")
            qkv_phase(l)
            nxt = (l + 1 < depth) and (cfg.stop is None)
            if l == 0:
                sb_attn_phase(pre=lambda: emit_gather(0, ("o", "ing", "inu", "down")))
            elif l % 2 == 0:
                sb_attn_phase(pre=(lambda l=l: emit_gather(l + 1, ("qkv", "o", "ing"))) if nxt else None)
            else:
                swa_attn_phase(l // 2)
            resid_phase(("o", l), D, oT, b_o, hA, b_hA, hB, b_hB, dbg.get(("mid", l)))
            halo_phase(hB, b_hB)
            if nxt:
                emit_gather(l + 1, ("inu", "down") if (l % 2 == 0 and l > 0) else ("qkv", "o", "ing", "inu", "down"))
            norm_phase(hB, b_hB, lambda kc, l=l: gF_s[:, l, kc:kc + 1], "xs")
            ffn_in_phase(l)
            resid_phase(("down", l), DFF, gT, b_g, hB, b_hB, hA, b_hA, dbg.get(("h", l)))
        norm_phase(hA, b_hA, lambda kc: gO_s[:, kc:kc + 1], "y")
        kb.barrier()
    return nc


def host_tables(cfg, core):
    T, NG = cfg.T, cfg.NG
    j = core % NG
    q0 = j * T
    p = np.arange(128)[:, None]
    v = np.arange(cfg.MW)[None, :]
    mbig = np.where(p < q0 + v - cfg.OFF, 0.0, NEG).astype(ml_dtypes.bfloat16)
    sel = np.zeros((128, NG), np.float32)
    if j > 0:
        sel[:, j - 1] = 1.0
    slopes = (2.0 ** (-8.0 * np.arange(1, cfg.SWH + 1) / cfg.SWH)).astype(np.float32)
    k = np.arange(128)[:, None]
    q = np.arange(128)[None, :]
    bt = np.zeros((128, cfg.SWH, 256), np.float32)
    for h in range(cfg.SWH):
        bt[:, h, 0:128] = np.where(k <= q, -slopes[h] * (q - k), NEG)
        bt[:, h, 128:256] = np.where(k > q, -slopes[h] * (128 + q - k), NEG)
    hm = np.full((128, 128), 0.0 if j > 0 else NEG, np.float32)
    cst = np.zeros((128, 4, 128), np.float32)
    cst[:, 0, :] = np.eye(128)
    cst[:, 1, :] = -(k >= q).astype(np.float32)
    cst[:, 2, :] = 1.0
    cst[:, 3, :] = -1.0
    return dict(mbig=mbig, sel=sel, bt=bt.astype(ml_dtypes.bfloat16), hm=hm.astype(ml_dtypes.bfloat16),
                cst=cst.astype(ml_dtypes.bfloat16))


def make_in_maps(cfg, inp):
    D, T, NG, NC, depth, KC, MTF = cfg.D, cfg.T, cfg.NG, cfg.NC, cfg.depth, cfg.KC, cfg.MTF
    S = NG * T

    def fm(a):
        dd, F = a.shape
        return np.ascontiguousarray(a.reshape(dd, F // 128, 128).transpose(2, 0, 1)).astype(np.float32)

    common = {}
    common["gA"] = fm(inp["attn_norm"])
    common["gF"] = fm(inp["ffn_norm"])
    common["gO"] = np.ascontiguousarray(inp["final_norm"].reshape(KC, 128).T).astype(np.float32)
    cw = inp["ffn_conv_w"]
    common["cw"] = np.ascontiguousarray(cw.reshape(depth, 3, 2 * MTF, 128).transpose(3, 0, 1, 2)).astype(np.float32)
    common["cb"] = fm(inp["ffn_conv_b"])
    common["sink"] = np.ascontiguousarray(np.broadcast_to(inp["swa_sinks"][None], (128,) + inp["swa_sinks"].shape)).astype(np.float32)
    wl = {}
    for l in range(depth):
        j = l // 2
        if l % 2 == 0:
            wl[("qkv", l)] = inp["sb_w_qkv"][j]
            wl[("o", l)] = inp["sb_w_o"][j]
        else:
            wl[("qkv", l)] = inp["swa_w_qkv"][j]
            wl[("o", l)] = inp["swa_w_o"][j]
        wl[("ing", l)] = inp["ffn_w_in"][l][:, :cfg.DFF]
        wl[("inu", l)] = inp["ffn_w_in"][l][:, cfg.DFF:]
        wl[("down", l)] = inp["ffn_w_down"][l]
    wblk = {}
    for key, w in wl.items():
        K, N = w.shape
        wblk[key] = np.ascontiguousarray(w.reshape(K // 128, 128, N // 128, 128).transpose(2, 1, 0, 3)).reshape(N, K)
    maps = []
    for c in range(NC):
        b, j = c // NG, c % NG
        m = dict(common)
        m["xT"] = np.ascontiguousarray(inp["x"][b, j * T:(j + 1) * T, :].T)
        for (nm, l), w in wl.items():
            K = w.shape[0]
            wb = wblk[(nm, l)]
            Nw, Kw = wb.shape
            nr = cfg.piece_rows(Nw, Kw)
            P = (Nw // NG) // nr
            m[f"w_{nm}{l}"] = np.ascontiguousarray(wb.reshape(P, NG, nr, Kw)[:, j].reshape(P * nr, Kw))
        m.update(host_tables(cfg, c))
        maps.append(m)
    return maps


_CACHE = {}


def run(cfg, inp):
    key = (cfg.D, cfg.T, cfg.G, cfg.depth, cfg.debug)
    if key not in _CACHE:
        _CACHE[key] = build_program(cfg)
    nc = _CACHE[key]
    maps = make_in_maps(cfg, inp)
    res = run_bass_kernel_spmd(nc, maps, core_ids=list(range(cfg.NC)))
    return res


def kernel(x, attn_norm, ffn_norm, sb_w_qkv, sb_w_o, swa_w_qkv, swa_w_o, swa_sinks,
           ffn_w_in, ffn_conv_w, ffn_conv_b, ffn_w_down, final_norm):
    cfg = Cfg()
    inp = dict(x=np.asarray(x), attn_norm=np.asarray(attn_norm), ffn_norm=np.asarray(ffn_norm),
               sb_w_qkv=np.asarray(sb_w_qkv), sb_w_o=np.asarray(sb_w_o), swa_w_qkv=np.asarray(swa_w_qkv),
               swa_w_o=np.asarray(swa_w_o), swa_sinks=np.asarray(swa_sinks), ffn_w_in=np.asarray(ffn_w_in),
               ffn_conv_w=np.asarray(ffn_conv_w), ffn_conv_b=np.asarray(ffn_conv_b),
               ffn_w_down=np.asarray(ffn_w_down), final_norm=np.asarray(final_norm))
    res = run(cfg, inp)
    out = np.empty((cfg.NB, cfg.NG * cfg.T, cfg.D), np.float32)
    for c in range(cfg.NC):
        b, j = c // cfg.NG, c % cfg.NG
        out[b, j * cfg.T:(j + 1) * cfg.T, :] = res.results[c]["yT"].T
    return out
```

```python
import contextlib
import os
import math
import numpy as np
import ml_dtypes
import concourse.bass as bass
import concourse.mybir as mybir
from concourse.bass_utils import run_bass_kernel_spmd

F32 = mybir.dt.float32
BF16 = mybir.dt.bfloat16
AF = mybir.ActivationFunctionType
ALU = mybir.AluOpType
NEG = -30000.0


class Cfg:
    def __init__(self, D=4096, T=1024, NG=4, NB=2, G=8, depth=4, debug=False):
        self.D, self.T, self.NG, self.NB, self.G, self.depth = D, T, NG, NB, G, depth
        self.NC = NG * NB
        self.KC = D // 128
        self.SBH = D // 128
        self.SWH = D // 64
        self.KVH = self.SWH // G
        self.DKV = self.KVH * 64
        self.DFF = 7 * D // 2
        self.MTF = self.DFF // 128
        self.TT = T + 2
        self.NKB = NG * T // 128
        self.OFF = 128 * (self.NKB - 1)
        self.MW = self.OFF + 2 * 512
        self.debug = debug
        self.stop = None

    def piece_rows(self, K, N, esize=2):
        q = K // self.NG
        mx = max(1, (1 << 20) // (esize * N))
        nr = 1
        for d in range(1, q + 1):
            if q % d == 0 and d <= mx:
                nr = d
        return nr

    def nqkv(self, l):
        return 3 * self.D if l % 2 == 0 else self.D + 2 * self.DKV


class Sem:
    def __init__(self, h):
        self.h = h
        self.cnt = 0


class Buf:
    def __init__(self, name, ap=None):
        self.name = name
        self.ap = ap
        self.w = None
        self.r = {}
        self.dsem = None


class Eng:
    def __init__(self, name, e, sem):
        self.name, self.e, self.sem = name, e, sem
        self.seen = {}


class KB:
    def __init__(self, nc, stack):
        self.nc = nc
        self.stack = stack
        self.sems = []
        mk = lambda n, e: Eng(n, e, self.new_sem("s_" + n))
        self.pe = mk("pe", nc.tensor)
        self.act = mk("act", nc.scalar)
        self.dve = mk("dve", nc.vector)
        self.pool = mk("pool", nc.gpsimd)
        self.sp = mk("sp", nc.sync)
        self.engs = [self.pe, self.act, self.dve, self.pool, self.sp]
        self.local_bufs = []
        self.free_sems = []
        self.nobar_sems = set()

    def new_sem(self, name):
        s = Sem(self.stack.enter_context(self.nc.semaphore(f"{name}_{len(self.sems)}")))
        self.sems.append(s)
        return s

    def buf(self, name, persistent=False):
        b = Buf(name)
        if not persistent:
            self.local_bufs.append(b)
        return b

    def get_dsem(self, b, prefix):
        if b.dsem is None:
            if b in self.local_bufs and self.free_sems:
                b.dsem = self.free_sems.pop()
            else:
                b.dsem = self.new_sem(prefix + b.name)
                if getattr(b, "nobar", False):
                    self.nobar_sems.add(b.dsem)
        return b.dsem

    def wait(self, E, sem, cnt):
        if cnt <= 0 or E.seen.get(sem, 0) >= cnt:
            return
        if sem is E.sem and E is self.pe:
            return
        E.e.wait_ge(sem.h, cnt)
        E.seen[sem] = cnt

    def deps(self, E, reads, writes, nowaw=False):
        for b in reads:
            if b.w:
                self.wait(E, *b.w)
        for b in writes:
            if b.w and not nowaw:
                self.wait(E, *b.w)
            for sem, c in list(b.r.items()):
                self.wait(E, sem, c)

    def op(self, E, fn, reads, writes, inc=True):
        self.deps(E, reads, writes)
        ins = fn(E.e)
        if inc:
            E.sem.cnt += 1
            ins.then_inc(E.sem.h, 1)
            c = E.sem.cnt
        else:
            c = E.sem.cnt + 1
        for b in reads:
            b.r[E.sem] = max(b.r.get(E.sem, 0), c)
        for b in writes:
            b.w = (E.sem, c)
            b.r = {}
        return ins

    def dma(self, out_ap, in_ap, src, dst, nowaw=False, Q=None):
        Q = Q or self.sp
        self.deps(Q, [src], [dst], nowaw=nowaw)
        sem = self.get_dsem(dst, "d_")
        ins = Q.e.dma_start(out=out_ap, in_=in_ap)
        sem.cnt += 16
        ins.then_inc(sem.h, 16)
        src.r[sem] = sem.cnt
        dst.w = (sem, sem.cnt)
        if not nowaw:
            dst.r = {}

    def coll(self, out_ap, in_ap, src, dst, groups, nowaw=False):
        Q = self.pool
        self.deps(Q, [src], [dst], nowaw=nowaw)
        sem = self.get_dsem(dst, "c_")
        ins = Q.e.collective_compute("AllGather", ALU.bypass, replica_groups=groups,
                                     ins=[in_ap], outs=[out_ap])
        sem.cnt += 1
        ins.then_inc(sem.h, 1)
        src.r[sem] = sem.cnt
        dst.w = (sem, sem.cnt)
        if not nowaw:
            dst.r = {}

    def barrier(self):
        for E in self.engs:
            for s in self.sems:
                if s is E.sem or s in self.nobar_sems:
                    continue
                self.wait(E, s, s.cnt)
        for b in self.local_bufs:
            if b.dsem is not None:
                self.free_sems.append(b.dsem)
                b.dsem = None
        self.local_bufs = []


def build_program(cfg):
    nc = bass.Bass("TRN2", target_bir_lowering=False)
    _orig_sbuf_tensor = nc.sbuf_tensor
    _uid = [0]

    def _sbuf_tensor(name, shape, dt):
        _uid[0] += 1
        return _orig_sbuf_tensor(f"{name}_u{_uid[0]}", shape, dt)

    D, T, TT, KC, NG, NC = cfg.D, cfg.T, cfg.TT, cfg.KC, cfg.NG, cfg.NC
    DFF, MTF, depth = cfg.DFF, cfg.MTF, cfg.depth
    groups = [list(range(b * NG, (b + 1) * NG)) for b in range(cfg.NB)]
    allg = [list(range(NC))]

    def din(name, shape, dt=F32):
        return nc.dram_tensor(name, list(shape), dt, kind="ExternalInput")

    def dint(name, shape, dt=F32):
        return nc.dram_tensor(name, list(shape), dt)

    xT = din("xT", [D, T])
    wsh = {}
    wdims = {}
    for l in range(depth):
        wdims[("qkv", l)] = (D, cfg.nqkv(l))
        wdims[("o", l)] = (D, D)
        wdims[("ing", l)] = (D, DFF)
        wdims[("inu", l)] = (D, DFF)
        wdims[("down", l)] = (DFF, D)
    for (nm, l), (K, N) in wdims.items():
        wsh[(nm, l)] = din(f"w_{nm}{l}", [N // NG, K])
    gA = din("gA", [128, depth, KC])
    gF = din("gF", [128, depth, KC])
    gO = din("gO", [128, KC])
    cw_d = din("cw", [128, depth, 3, 2 * MTF])
    cb_d = din("cb", [128, depth, 2 * MTF])
    sink_d = din("sink", [128, depth // 2, cfg.SWH])
    mbig_d = din("mbig", [128, cfg.MW], BF16)
    sel_d = din("sel", [128, NG])
    bt_d = din("bt", [128, cfg.SWH, 256], BF16)
    hm_d = din("hm", [128, 128], BF16)
    cst_d = din("cst", [128, 4, 128], BF16)
    yT = nc.dram_tensor("yT", [D, T], F32, kind="ExternalOutput")
    dbg = {}
    if cfg.debug:
        for l in range(depth):
            dbg[("mid", l)] = nc.dram_tensor(f"dbg_mid{l}", [D, T], F32, kind="ExternalOutput")
            dbg[("h", l)] = nc.dram_tensor(f"dbg_h{l}", [D, T], F32, kind="ExternalOutput")

    wbn = {k: dint(f"wb_{k[0]}{k[1]}", [wdims[k][1] // NG, wdims[k][0]], BF16) for k in wdims}
    wfl = {k: dint(f"wf_{k[0]}{k[1]}", [wdims[k][1], wdims[k][0]], BF16) for k in wdims}
    TTP = T + 32
    hA = dint("hA", [D, TTP])
    hB = dint("hB", [D, TTP])
    xsT = dint("xsT", [D, TTP], BF16)
    qT = dint("qT", [D, T], BF16)
    kv_own = dint("kv_own", [2 * D, T], BF16)
    kv_all = dint("kv_all", [NG * 2 * D, T], BF16)
    oT = dint("oT", [D, T], BF16)
    gT = dint("gT", [DFF, T], BF16)
    HW = max(16, 2 * KC)
    hal_own = dint("hal_own", [128, HW])
    hal_all = dint("hal_all", [NG * 128, HW])
    kvh_own = dint("kvh_own", [2 * cfg.DKV, 128], BF16)
    kvh_all = dint("kvh_all", [NG * 2 * cfg.DKV, 128], BF16)
    kvh_sel = dint("kvh_sel", [2 * cfg.DKV, 128], BF16)

    with contextlib.ExitStack() as stack:
        kb = KB(nc, stack)
        PB = lambda n: kb.buf(n, persistent=True)
        B = PB
        b_x = B("xT")
        b_wsh = {k: B("wsh") for k in wdims}
        b_wbn = {k: B(f"wbn{k[0]}{k[1]}") for k in wdims}
        b_wfl = {k: B(f"wfl{k[0]}{k[1]}") for k in wdims}
        for _d in (b_wsh, b_wbn, b_wfl):
            for _b in _d.values():
                _b.nobar = True
        b_hA, b_hB, b_xs, b_q, b_kvo, b_kva, b_o, b_g = (B("hA"), B("hB"), B("xsT"), B("qT"), B("kvo"),
                                                         B("kva"), B("oT"), B("gT"))
        b_halo, b_hala, b_kvho, b_kvha, b_kvhs = B("halo"), B("hala"), B("kvho"), B("kvha"), B("kvhs")
        b_y = B("yT")
        b_dbg = B("dbg")
        b_cin = B("cin")

        B = kb.buf

        def sb(name, shape, dt):
            return stack.enter_context(_sbuf_tensor(name, list(shape), dt))

        cst = sb("cst", [128, 4, 128], BF16)
        gA_s = sb("gA_s", [128, depth, KC], F32)
        gF_s = sb("gF_s", [128, depth, KC], F32)
        gO_s = sb("gO_s", [128, KC], F32)
        cw_s = sb("cw_s", [128, depth, 3, 2 * MTF], F32)
        cb_s = sb("cb_s", [128, depth, 2 * MTF], F32)
        sel_s = sb("sel_s", [128, NG], F32)
        b_cst = PB("cst")
        for t_s, t_d in ((cst, cst_d), (gA_s, gA), (gF_s, gF), (gO_s, gO), (cw_s, cw_d), (cb_s, cb_d),
                         (sel_s, sel_d)):
            kb.dma(t_s[:], t_d.ap(), b_cin, b_cst, nowaw=True)
        ident = cst[:, 0, :]
        negtri = cst[:, 1, :]
        ones_b = cst[:, 2, :]
        negones = cst[:, 3, :]

        ps = [stack.enter_context(nc.psum_tensor(f"ps{i}", [128, 512], F32)) for i in range(8)]
        b_ps = [PB(f"ps{i}") for i in range(8)]

        def emit_gather(l, names=("qkv", "o", "ing", "inu", "down")):
            ks = [(n_, l) for n_ in names]
            for k in ks:
                Kk, Nk = wdims[k]
                rows = Nk // NG
                rstep = max(1, min(rows, (4 << 20) // (4 * Kk)))
                for r0 in range(0, rows, rstep):
                    r1 = min(rows, r0 + rstep)
                    kb.dma(wbn[k][r0:r1, :], wsh[k][r0:r1, :], b_wsh[k], b_wbn[k], nowaw=(r0 > 0), Q=kb.pool)
            for k in ks:
                Kk, Nk = wdims[k]
                nr = cfg.piece_rows(Nk, Kk)
                for p in range((Nk // NG) // nr):
                    kb.coll(wfl[k][p * NG * nr:(p + 1) * NG * nr, :], wbn[k][p * nr:(p + 1) * nr, :],
                            b_wbn[k], b_wfl[k], groups, nowaw=(p > 0))

        emit_gather(0, ("qkv",))
        with contextlib.ExitStack() as ph:
            if os.environ.get("SKIP_INIT"):
                raise_skip = True
            zt = ph.enter_context(_sbuf_tensor("zt", [128, KC, 2], F32))
            b_zt = B("zt")
            kb.op(kb.dve, lambda e: e.memset(zt[:], 0.0), [], [b_zt])
            if not os.environ.get("SKIP_INIT"):
                kb.dma(hA[:, 0:2].rearrange("(k p) c -> p k c", p=128), zt[:], b_zt, b_hA)
                kb.dma(hB[:, 0:2].rearrange("(k p) c -> p k c", p=128), zt[:], b_zt, b_hB, )
            if not os.environ.get("SKIP_X"):
                kb.dma(hA[:, 2:TT], xT.ap(), b_x, b_hA, nowaw=True)
            kb.barrier()

        def norm_phase(h_src, b_src, gain_ap_fn, dst_kind):
            with contextlib.ExitStack() as ph:
                hb = [ph.enter_context(_sbuf_tensor(f"n_hb{i}", [128, TT], F32)) for i in range(2)]
                sq = [ph.enter_context(_sbuf_tensor(f"n_sq{i}", [128, TT], BF16)) for i in range(2)]
                rt = ph.enter_context(_sbuf_tensor("n_rt", [128, TT], F32))
                rb = ph.enter_context(_sbuf_tensor("n_rb", [128, TT], F32))
                odt = BF16 if dst_kind == "xs" else F32
                ob = [ph.enter_context(_sbuf_tensor(f"n_ob{i}", [128, TT], odt)) for i in range(2)]
                b_hb = [B("hb0"), B("hb1")]
                b_sq = [B("sq0"), B("sq1")]
                b_rt, b_rb = B("rt"), B("rb")
                b_ob = [B("ob0"), B("ob1")]
                tiles = [(0, 512, 0), (512, 512, 1), (1024, TT - 1024, 2)]
                import os
                CUT = int(os.environ.get("NORM_CUT", "99"))
                for kc in range(KC):
                    s = kc % 2
                    kb.dma(hb[s][:], h_src[kc * 128:(kc + 1) * 128, 0:TT], b_src, b_hb[s])
                    if CUT < 2:
                        continue
                    kb.op(kb.act, lambda e: e.activation(out=sq[s][:], in_=hb[s][:], func=AF.Square),
                          [b_hb[s]], [b_sq[s]])
                    if CUT < 3:
                        continue
                    for (c0, n, bi) in tiles:
                        kb.op(kb.pe, lambda e: e.matmul(ps[bi][:, 0:n], lhsT=ones_b, rhs=sq[s][:, c0:c0 + n],
                                                        start=(kc == 0), stop=(kc == KC - 1)),
                              [b_sq[s], b_cst], [b_ps[bi]], inc=(kc == KC - 1 or bi == 2))
                if CUT < 4:
                    kb.barrier()
                    return
                for (c0, n, bi) in tiles:
                    kb.op(kb.act, lambda e: e.activation(out=rt[:, c0:c0 + n], in_=ps[bi][:, 0:n], func=AF.Sqrt,
                                                         bias=1e-6, scale=1.0 / D),
                          [b_ps[bi]], [b_rt])
                kb.op(kb.dve, lambda e: e.reciprocal(out=rb[:], in_=rt[:]), [b_rt], [b_rb])
                if CUT < 5:
                    kb.barrier()
                    return
                for kc in range(KC):
                    s = kc % 2
                    kb.dma(hb[s][:], h_src[kc * 128:(kc + 1) * 128, 0:TT], b_src, b_hb[s])
                    kb.op(kb.dve, lambda e: e.scalar_tensor_tensor(out=ob[s][:], in0=hb[s][:],
                                                                   scalar=gain_ap_fn(kc), in1=rb[:],
                                                                   op0=ALU.mult, op1=ALU.mult),
                          [b_hb[s], b_rb, b_cst], [b_ob[s]])
                    if dst_kind == "xs":
                        kb.dma(xsT[kc * 128:(kc + 1) * 128, 0:TT], ob[s][:], b_ob[s], b_xs, nowaw=True)
                    else:
                        kb.dma(yT[kc * 128:(kc + 1) * 128, :], ob[s][:, 2:TT], b_ob[s], b_y, nowaw=True)
                kb.barrier()

        def linear_phase(wkey, K, mtiles, src, b_srcact, src_cols, tok_tiles, epilogue, ph_alloc=None,
                         kseg=32, wsel=None):
            KCl = K // 128
            nseg = (KCl + kseg - 1) // kseg
            assert KCl % nseg == 0
            kseg = KCl // nseg
            c0s, ns = src_cols
            if wsel is None:
                wsel = lambda mt: (wkey, mt)
            with contextlib.ExitStack() as ph:
                insb = ph.enter_context(_sbuf_tensor("l_in", [128, KCl, ns], BF16))
                wbf = [ph.enter_context(_sbuf_tensor(f"l_wbf{i}", [128, kseg * 128], BF16)) for i in range(3)]
                b_in = B("l_in")
                b_wbf = [B("wbf0"), B("wbf1"), B("wbf2")]
                ctx = ph_alloc(ph) if ph_alloc else None
                step = max(1, KCl // 4)
                for k0 in range(0, KCl, step):
                    k1 = min(KCl, k0 + step)
                    kb.dma(insb[:, k0:k1, :],
                           src[k0 * 128:k1 * 128, c0s:c0s + ns].rearrange("(k p) t -> p k t", p=128),
                           b_srcact, b_in, nowaw=True)
                items = [(mi, mt, sg) for mi, mt in enumerate(mtiles) for sg in range(nseg)]

                def wload(j):
                    _mi, _mt, _sg = items[j]
                    wk, mtl = wsel(_mt)
                    c0w = _sg * kseg * 128
                    kb.dma(wbf[j % 3][:], wfl[wk][mtl * 128:(mtl + 1) * 128, c0w:c0w + kseg * 128],
                           b_wfl[wk], b_wbf[j % 3])

                for j in range(min(2, len(items))):
                    wload(j)
                for j, (mi, mt, sg) in enumerate(items):
                    if j + 2 < len(items):
                        wload(j + 2)
                    bset = (mi % 2) * 4
                    s3 = j % 3
                    for kc in range(kseg):
                        gk = sg * kseg + kc
                        last = (gk == KCl - 1)
                        for ti, (t0, n, bi) in enumerate(tok_tiles):
                            kb.op(kb.pe, lambda e: e.matmul(ps[bset + bi][:, 0:n],
                                                            lhsT=wbf[s3][:, kc * 128:(kc + 1) * 128],
                                                            rhs=insb[:, gk, t0:t0 + n],
                                                            start=(gk == 0), stop=last),
                                  [b_wbf[s3], b_in], [b_ps[bset + bi]],
                                  inc=(last or (kc == kseg - 1 and ti == len(tok_tiles) - 1)))
                    if sg == nseg - 1:
                        epilogue(mi, mt, bset, ctx)
                kb.barrier()

        def qkv_phase(l):
            sbl = (l % 2 == 0)
            nq = cfg.nqkv(l) // 128
            qscale = (128 ** -0.5) if sbl else 0.125

            def alloc(ph):
                o = [ph.enter_context(_sbuf_tensor(f"q_o{i}", [128, T], BF16)) for i in range(2)]
                return (o, [B("qo0"), B("qo1")])

            def epi(mi, mt, bset, ctx):
                o, b_o2 = ctx
                s = mi % 2
                sc = qscale if mt < KC else 1.0
                for (t0, n, bi) in ((0, 512, 0), (512, 512, 1)):
                    kb.op(kb.act, lambda e: e.activation(out=o[s][:, t0:t0 + n], in_=ps[bset + bi][:, 0:n],
                                                         func=AF.Identity, scale=sc),
                          [b_ps[bset + bi]], [b_o2[s]])
                if mt < KC:
                    kb.dma(qT[mt * 128:(mt + 1) * 128, :], o[s][:], b_o2[s], b_q, nowaw=True)
                else:
                    r = (mt - KC) * 128
                    kb.dma(kv_own[r:r + 128, :], o[s][:], b_o2[s], b_kvo, nowaw=True)

            linear_phase(("qkv", l), D, list(range(nq)), xsT, b_xs, (2, T),
                         [(0, 512, 0), (512, 512, 1)], epi, alloc)

        def resid_phase(wkey, K, src, b_srcact, h_in, b_hin, h_out, b_hout, dbg_out=None):
            ntt = 1 if K > D else 2
            passes = [(0, T)] if K == D else [(0, 512), (512, 512)]
            for (p0, pn) in passes:
                def alloc(ph):
                    r = [ph.enter_context(_sbuf_tensor(f"r_r{i}", [128, pn], F32)) for i in range(2)]
                    o = [ph.enter_context(_sbuf_tensor(f"r_o{i}", [128, pn], F32)) for i in range(2)]
                    return (r, o, [B("rr0"), B("rr1")], [B("ro0"), B("ro1")])

                tiles = [(i * 512, 512, i) for i in range(pn // 512)]

                def epi(mi, mt, bset, ctx):
                    r, o, b_r, b_o2 = ctx
                    s = mi % 2
                    kb.dma(r[s][:], h_in[mt * 128:(mt + 1) * 128, 2 + p0:2 + p0 + pn], b_hin, b_r[s])
                    for (t0, n, bi) in tiles:
                        kb.op(kb.dve, lambda e: e.tensor_tensor(out=o[s][:, t0:t0 + n], in0=ps[bset + bi][:, 0:n],
                                                                in1=r[s][:, t0:t0 + n], op=ALU.add),
                              [b_ps[bset + bi], b_r[s]], [b_o2[s]])
                    kb.dma(h_out[mt * 128:(mt + 1) * 128, 2 + p0:2 + p0 + pn], o[s][:], b_o2[s], b_hout, nowaw=True)
                    if dbg_out is not None:
                        kb.dma(dbg_out[mt * 128:(mt + 1) * 128, p0:p0 + pn], o[s][:], b_o2[s], b_dbg, nowaw=True)

                linear_phase(wkey, K, list(range(KC)), src, b_srcact, (p0, pn), tiles, epi, alloc, kseg=28 if K > D else 32)

        def ffn_in_phase(l):
            mts = []
            for i in range(MTF):
                mts += [i, MTF + i]

            def alloc(ph):
                hs = [ph.enter_context(_sbuf_tensor(f"f_hs{i}", [128, TT], F32)) for i in range(2)]
                yg = ph.enter_context(_sbuf_tensor("f_yg", [128, T], F32))
                yu = ph.enter_context(_sbuf_tensor("f_yu", [128, T], F32))
                sg = ph.enter_context(_sbuf_tensor("f_sg", [128, T], F32))
                go = [ph.enter_context(_sbuf_tensor(f"f_go{i}", [128, T], BF16)) for i in range(2)]
                return dict(hs=hs, yg=yg, yu=yu, sg=sg, go=go, b_hs=[B("hs0"), B("hs1")], b_yg=B("yg"),
                            b_yu=B("yu"), b_sg=B("sg"), b_go=[B("go0"), B("go1")])

            def epi(mi, mt, bset, c):
                s = mi % 2
                hs, b_hs = c["hs"][s], c["b_hs"][s]
                isg = (mi % 2 == 0)
                y, b_yy = (c["yg"], c["b_yg"]) if isg else (c["yu"], c["b_yu"])
                kb.op(kb.dve, lambda e: e.tensor_copy(out=hs[:, 0:2], in_=ps[bset + 0][:, 0:2]),
                      [b_ps[bset + 0]], [b_hs])
                kb.op(kb.act, lambda e: e.activation(out=hs[:, 2:514], in_=ps[bset + 1][:, 0:512], func=AF.Identity),
                      [b_ps[bset + 1]], [b_hs])
                kb.op(kb.dve, lambda e: e.tensor_copy(out=hs[:, 514:TT], in_=ps[bset + 2][:, 0:512]),
                      [b_ps[bset + 2]], [b_hs])
                w0, w1, w2 = (cw_s[:, l, j, mt:mt + 1] for j in range(3))
                kb.op(kb.act, lambda e: e.activation(out=y[:], in_=hs[:, 2:TT], func=AF.Identity,
                                                     bias=cb_s[:, l, mt:mt + 1], scale=w2),
                      [b_hs, b_cst], [b_yy])
                kb.op(kb.dve, lambda e: e.scalar_tensor_tensor(out=y[:], in0=hs[:, 1:TT - 1], scalar=w1, in1=y[:],
                                                               op0=ALU.mult, op1=ALU.add),
                      [b_hs, b_yy, b_cst], [b_yy])
                kb.op(kb.dve, lambda e: e.scalar_tensor_tensor(out=y[:], in0=hs[:, 0:T], scalar=w0, in1=y[:],
                                                               op0=ALU.mult, op1=ALU.add),
                      [b_hs, b_yy, b_cst], [b_yy])
                if isg:
                    kb.op(kb.act, lambda e: e.activation(out=c["sg"][:], in_=y[:], func=AF.Silu),
                          [b_yy], [c["b_sg"]])
                else:
                    i = mt - MTF
                    gs = (mi // 2) % 2
                    kb.op(kb.dve, lambda e: e.tensor_tensor(out=c["go"][gs][:], in0=c["sg"][:], in1=y[:],
                                                            op=ALU.mult),
                          [c["b_sg"], b_yy], [c["b_go"][gs]])
                    kb.dma(gT[i * 128:(i + 1) * 128, :], c["go"][gs][:], c["b_go"][gs], b_g, nowaw=True)

            linear_phase(("ing", l), D, mts, xsT, b_xs, (0, TT),
                         [(0, 2, 0), (2, 512, 1), (514, 512, 2)], epi, alloc,
                         wsel=lambda mt: (("ing", l), mt) if mt < MTF else (("inu", l), mt - MTF))

        def sb_attn_phase(pre=None):
            NKB, OFF = cfg.NKB, cfg.OFF
            PR = 512
            for p in range(2 * D // PR):
                kb.coll(kv_all[p * NG * PR:(p + 1) * NG * PR, :], kv_own[p * PR:(p + 1) * PR, :], b_kvo, b_kva,
                        groups, nowaw=(p > 0))
            if pre is not None:
                pre()
            with contextlib.ExitStack() as ph:
                A = lambda n, s, d: ph.enter_context(_sbuf_tensor(n, list(s), d))
                mb = A("a_mb", [128, cfg.MW], BF16)
                b_mb = B("mb")
                kb.dma(mb[:], mbig_d.ap(), b_cin, b_mb)
                qh = [A(f"a_q{i}", [128, T], BF16) for i in range(2)]
                kh = [A(f"a_k{i}", [128, NG, T], BF16) for i in range(2)]
                vth = [A(f"a_vt{i}", [128, NG, T], BF16) for i in range(2)]
                vh = [A(f"a_v{i}", [128, NKB, 128], BF16) for i in range(2)]
                ee = [A(f"a_e{i}", [128, 512], F32) for i in range(4)]
                spb = [A(f"a_sp{i}", [128, 512], BF16) for i in range(4)]
                wb = [A(f"a_w{i}", [128, 512], BF16) for i in range(4)]
                s32 = [A(f"a_s32{q}", [128, 512], F32) for q in range(T // 512)]
                sbf = [[A(f"a_sbf{q}{i}", [128, 512], BF16) for i in range(2)] for q in range(T // 512)]
                ob = [A(f"a_o{i}", [128, 512], BF16) for i in range(T // 512)]
                b_qh, b_kh, b_vth, b_vh = ([B("qh0"), B("qh1")], [B("kh0"), B("kh1")], [B("vt0"), B("vt1")],
                                           [B("vh0"), B("vh1")])
                b_ee, b_spb, b_wb = [B(f"e{i}") for i in range(4)], [B(f"sp{i}") for i in range(4)], [B(f"w{i}") for i in range(4)]
                b_s32 = [B(f"s32{q}") for q in range(T // 512)]
                b_sbf = [[B(f"sbf{q}{i}") for i in range(2)] for q in range(T // 512)]
                b_ob = [B(f"ob{q}") for q in range(T // 512)]
                kva5 = kv_all.ap().rearrange("(p r i) t -> p i r t", r=NG, i=PR)

                def kvrows(f0):
                    return kva5[f0 // PR, (f0 % PR):(f0 % PR) + 128, :, :]
                NQ = T // 512
                ZB = [0, 1, 2, 7]
                zi = 0
                si = [0] * NQ
                for h in range(cfg.SBH):
                    s = h % 2
                    kb.dma(qh[s][:], qT[h * 128:(h + 1) * 128, :], b_q, b_qh[s])
                    kb.dma(kh[s][:], kvrows(h * 128), b_kva, b_kh[s])
                    kb.dma(vth[s][:], kvrows(D + h * 128), b_kva, b_vth[s])
                    for g4 in range(NKB // 4):
                        bi = 5 + (g4 % 2)
                        for j in range(4):
                            kbk = g4 * 4 + j
                            r, c = kbk // (T // 128), (kbk % (T // 128)) * 128
                            kb.op(kb.pe, lambda e: e.matmul(ps[bi][:, j * 128:(j + 1) * 128],
                                                            lhsT=vth[s][:, r, c:c + 128], rhs=ident,
                                                            start=True, stop=True),
                                  [b_vth[s], b_cst], [b_ps[bi]], inc=(j == 3))
                        kb.op(kb.dve, lambda e: e.tensor_copy(
                            out=vh[s][:, g4 * 4:(g4 + 1) * 4, :],
                            in_=ps[bi][:, :].rearrange("p (j d) -> p j d", d=128)),
                              [b_ps[bi]], [b_vh[s]])
                    def stage1(tl):
                        qb, n_i, kbk, zs, zb = tl["qb"], tl["n_i"], tl["kbk"], tl["zs"], tl["zb"]
                        r, c = kbk // (T // 128), (kbk % (T // 128)) * 128
                        v0 = 512 * qb - 128 * kbk + OFF
                        kb.op(kb.pe, lambda e: e.matmul(ps[zb][:, :], lhsT=kh[s][:, r, c:c + 128],
                                                        rhs=qh[s][:, qb * 512:(qb + 1) * 512],
                                                        start=True, stop=False),
                              [b_kh[s], b_qh[s]], [b_ps[zb]], inc=False)
                        kb.op(kb.pe, lambda e: e.matmul(ps[zb][:, :], lhsT=ident, rhs=mb[:, v0:v0 + 512],
                                                        start=False, stop=True),
                              [b_mb, b_cst], [b_ps[zb]])
                        kb.op(kb.act, lambda e: e.activation(out=ee[zs][:], in_=ps[zb][:, :], func=AF.Exp),
                              [b_ps[zb]], [b_ee[zs]])
                        kb.op(kb.act, lambda e: e.activation(out=spb[zs][:], in_=ee[zs][:], func=AF.Ln,
                                                             bias=1.0, scale=1.0),
                              [b_ee[zs]], [b_spb[zs]])
                        tl["sl_in"] = si[qb] % 2
                        if n_i == 0:
                            kb.op(kb.dve, lambda e: e.tensor_copy(out=s32[qb][:], in_=spb[zs][:]),
                                  [b_spb[zs]], [b_s32[qb]])
                        else:
                            kb.op(kb.dve, lambda e: e.tensor_tensor(out=s32[qb][:], in0=s32[qb][:],
                                                                     in1=spb[zs][:], op=ALU.add),
                                  [b_spb[zs], b_s32[qb]], [b_s32[qb]])
                        if n_i < NKB - 1:
                            si[qb] += 1
                            sl = si[qb] % 2
                            kb.op(kb.dve, lambda e: e.tensor_copy(out=sbf[qb][sl][:], in_=s32[qb][:]),
                                  [b_s32[qb]], [b_sbf[qb][sl]])

                    def stage2(tl):
                        qb, n_i, zs, zb = tl["qb"], tl["n_i"], tl["zs"], tl["zb"]
                        lastmm = (n_i == 0)
                        kb.op(kb.pe, lambda e: e.matmul(ps[zb][:, :], lhsT=negtri, rhs=spb[zs][:],
                                                        start=False, stop=lastmm),
                              [b_spb[zs], b_cst], [b_ps[zb]], inc=lastmm)
                        if n_i > 0:
                            sl = tl["sl_in"]
                            kb.op(kb.pe, lambda e: e.matmul(ps[zb][:, :], lhsT=negones, rhs=sbf[qb][sl][:],
                                                            start=False, stop=True),
                                  [b_sbf[qb][sl], b_cst], [b_ps[zb]])
                        kb.op(kb.act, lambda e: e.activation(out=wb[zs][:], in_=ps[zb][:, :], func=AF.Exp),
                              [b_ps[zb]], [b_wb[zs]])

                    def stage3(tl):
                        qb, n_i, kbk, zs = tl["qb"], tl["n_i"], tl["kbk"], tl["zs"]
                        ob_i = 3 + qb
                        kb.op(kb.pe, lambda e: e.matmul(ps[ob_i][:, :], lhsT=vh[s][:, kbk, :], rhs=wb[zs][:],
                                                        start=(n_i == 0), stop=(n_i == NKB - 1)),
                              [b_vh[s], b_wb[zs]], [b_ps[ob_i]], inc=True)

                    tiles_h = []
                    for n_i, kbk in enumerate(range(NKB - 1, -1, -1)):
                        for qb in range(NQ):
                            tiles_h.append(dict(qb=qb, n_i=n_i, kbk=kbk, zs=zi % 4, zb=ZB[zi % 4]))
                            zi += 1
                    nt = len(tiles_h)
                    for step in range(nt + 2):
                        if step < nt:
                            stage1(tiles_h[step])
                        if 0 <= step - 1 < nt:
                            stage2(tiles_h[step - 1])
                        if 0 <= step - 2 < nt:
                            stage3(tiles_h[step - 2])
                    for qb in range(NQ):
                        ob_i = 3 + qb
                        kb.op(kb.dve, lambda e: e.tensor_copy(out=ob[qb][:], in_=ps[ob_i][:, :]),
                              [b_ps[ob_i]], [b_ob[qb]])
                        kb.dma(oT[h * 128:(h + 1) * 128, qb * 512:(qb + 1) * 512], ob[qb][:], b_ob[qb], b_o,
                               nowaw=True)
                kb.barrier()

        def swa_attn_phase(jl):
            DKV, KVH, G, SWH = cfg.DKV, cfg.KVH, cfg.G, cfg.SWH
            NQB = T // 128
            NCH = 2 * DKV // 128
            with contextlib.ExitStack() as ph:
                A = lambda n, s, d: ph.enter_context(_sbuf_tensor(n, list(s), d))
                kb.dma(kvh_own.ap(), kv_own[0:2 * DKV, T - 128:T], b_kvo, b_kvho)
                kb.coll(kvh_all.ap().opt(), kvh_own.ap().opt(), b_kvho, b_kvha, groups)
                ha = A("s_ha", [128, NG, NCH, 128], BF16)
                hsel = A("s_hsel", [128, NCH, 128], BF16)
                b_ha, b_hsel = B("ha"), B("hsel")
                kb.dma(ha[:], kvh_all.ap().rearrange("(r c p) t -> p r c t", r=NG, p=128), b_kvha, b_ha)
                for r in range(NG):
                    if r == 0:
                        kb.op(kb.dve, lambda e: e.tensor_scalar(out=hsel[:], in0=ha[:, r], scalar1=sel_s[:, r:r + 1],
                                                                scalar2=None, op0=ALU.mult),
                              [b_ha, b_cst], [b_hsel])
                    else:
                        kb.op(kb.dve, lambda e: e.scalar_tensor_tensor(out=hsel[:], in0=ha[:, r],
                                                                       scalar=sel_s[:, r:r + 1], in1=hsel[:],
                                                                       op0=ALU.mult, op1=ALU.add),
                              [b_ha, b_hsel, b_cst], [b_hsel])
                kb.dma(kvh_sel.ap().rearrange("(c p) t -> p c t", p=128), hsel[:], b_hsel, b_kvhs)
                bt = A("s_bt", [128, SWH, 256], BF16)
                hm = A("s_hm", [128, 128], BF16)
                sk = A("s_sk", [128, SWH], F32)
                esk = A("s_esk", [128, SWH], F32)
                b_bt, b_sk, b_esk = B("bt"), B("sk"), B("esk")
                kb.dma(bt[:], bt_d.ap(), b_cin, b_bt)
                kb.dma(hm[:], hm_d.ap(), b_cin, b_bt, nowaw=True)
                kb.dma(sk[:], sink_d[:, jl, :], b_cin, b_sk)
                kb.op(kb.act, lambda e: e.activation(out=esk[:], in_=sk[:], func=AF.Exp), [b_sk], [b_esk])
                k2 = [A(f"s_k2{i}", [128, 128 + T], BF16) for i in range(2)]
                vt = [A(f"s_vt{i}", [64, 128 + T], BF16) for i in range(2)]
                vg = [A(f"s_vg{i}", [128, NQB + 1, 64], BF16) for i in range(2)]
                qs = [A(f"s_q{i}", [128, T], BF16) for i in range(2)]
                pc = [A(f"s_pc{i}", [128, T], BF16) for i in range(2)]
                pp = [A(f"s_pp{i}", [128, T], BF16) for i in range(2)]
                dn = A("s_dn", [64, T], F32)
                rd = A("s_rd", [64, T], F32)
                oo = [A(f"s_oo{i}", [64, T], BF16) for i in range(2)]
                b_k2, b_vt, b_vg, b_qs = ([B("k20"), B("k21")], [B("vt0"), B("vt1")], [B("vg0"), B("vg1")],
                                          [B("qs0"), B("qs1")])
                b_pc, b_pp, b_dn, b_rd, b_oo = ([B("pc0"), B("pc1")], [B("pp0"), B("pp1")], B("dn"), B("rd"),
                                                [B("oo0"), B("oo1")])
                hi = 0
                for g in range(KVH):
                    s = g % 2
                    for half in range(2):
                        kb.dma(k2[s][half * 64:(half + 1) * 64, 0:128], kvh_sel[g * 64:(g + 1) * 64, :], b_kvhs,
                               b_k2[s], nowaw=(half == 1))
                        kb.dma(k2[s][half * 64:(half + 1) * 64, 128:128 + T], kv_own[g * 64:(g + 1) * 64, :],
                               b_kvo, b_k2[s], nowaw=True)
                    kb.dma(vt[s][:, 0:128], kvh_sel[DKV + g * 64:DKV + (g + 1) * 64, :], b_kvhs, b_vt[s])
                    kb.dma(vt[s][:, 128:128 + T], kv_own[DKV + g * 64:DKV + (g + 1) * 64, :], b_kvo, b_vt[s],
                           nowaw=True)
                    nblk = NQB + 1
                    for b0 in range(0, nblk, 8):
                        bi = 0 + ((b0 // 8) % 2)
                        nb_ = min(8, nblk - b0)
                        for j in range(nb_):
                            blk = b0 + j
                            kb.op(kb.pe, lambda e: e.matmul(ps[bi][:, j * 64:(j + 1) * 64],
                                                            lhsT=vt[s][:, blk * 128:(blk + 1) * 128],
                                                            rhs=cst[0:64, 0, 0:64], start=True, stop=True),
                                  [b_vt[s], b_cst], [b_ps[bi]], inc=(j == nb_ - 1))
                        kb.op(kb.dve, lambda e: e.tensor_copy(
                            out=vg[s][:, b0:b0 + nb_, :],
                            in_=ps[bi][:, 0:nb_ * 64].rearrange("p (j d) -> p j d", d=64)),
                              [b_ps[bi]], [b_vg[s]])
                    for gi in range(G):
                        h = g * G + gi
                        par = h % 2
                        qsl = (h // 2) % 2
                        if par == 0:
                            kb.dma(qs[qsl][:], qT[(h // 2) * 128:(h // 2 + 1) * 128, :], b_q, b_qs[qsl])
                        hs_ = hi % 2
                        hi += 1
                        P0, P1 = par * 64, (par + 1) * 64
                        for i in range(NQB):
                            bi = 0 + i // 4
                            reg = ps[bi][:, (i % 4) * 128:(i % 4 + 1) * 128]
                            kb.op(kb.pe, lambda e: e.matmul(reg, lhsT=k2[s][P0:P1, 128 + i * 128:256 + i * 128],
                                                            rhs=qs[qsl][P0:P1, i * 128:(i + 1) * 128],
                                                            start=True, stop=False),
                                  [b_k2[s], b_qs[qsl]], [b_ps[bi]], inc=False)
                            kb.op(kb.pe, lambda e: e.matmul(reg, lhsT=ident, rhs=bt[:, h, 0:128],
                                                            start=False, stop=True),
                                  [b_bt, b_cst], [b_ps[bi]], inc=(i % 4 == 3))
                        for i in range(NQB):
                            bi = 2 + i // 4
                            reg = ps[bi][:, (i % 4) * 128:(i % 4 + 1) * 128]
                            kb.op(kb.pe, lambda e: e.matmul(reg, lhsT=k2[s][P0:P1, i * 128:(i + 1) * 128],
                                                            rhs=qs[qsl][P0:P1, i * 128:(i + 1) * 128],
                                                            start=True, stop=False),
                                  [b_k2[s], b_qs[qsl]], [b_ps[bi]], inc=False)
                            if i == 0:
                                kb.op(kb.pe, lambda e: e.matmul(reg, lhsT=ident, rhs=hm[:, :], start=False,
                                                                stop=False),
                                      [b_bt, b_cst], [b_ps[bi]], inc=False)
                            kb.op(kb.pe, lambda e: e.matmul(reg, lhsT=ident, rhs=bt[:, h, 128:256],
                                                            start=False, stop=True),
                                  [b_bt, b_cst], [b_ps[bi]], inc=(i % 4 == 3))
                        for half in range(2):
                            kb.op(kb.act, lambda e: e.activation(out=pc[hs_][:, half * 512:(half + 1) * 512],
                                                                 in_=ps[0 + half][:, :], func=AF.Exp),
                                  [b_ps[0 + half]], [b_pc[hs_]])
                            kb.op(kb.act, lambda e: e.activation(out=pp[hs_][:, half * 512:(half + 1) * 512],
                                                                 in_=ps[2 + half][:, :], func=AF.Exp),
                                  [b_ps[2 + half]], [b_pp[hs_]])
                        for i in range(NQB):
                            bi = 4 + i // 4
                            reg = ps[bi][0:64, (i % 4) * 128:(i % 4 + 1) * 128]
                            kb.op(kb.pe, lambda e: e.matmul(reg, lhsT=vg[s][:, i + 1, :],
                                                            rhs=pc[hs_][:, i * 128:(i + 1) * 128],
                                                            start=True, stop=False),
                                  [b_vg[s], b_pc[hs_]], [b_ps[bi]], inc=False)
                            kb.op(kb.pe, lambda e: e.matmul(reg, lhsT=vg[s][:, i, :],
                                                            rhs=pp[hs_][:, i * 128:(i + 1) * 128],
                                                            start=False, stop=True),
                                  [b_vg[s], b_pp[hs_]], [b_ps[bi]], inc=(i % 4 == 3))
                        for half in range(2):
                            bi = 6 + half
                            kb.op(kb.pe, lambda e: e.matmul(ps[bi][0:64, :], lhsT=cst[:, 2, 0:64],
                                                            rhs=pc[hs_][:, half * 512:(half + 1) * 512],
                                                            start=True, stop=False),
                                  [b_pc[hs_], b_cst], [b_ps[bi]], inc=False)
                            kb.op(kb.pe, lambda e: e.matmul(ps[bi][0:64, :], lhsT=cst[:, 2, 0:64],
                                                            rhs=pp[hs_][:, half * 512:(half + 1) * 512],
                                                            start=False, stop=True),
                                  [b_pp[hs_], b_cst], [b_ps[bi]])
                            kb.op(kb.dve, lambda e: e.tensor_scalar(out=dn[:, half * 512:(half + 1) * 512],
                                                                    in0=ps[bi][0:64, :], scalar1=esk[0:64, h:h + 1],
                                                                    scalar2=None, op0=ALU.add),
                                  [b_ps[bi], b_esk], [b_dn])
                        kb.op(kb.dve, lambda e: e.reciprocal(out=rd[:], in_=dn[:]), [b_dn], [b_rd])
                        for half in range(2):
                            kb.op(kb.dve, lambda e: e.tensor_tensor(out=oo[hs_][:, half * 512:(half + 1) * 512],
                                                                    in0=ps[4 + half][0:64, :],
                                                                    in1=rd[:, half * 512:(half + 1) * 512],
                                                                    op=ALU.mult),
                                  [b_ps[4 + half], b_rd], [b_oo[hs_]])
                        kb.dma(oT[h * 64:(h + 1) * 64, :], oo[hs_][:], b_oo[hs_], b_o, nowaw=True)
                kb.barrier()

        def halo_phase(h_t, b_h):
            with contextlib.ExitStack() as ph:
                A = lambda n, s, d: ph.enter_context(_sbuf_tensor(n, list(s), d))
                kb.dma(hal_own[:, 0:2 * KC].rearrange("p (k c) -> p k c", c=2),
                       h_t[:, TT - 2:TT].rearrange("(k p) c -> p k c", p=128), b_h, b_halo)
                kb.coll(hal_all.ap().opt(), hal_own.ap().opt(), b_halo, b_hala, groups)
                ha = A("h_ha", [128, NG, KC, 2], F32)
                hs_ = A("h_hs", [128, KC, 2], F32)
                b_ha, b_hs2 = B("hha"), B("hhs")
                kb.dma(ha[:], hal_all[:, 0:2 * KC].rearrange("(r p) (k c) -> p r k c", r=NG, c=2), b_hala, b_ha)
                for r in range(NG):
                    if r == 0:
                        kb.op(kb.dve, lambda e: e.tensor_scalar(out=hs_[:], in0=ha[:, r], scalar1=sel_s[:, r:r + 1],
                                                                scalar2=None, op0=ALU.mult),
                              [b_ha, b_cst], [b_hs2])
                    else:
                        kb.op(kb.dve, lambda e: e.scalar_tensor_tensor(out=hs_[:], in0=ha[:, r],
                                                                       scalar=sel_s[:, r:r + 1], in1=hs_[:],
                                                                       op0=ALU.mult, op1=ALU.add),
                              [b_ha, b_hs2, b_cst], [b_hs2])
                kb.dma(h_t[:, 0:2].rearrange("(k p) c -> p k c", p=128), hs_[:], b_hs2, b_h)
                kb.barrier()

        _pc = [0]

        def _lim(fn):
            def w(*a, **k):
                _pc[0] += 1
                if cfg.stop is not None and _pc[0] > cfg.stop:
                    return
                print("phase", _pc[0], fn.__name__, flush=True) if cfg.debug else None
                return fn(*a, **k)
            return w
        norm_phase, qkv_phase, sb_attn_phase, swa_attn_phase, resid_phase, halo_phase, ffn_in_phase = map(
            _lim, (norm_phase, qkv_phase, sb_attn_phase, swa_attn_phase, resid_phase, halo_phase, ffn_in_phase))
        for l in range(depth):
            norm_phase(hA, b_hA, lambda kc, l=l: gA_s[:, l, kc:kc + 1], "xs")
            qkv_phase(l)
            nxt = (l + 1 < depth) and (cfg.stop is None)
            if l == 0:
                sb_attn_phase(pre=lambda: emit_gather(0, ("o", "ing", "inu", "down")))
            elif l % 2 == 0:
                sb_attn_phase(pre=(lambda l=l: emit_gather(l + 1, ("qkv", "o", "ing"))) if nxt else None)
            else:
                swa_attn_phase(l // 2)
            resid_phase(("o", l), D, oT, b_o, hA, b_hA, hB, b_hB, dbg.get(("mid", l)))
            halo_phase(hB, b_hB)
            if nxt:
                emit_gather(l + 1, ("inu", "down") if (l % 2 == 0 and l > 0) else ("qkv", "o", "ing", "inu", "down"))
            norm_phase(hB, b_hB, lambda kc, l=l: gF_s[:, l, kc:kc + 1], "xs")
            ffn_in_phase(l)
            resid_phase(("down", l), DFF, gT, b_g, hB, b_hB, hA, b_hA, dbg.get(("h", l)))
        norm_phase(hA, b_hA, lambda kc: gO_s[:, kc:kc + 1], "y")
        kb.barrier()
    return nc


def host_tables(cfg, core):
    T, NG = cfg.T, cfg.NG
    j = core % NG
    q0 = j * T
    p = np.arange(128)[:, None]
    v = np.arange(cfg.MW)[None, :]
    mbig = np.where(p < q0 + v - cfg.OFF, 0.0, NEG).astype(ml_dtypes.bfloat16)
    sel = np.zeros((128, NG), np.float32)
    if j > 0:
        sel[:, j - 1] = 1.0
    slopes = (2.0 ** (-8.0 * np.arange(1, cfg.SWH + 1) / cfg.SWH)).astype(np.float32)
    k = np.arange(128)[:, None]
    q = np.arange(128)[None, :]
    bt = np.zeros((128, cfg.SWH, 256), np.float32)
    for h in range(cfg.SWH):
        bt[:, h, 0:128] = np.where(k <= q, -slopes[h] * (q - k), NEG)
        bt[:, h, 128:256] = np.where(k > q, -slopes[h] * (128 + q - k), NEG)
    hm = np.full((128, 128), 0.0 if j > 0 else NEG, np.float32)
    cst = np.zeros((128, 4, 128), np.float32)
    cst[:, 0, :] = np.eye(128)
    cst[:, 1, :] = -(k >= q).astype(np.float32)
    cst[:, 2, :] = 1.0
    cst[:, 3, :] = -1.0
    return dict(mbig=mbig, sel=sel, bt=bt.astype(ml_dtypes.bfloat16), hm=hm.astype(ml_dtypes.bfloat16),
                cst=cst.astype(ml_dtypes.bfloat16))


def make_in_maps(cfg, inp):
    D, T, NG, NC, depth, KC, MTF = cfg.D, cfg.T, cfg.NG, cfg.NC, cfg.depth, cfg.KC, cfg.MTF
    S = NG * T

    def fm(a):
        dd, F = a.shape
        return np.ascontiguousarray(a.reshape(dd, F // 128, 128).transpose(2, 0, 1)).astype(np.float32)

    common = {}
    common["gA"] = fm(inp["attn_norm"])
    common["gF"] = fm(inp["ffn_norm"])
    common["gO"] = np.ascontiguousarray(inp["final_norm"].reshape(KC, 128).T).astype(np.float32)
    cw = inp["ffn_conv_w"]
    common["cw"] = np.ascontiguousarray(cw.reshape(depth, 3, 2 * MTF, 128).transpose(3, 0, 1, 2)).astype(np.float32)
    common["cb"] = fm(inp["ffn_conv_b"])
    common["sink"] = np.ascontiguousarray(np.broadcast_to(inp["swa_sinks"][None], (128,) + inp["swa_sinks"].shape)).astype(np.float32)
    wl = {}
    for l in range(depth):
        j = l // 2
        if l % 2 == 0:
            wl[("qkv", l)] = inp["sb_w_qkv"][j]
            wl[("o", l)] = inp["sb_w_o"][j]
        else:
            wl[("qkv", l)] = inp["swa_w_qkv"][j]
            wl[("o", l)] = inp["swa_w_o"][j]
        wl[("ing", l)] = inp["ffn_w_in"][l][:, :cfg.DFF]
        wl[("inu", l)] = inp["ffn_w_in"][l][:, cfg.DFF:]
        wl[("down", l)] = inp["ffn_w_down"][l]
    wblk = {}
    for key, w in wl.items():
        K, N = w.shape
        wblk[key] = np.ascontiguousarray(w.reshape(K // 128, 128, N // 128, 128).transpose(2, 1, 0, 3)).reshape(N, K)
    maps = []
    for c in range(NC):
        b, j = c // NG, c % NG
        m = dict(common)
        m["xT"] = np.ascontiguousarray(inp["x"][b, j * T:(j + 1) * T, :].T)
        for (nm, l), w in wl.items():
            K = w.shape[0]
            wb = wblk[(nm, l)]
            Nw, Kw = wb.shape
            nr = cfg.piece_rows(Nw, Kw)
            P = (Nw // NG) // nr
            m[f"w_{nm}{l}"] = np.ascontiguousarray(wb.reshape(P, NG, nr, Kw)[:, j].reshape(P * nr, Kw))
        m.update(host_tables(cfg, c))
        maps.append(m)
    return maps


_CACHE = {}


def run(cfg, inp):
    key = (cfg.D, cfg.T, cfg.G, cfg.depth, cfg.debug)
    if key not in _CACHE:
        _CACHE[key] = build_program(cfg)
    nc = _CACHE[key]
    maps = make_in_maps(cfg, inp)
    res = run_bass_kernel_spmd(nc, maps, core_ids=list(range(cfg.NC)))
    return res


def kernel(x, attn_norm, ffn_norm, sb_w_qkv, sb_w_o, swa_w_qkv, swa_w_o, swa_sinks,
           ffn_w_in, ffn_conv_w, ffn_conv_b, ffn_w_down, final_norm):
    cfg = Cfg()
    inp = dict(x=np.asarray(x), attn_norm=np.asarray(attn_norm), ffn_norm=np.asarray(ffn_norm),
               sb_w_qkv=np.asarray(sb_w_qkv), sb_w_o=np.asarray(sb_w_o), swa_w_qkv=np.asarray(swa_w_qkv),
               swa_w_o=np.asarray(swa_w_o), swa_sinks=np.asarray(swa_sinks), ffn_w_in=np.asarray(ffn_w_in),
               ffn_conv_w=np.asarray(ffn_conv_w), ffn_conv_b=np.asarray(ffn_conv_b),
               ffn_w_down=np.asarray(ffn_w_down), final_norm=np.asarray(final_norm))
    res = run(cfg, inp)
    out = np.empty((cfg.NB, cfg.NG * cfg.T, cfg.D), np.float32)
    for c in range(cfg.NC):
        b, j = c // cfg.NG, c % cfg.NG
        out[b, j * cfg.T:(j + 1) * cfg.T, :] = res.results[c]["yT"].T
    return out
```
